# Optimizing a Trainium2 kernel written in Bass

```python
import math
import jax
import jax.numpy as jnp
from jax import lax
import numpy as np

D_MODEL = 1024
BATCH = 16
SEQ = 2048
DEPTH = 2

GRID_W = 64
CTX_LEN = 256
N_EVEN = (DEPTH + 1) // 2
N_ODD = DEPTH // 2
NORM_EPS = 1e-6

A_HEADS = 12
A_KV_HEADS = 4
A_HEAD_DIM = 64
A_GROUP = A_HEADS // A_KV_HEADS
A_Q_W = A_HEADS * A_HEAD_DIM
A_KV_W = A_KV_HEADS * A_HEAD_DIM
ROPE_THETA = 10000.0
Q_BLOCK = 128

B_DIM = 512
B_CONV_WIDTH = 31

EVEN_IN = A_Q_W + 2 * A_KV_W + 2 * B_DIM
EVEN_MIX = A_Q_W + B_DIM

C_DIM = 512
C_ORDER = 2
C_SHORT = 3
C_EMB = 33
C_FILTER_HIDDEN = 64
C_FAST_DECAY = 0.3
C_SLOW_DECAY = 1.5
C_DECAY_TARGET = 1e-2
C_IN = (C_ORDER + 1) * C_DIM

D_HEADS = 4
D_KEY_DIM = 128
D_VAL_DIM = 128
D_FORGET = D_HEADS * D_KEY_DIM
D_DIM = D_HEADS * D_VAL_DIM
D_CHUNK = 32
D_IN = 3 * D_FORGET + 2 * D_DIM

ODD_IN = C_IN + D_IN
ODD_MIX = C_DIM + D_DIM

D_FF = 2816
FFN_CONV_WIDTH = 3

kernel_name = 'hybrid_dit_gqa_conformer_hyena_hgrn2'


def rms_norm(x, g):
    x32 = x.astype(jnp.float32)
    y = x32 * lax.rsqrt(jnp.mean(x32 * x32, axis=-1, keepdims=True) + NORM_EPS)
    return y.astype(x.dtype) * g


def layer_norm(x, g, b):
    x32 = x.astype(jnp.float32)
    mu = jnp.mean(x32, axis=-1, keepdims=True)
    var = jnp.mean(jnp.square(x32 - mu), axis=-1, keepdims=True)
    return ((x32 - mu) * lax.rsqrt(var + NORM_EPS)).astype(x.dtype) * g + b


def depthwise_conv(x, w, b):
    y = lax.conv_general_dilated(x, w[:, None, :].astype(x.dtype), (1,), 'SAME',
                                 dimension_numbers=('NWC', 'WIO', 'NWC'),
                                 feature_group_count=x.shape[-1])
    return y + b


def axial_rope(L):
    rows = L // GRID_W
    row = jnp.repeat(jnp.arange(rows, dtype=jnp.float32), GRID_W)
    col = jnp.tile(jnp.arange(GRID_W, dtype=jnp.float32), rows)
    axis_dim = A_HEAD_DIM // 2
    inv_freq = ROPE_THETA ** (-jnp.arange(0, axis_dim, 2, dtype=jnp.float32) / axis_dim)
    ang = jnp.concatenate([row[:, None] * inv_freq, col[:, None] * inv_freq], axis=-1)
    return jnp.cos(ang), jnp.sin(ang)


def apply_rope(x, cos, sin):
    xp = x.astype(jnp.float32).reshape(x.shape[:-1] + (A_HEAD_DIM // 2, 2))
    x0, x1 = xp[..., 0], xp[..., 1]
    c = cos[None, :, None, :]
    s = sin[None, :, None, :]
    out = jnp.stack([x0 * c - x1 * s, x0 * s + x1 * c], axis=-1)
    return out.reshape(x.shape).astype(x.dtype)


def block_attention(q, k, v):
    B, Lq = q.shape[:2]
    nb = Lq // Q_BLOCK
    qb = (q * (A_HEAD_DIM ** -0.5)).reshape(B, nb, Q_BLOCK, A_KV_HEADS, A_GROUP, A_HEAD_DIM)
    qb = qb.transpose(1, 0, 2, 3, 4, 5)

    def one_block(q_blk):
        s = jnp.einsum('bqhgd,bkhd->bhgqk', q_blk, k).astype(jnp.float32)
        p = jax.nn.softmax(s, axis=-1).astype(v.dtype)
        return jnp.einsum('bhgqk,bkhd->bqhgd', p, v)

    o = lax.map(one_block, qb)
    return o.transpose(1, 0, 2, 3, 4, 5).reshape(B, Lq, A_Q_W)


def conformer_conv(p, w, b, g, beta):
    a, gate = jnp.split(p, 2, axis=-1)
    u = depthwise_conv(a * jax.nn.sigmoid(gate), w, b)
    return jax.nn.silu(layer_norm(u, g, beta))


def even_mixer(h_lat, h_ctx, w_in, q_gain, k_gain, conv_w, conv_b, ln_g, ln_b, w_out, cos, sin, need_ctx):
    def split(p):
        B, L = p.shape[:2]
        q, k, v, glu = jnp.split(p, [A_Q_W, A_Q_W + A_KV_W, A_Q_W + 2 * A_KV_W], axis=-1)
        return (rms_norm(q.reshape(B, L, A_HEADS, A_HEAD_DIM), q_gain),
                rms_norm(k.reshape(B, L, A_KV_HEADS, A_HEAD_DIM), k_gain),
                v.reshape(B, L, A_KV_HEADS, A_HEAD_DIM), glu)

    q_l, k_l, v_l, glu_l = split(h_lat @ w_in)
    q_c, k_c, v_c, glu_c = split(h_ctx @ w_in)
    q_l = apply_rope(q_l, cos, sin)
    k_l = apply_rope(k_l, cos, sin)
    attn_l = block_attention(q_l, jnp.concatenate([k_c, k_l], axis=1), jnp.concatenate([v_c, v_l], axis=1))
    conv_l = conformer_conv(glu_l, conv_w, conv_b, ln_g, ln_b)
    y_lat = jnp.concatenate([attn_l, conv_l], axis=-1) @ w_out
    y_ctx = None
    if need_ctx:
        attn_c = block_attention(q_c, k_c, v_c)
        conv_c = conformer_conv(glu_c, conv_w, conv_b, ln_g, ln_b)
        y_ctx = jnp.concatenate([attn_c, conv_c], axis=-1) @ w_out
    return y_lat, y_ctx


def hyena_filter_spectra(L, w1, b1, w2, b2, w3, freq):
    f32 = jnp.float32
    t = jnp.linspace(0.0, 1.0, L, dtype=f32)[:, None]
    bands = (C_EMB - 1) // 2
    fb = jnp.linspace(1e-4, bands - 1, bands, dtype=f32)
    wpos = (2.0 * math.pi / L) * jnp.arange(L, dtype=f32)[:, None]
    feats = jnp.concatenate([t, jnp.cos(fb * wpos), -jnp.sin(fb * wpos)], axis=-1)
    fr = freq.astype(f32)
    hdn = jnp.sin(fr * (feats @ w1.astype(f32) + b1.astype(f32)))
    hdn = jnp.sin(fr * (hdn @ w2.astype(f32) + b2.astype(f32)))
    h = (hdn @ w3.astype(f32)).reshape(L, C_ORDER, 2, C_DIM)
    deltas = jnp.abs(jnp.linspace(math.log(C_DECAY_TARGET) / C_SLOW_DECAY,
                                  math.log(C_DECAY_TARGET) / C_FAST_DECAY, C_DIM, dtype=f32))
    h = h * jnp.exp(-t * deltas)[:, None, None, :]
    filt = jnp.concatenate([h[:, :, 0], jnp.zeros((1, C_ORDER, C_DIM), f32), h[:0:-1, :, 1]], axis=0)
    filt = filt / jnp.sum(jnp.abs(filt), axis=0, keepdims=True)
    return jnp.fft.rfft(filt, axis=0)


def fft_long_conv(u, spec):
    L = u.shape[1]
    U = jnp.fft.rfft(u.astype(jnp.float32), n=2 * L, axis=1)
    y = jnp.fft.irfft(U * spec[None], n=2 * L, axis=1)[:, :L]
    return y.astype(u.dtype)


def hyena(p, short_w, short_b, w1, b1, w2, b2, w3, freq, skip):
    L = p.shape[1]
    u = depthwise_conv(p, short_w, short_b)
    v, x1, x2 = jnp.split(u, 3, axis=-1)
    spec = hyena_filter_spectra(L, w1, b1, w2, b2, w3, freq)
    z = v
    for o, gate in enumerate((x1, x2)):
        z = gate * (fft_long_conv(z, spec[:, o]) + skip[o] * z)
    return z


def hgrn2_project(p, lb):
    B, L = p.shape[:2]
    q, f_f, f_b, i, g = jnp.split(p, [D_FORGET, 2 * D_FORGET, 3 * D_FORGET, 3 * D_FORGET + D_DIM], axis=-1)

    def heads(t, d):
        return t.astype(jnp.float32).reshape(B, L, D_HEADS, d)

    dirs = []
    for f_raw, lb_d in ((f_f, lb[0]), (f_b, lb[1])):
        f = lb_d + (1.0 - lb_d) * jax.nn.sigmoid(f_raw.astype(jnp.float32))
        dirs.append((heads(1.0 - f, D_KEY_DIM), heads(jnp.log(f), D_KEY_DIM)))
    return heads(q, D_KEY_DIM), dirs, heads(i, D_VAL_DIM), g


def hgrn2_chunk_scan(q, k, v, logf, s0):
    B, L = q.shape[:2]
    n = L // D_CHUNK

    def chunks(t):
        return t.reshape(B, n, D_CHUNK, D_HEADS, t.shape[-1]).transpose(1, 0, 3, 2, 4)

    q, k, v, logf = chunks(q), chunks(k), chunks(v), chunks(logf)
    b = jnp.cumsum(logf, axis=-2)
    b_last = b[..., -1:, :]
    q_dec = q * jnp.exp(b)
    scores = jnp.einsum('nbhtd,nbhsd->nbhts', q_dec, k * jnp.exp(-b))
    lower_tri = jnp.tril(jnp.ones((D_CHUNK, D_CHUNK), dtype=bool))
    scores = jnp.where(lower_tri, scores, 0.0)
    o_intra = jnp.einsum('nbhts,nbhse->nbhte', scores, v)
    upd = jnp.einsum('nbhsd,nbhse->nbhde', k * jnp.exp(b_last - b), v)
    chunk_decay = jnp.exp(b_last[..., 0, :])

    def step(s, xs):
        q_i, u_i, d_i = xs
        o_i = jnp.einsum('bhtd,bhde->bhte', q_i, s)
        return d_i[..., None] * s + u_i, o_i

    s_fin, o_inter = lax.scan(step, s0, (q_dec, upd, chunk_decay))
    o = (o_intra + o_inter).transpose(1, 0, 3, 2, 4).reshape(B, L, D_HEADS, v.shape[-1])
    return o, s_fin


def hgrn2_readout(o, g, gn_g):
    B, L = o.shape[:2]
    o = rms_norm(o, gn_g.astype(jnp.float32)).reshape(B, L, D_DIM)
    return (o * jax.nn.silu(g.astype(jnp.float32))).astype(g.dtype)


def hgrn2_mixer(p_lat, p_ctx, lb, gn_g, need_ctx):
    q_l, dirs_l, i_l, g_l = hgrn2_project(p_lat, lb)
    q_c, dirs_c, i_c, g_c = hgrn2_project(p_ctx, lb)
    (kf_l, lgf_l), (kb_l, lgb_l) = dirs_l
    (kf_c, lgf_c), (kb_c, lgb_c) = dirs_c

    def flip(t):
        return jnp.flip(t, axis=1)

    s0 = jnp.zeros((p_lat.shape[0], D_HEADS, D_KEY_DIM, D_VAL_DIM), jnp.float32)
    o_cf, s_cf = hgrn2_chunk_scan(q_c, kf_c, i_c, lgf_c, s0)
    o_cb, s_cb = hgrn2_chunk_scan(flip(q_c), flip(kb_c), flip(i_c), flip(lgb_c), s0)
    o_lf, _ = hgrn2_chunk_scan(q_l, kf_l, i_l, lgf_l, s_cf)
    o_lb, _ = hgrn2_chunk_scan(flip(q_l), flip(kb_l), flip(i_l), flip(lgb_l), s_cb)
    y_lat = hgrn2_readout(o_lf + flip(o_lb), g_l, gn_g)
    y_ctx = hgrn2_readout(o_cf + flip(o_cb), g_c, gn_g) if need_ctx else None
    return y_lat, y_ctx


def odd_mixer(h_lat, h_ctx, w_in, short_w, short_b, w1, b1, w2, b2, w3, freq, skip, lb, gn_g, w_out, need_ctx):
    p_lat = h_lat @ w_in
    p_ctx = h_ctx @ w_in
    d_lat, d_ctx = hgrn2_mixer(p_lat[..., C_IN:], p_ctx[..., C_IN:], lb, gn_g, need_ctx)
    c_lat = hyena(p_lat[..., :C_IN], short_w, short_b, w1, b1, w2, b2, w3, freq, skip)
    y_lat = jnp.concatenate([c_lat, d_lat], axis=-1) @ w_out
    y_ctx = None
    if need_ctx:
        c_ctx_out = hyena(p_ctx[..., :C_IN], short_w, short_b, w1, b1, w2, b2, w3, freq, skip)
        y_ctx = jnp.concatenate([c_ctx_out, d_ctx], axis=-1) @ w_out
    return y_lat, y_ctx


def conv_ffn(h, w_up, cw, cb, w_down):
    u = depthwise_conv(h @ w_up, cw, cb)
    val, gate = jnp.split(u, 2, axis=-1)
    return (jax.nn.silu(gate) * val) @ w_down


def setup_inputs(seed: int = 0) -> dict:
    key = jax.random.key(seed)
    ks = iter(jax.random.split(key, 40))
    D = D_MODEL

    def nrm(shape, scale):
        return jax.random.normal(next(ks), shape, jnp.float32) * scale

    def gain(shape):
        return 1.0 + nrm(shape, 0.02)

    return {
        'x': nrm((BATCH, SEQ, D), 1.0),
        'c': nrm((BATCH, D), 1.0),
        'ctx': nrm((BATCH, CTX_LEN, D), 1.0),
        'c_ctx': nrm((D,), 1.0),
        'ada_w': nrm((DEPTH, D, 6 * D), 0.5 * D ** -0.5),
        'ada_b': nrm((DEPTH, 6 * D), 0.01),
        'norm1_g': gain((DEPTH, D)),
        'norm2_g': gain((DEPTH, D)),
        'final_g': gain((D,)),
        'ev_w_in': nrm((N_EVEN, D, EVEN_IN), D ** -0.5),
        'ev_q_gain': gain((N_EVEN, A_HEAD_DIM)),
        'ev_k_gain': gain((N_EVEN, A_HEAD_DIM)),
        'ev_conv_w': nrm((N_EVEN, B_CONV_WIDTH, B_DIM), B_CONV_WIDTH ** -0.5),
        'ev_conv_b': nrm((N_EVEN, B_DIM), 0.01),
        'ev_ln_g': gain((N_EVEN, B_DIM)),
        'ev_ln_b': nrm((N_EVEN, B_DIM), 0.01),
        'ev_w_out': nrm((N_EVEN, EVEN_MIX, D), EVEN_MIX ** -0.5),
        'od_w_in': nrm((N_ODD, D, ODD_IN), D ** -0.5),
        'od_short_w': nrm((N_ODD, C_SHORT, C_IN), C_SHORT ** -0.5),
        'od_short_b': nrm((N_ODD, C_IN), 0.01),
        'od_filt_w1': nrm((N_ODD, C_EMB, C_FILTER_HIDDEN), C_EMB ** -0.5),
        'od_filt_b1': nrm((N_ODD, C_FILTER_HIDDEN), 0.1),
        'od_filt_w2': nrm((N_ODD, C_FILTER_HIDDEN, C_FILTER_HIDDEN), C_FILTER_HIDDEN ** -0.5),
        'od_filt_b2': nrm((N_ODD, C_FILTER_HIDDEN), 0.1),
        'od_filt_w3': nrm((N_ODD, C_FILTER_HIDDEN, 2 * C_ORDER * C_DIM), C_FILTER_HIDDEN ** -0.5),
        'od_filt_freq': gain((N_ODD, C_FILTER_HIDDEN)),
        'od_hyena_skip': nrm((N_ODD, C_ORDER, C_DIM), 0.5),
        'od_lower_bound': nrm((DEPTH, 2, D_FORGET), 0.1),
        'od_gnorm_g': gain((N_ODD, D_VAL_DIM)),
        'od_w_out': nrm((N_ODD, ODD_MIX, D), ODD_MIX ** -0.5),
        'ffn_w_up': nrm((DEPTH, D, 2 * D_FF), D ** -0.5),
        'ffn_conv_w': nrm((DEPTH, FFN_CONV_WIDTH, 2 * D_FF), FFN_CONV_WIDTH ** -0.5),
        'ffn_conv_b': nrm((DEPTH, 2 * D_FF), 0.01),
        'ffn_w_down': nrm((DEPTH, D_FF, D), D_FF ** -0.5),
    }


def reference(x, c, ctx, c_ctx, ada_w, ada_b, norm1_g, norm2_g, final_g, ev_w_in, ev_q_gain, ev_k_gain,
              ev_conv_w, ev_conv_b, ev_ln_g, ev_ln_b, ev_w_out, od_w_in, od_short_w, od_short_b,
              od_filt_w1, od_filt_b1, od_filt_w2, od_filt_b2, od_filt_w3, od_filt_freq, od_hyena_skip,
              od_lower_bound, od_gnorm_g, od_w_out, ffn_w_up, ffn_conv_w, ffn_conv_b, ffn_w_down):
    L = x.shape[1]
    cos, sin = axial_rope(L)
    lb_soft = jax.nn.softmax(od_lower_bound.astype(jnp.float32), axis=0)
    lower_bounds = jnp.cumsum(lb_soft, axis=0) - lb_soft[0:1]
    sc = jax.nn.silu(c)
    scc = jax.nn.silu(c_ctx)
    for l in range(DEPTH):
        last = l == DEPTH - 1
        j = l // 2
        mod_l = jnp.split((sc @ ada_w[l] + ada_b[l])[:, None, :], 6, axis=-1)
        mod_c = jnp.split(scc @ ada_w[l] + ada_b[l], 6, axis=-1)
        h_lat = rms_norm(x, norm1_g[l]) * (1 + mod_l[1]) + mod_l[0]
        h_ctx = rms_norm(ctx, norm1_g[l]) * (1 + mod_c[1]) + mod_c[0]
        if l % 2 == 0:
            y_lat, y_ctx = even_mixer(h_lat, h_ctx, ev_w_in[j], ev_q_gain[j], ev_k_gain[j], ev_conv_w[j],
                                      ev_conv_b[j], ev_ln_g[j], ev_ln_b[j], ev_w_out[j], cos, sin, not last)
        else:
            y_lat, y_ctx = odd_mixer(h_lat, h_ctx, od_w_in[j], od_short_w[j], od_short_b[j], od_filt_w1[j],
                                     od_filt_b1[j], od_filt_w2[j], od_filt_b2[j], od_filt_w3[j], od_filt_freq[j],
                                     od_hyena_skip[j], lower_bounds[l], od_gnorm_g[j], od_w_out[j], not last)
        x = x + mod_l[2] * y_lat
        h = rms_norm(x, norm2_g[l]) * (1 + mod_l[4]) + mod_l[3]
        x = x + mod_l[5] * conv_ffn(h, ffn_w_up[l], ffn_conv_w[l], ffn_conv_b[l], ffn_w_down[l])
        if not last:
            ctx = ctx + mod_c[2] * y_ctx
            hc = rms_norm(ctx, norm2_g[l]) * (1 + mod_c[4]) + mod_c[3]
            ctx = ctx + mod_c[5] * conv_ffn(hc, ffn_w_up[l], ffn_conv_w[l], ffn_conv_b[l], ffn_w_down[l])
    return rms_norm(x, final_g)
```

```python
from contextlib import ExitStack
import math
import numpy as np
import ml_dtypes
import concourse.bass as bass
import concourse.mybir as mybir
from concourse.bass_utils import run_bass_kernel_spmd

F32 = mybir.dt.float32
BF16 = mybir.dt.bfloat16
ALU = mybir.AluOpType
AF = mybir.ActivationFunctionType

N_DMA_SEMS = 40
NT = 2304
CH = [(0, 256), (256, 512), (768, 512), (1280, 512), (1792, 512)]
EPS = 1e-6


class Builder:
    def __init__(self):
        self.nc = bass.Bass("TRN2", target_bir_lowering=False)
        self.ops = []
        self.es = ExitStack()
        self._n = 0

    def dram(self, name, shape, dtype, kind):
        return self.nc.dram_tensor(name, list(shape), dtype, kind=kind).ap()

    def sbuf(self, shape, dtype, name=None):
        self._n += 1
        return self.es.enter_context(self.nc.sbuf_tensor(name or f"sb{self._n}", list(shape), dtype))

    def psum(self, shape, dtype, name=None):
        self._n += 1
        return self.es.enter_context(self.nc.psum_tensor(name or f"ps{self._n}", list(shape), dtype))

    def op(self, stream, fn, r=(), w=(), acc=False):
        self.ops.append((stream, "c", fn, tuple(r), tuple(w), acc))

    def dma(self, queue, out, in_, r=(), w=()):
        self.ops.append((queue, "d", (out, in_), tuple(r), tuple(w), False))

    def fence(self):
        self.ops.append((None, "f", None, (), (), False))

    def mm(self, out, lhsT, rhs, start, stop, r, w):
        self.op("pe", lambda e: e.matmul(out, lhsT, rhs, start=start, stop=stop), r=r, w=w, acc=not start)

    def tr(self, out, in_, ident, r, w):
        self.op("pe", lambda e: e.transpose(out, in_, ident), r=r, w=w)

    def act(self, out, in_, func, r, w, bias=None, scale=None):
        kw = {}
        if bias is not None:
            kw["bias"] = bias
        if scale is not None:
            kw["scale"] = scale
        self.op("act", lambda e: e.activation(out, in_, func, **kw), r=r, w=w)

    def tt(self, out, a, b, op, r, w, eng="dve"):
        self.op(eng, lambda e: e.tensor_tensor(out, a, b, op), r=r, w=w)

    def ts(self, out, a, s1, s2, op0, op1, r, w, eng="dve"):
        if s2 is None:
            self.op(eng, lambda e: e.tensor_scalar(out, a, s1, None, op0), r=r, w=w)
        else:
            self.op(eng, lambda e: e.tensor_scalar(out, a, s1, s2, op0, op1), r=r, w=w)

    def stt(self, out, a, s, b, op0, op1, r, w):
        self.op("dve", lambda e: e.scalar_tensor_tensor(out, a, s, b, op0, op1), r=r, w=w)

    def cp(self, out, in_, r, w, eng="dve"):
        self.op(eng, lambda e: e.tensor_copy(out, in_), r=r, w=w)

    def recip(self, out, in_, r, w):
        self.op("dve", lambda e: e.reciprocal(out, in_), r=r, w=w)

    def memset(self, ap, v, w, eng="dve"):
        self.op(eng, lambda e: e.memset(ap, v), w=w)

    def build(self):
        nc = self.nc
        ops = self.ops
        n = len(ops)
        streams = ["pe", "act", "dve", "pool", "sp"]
        last_w = {}
        readers = {}
        deps = [set() for _ in range(n)]
        last_on = {s: None for s in streams}
        open_dmas = []
        pending_fence = {s: None for s in streams}
        for i, (st, kind, fn, R, W, acc) in enumerate(ops):
            if kind == "f":
                fd = set(j for j in last_on.values() if j is not None) | set(open_dmas)
                for s in streams:
                    pending_fence[s] = (pending_fence[s] or set()) | fd
                open_dmas = []
                continue
            if pending_fence[st]:
                deps[i] |= pending_fence[st]
                pending_fence[st] = None
            for k in R:
                if k in last_w:
                    deps[i].add(last_w[k])
            for k in W:
                if k in last_w:
                    j = last_w[k]
                    if not ((acc or st == "pe") and ops[j][0] == st and ops[j][1] == "c" and kind == "c"):
                        deps[i].add(j)
                for rr in readers.get(k, ()):
                    deps[i].add(rr)
            for k in R:
                readers.setdefault(k, []).append(i)
            for k in W:
                last_w[k] = i
                readers[k] = []
            deps[i].discard(i)
            if kind == "c":
                last_on[st] = i
            else:
                open_dmas.append(i)
        dma_sem_of, dma_val_of = {}, {}
        sem_count = [0] * N_DMA_SEMS
        sem_last = [None] * N_DMA_SEMS
        qn = {"sp": 0, "pool": 0}
        NSP = 24
        for i in range(n):
            if ops[i][1] != "d":
                continue
            if ops[i][0] == "sp":
                s = qn["sp"] % NSP
            else:
                s = NSP + qn["pool"] % (N_DMA_SEMS - NSP)
            qn[ops[i][0]] += 1
            if sem_last[s] is not None:
                deps[i].add(sem_last[s])
            sem_last[s] = i
            sem_count[s] += 16
            dma_sem_of[i] = s
            dma_val_of[i] = sem_count[s]
        needed = [False] * n
        for i in range(n):
            for j in deps[i]:
                needed[j] = True
        cnt = {s: 0 for s in streams}
        sigval = {}
        for i, (st, kind, *_r) in enumerate(ops):
            if kind == "c" and needed[i]:
                cnt[st] += 1
                sigval[i] = cnt[st]
        es = self.es
        esem = {s: es.enter_context(nc.semaphore(f"sem_{s}")) for s in streams}
        dsem = [es.enter_context(nc.semaphore(f"dsem{k}")) for k in range(N_DMA_SEMS)]
        block = es.enter_context(nc.Block())

        def target(j):
            if ops[j][1] == "d":
                return ("d", dma_sem_of[j]), dsem[dma_sem_of[j]], dma_val_of[j]
            return ("c", ops[j][0]), esem[ops[j][0]], sigval[j]

        def emit_stream(st, eng):
            waited = {}
            for i, (s2, kind, fn, R, W, acc) in enumerate(ops):
                if s2 != st:
                    continue
                need = {}
                for j in deps[i]:
                    key, sem, val = target(j)
                    if waited.get(key, 0) >= val:
                        continue
                    if need.get(key, (None, 0))[1] < val:
                        need[key] = (sem, val)
                for key, (sem, val) in need.items():
                    eng.wait_ge(sem, val)
                    waited[key] = val
                if kind == "c":
                    ins = fn(eng)
                    if needed[i]:
                        ins.then_inc(esem[st], 1)
                else:
                    out, in_ = fn
                    eng.dma_start(out=out, in_=in_).then_inc(dsem[dma_sem_of[i]], 16)
            if st == "sp":
                for s in range(N_DMA_SEMS):
                    if sem_count[s] > 0 and waited.get(("d", s), 0) < sem_count[s]:
                        eng.wait_ge(dsem[s], sem_count[s])
                for s3 in streams:
                    if cnt[s3] > 0 and waited.get(("c", s3), 0) < cnt[s3]:
                        eng.wait_ge(esem[s3], cnt[s3])

        @block.tensor
        def _(e):
            emit_stream("pe", e)

        @block.scalar
        def _(e):
            emit_stream("act", e)

        @block.vector
        def _(e):
            emit_stream("dve", e)

        @block.gpsimd
        def _(e):
            emit_stream("pool", e)

        @block.sync
        def _(e):
            emit_stream("sp", e)

        es.close()
        return nc


ARENA_BYTES = 200704


class Prog:
    def __init__(self, taps=()):
        self.B = Builder()
        self.taps = set(taps)
        self.uid = 0
        B = self.B
        self.arena = B.sbuf([128, ARENA_BYTES // 4], F32, name="arena")
        self.ps = [B.psum([128, 512], F32, name=f"psb{i}") for i in range(8)]
        self.outs = {}

    def k(self, s):
        self.uid += 1
        return f"{s}#{self.uid}"

    def view(self, off, free_shape, dtype):
        esz = 4 if dtype == F32 else 2
        nel = int(np.prod(free_shape))
        nb = nel * esz
        assert off % 4 == 0 and nb % 4 == 0 and off + nb <= ARENA_BYTES, (off, nb)
        a = self.arena[:, off // 4:(off + nb) // 4]
        if dtype != F32:
            a = a.bitcast(dtype)
        if len(free_shape) == 2:
            a = a.rearrange("p (a b) -> p a b", b=free_shape[1])
        elif len(free_shape) == 3:
            a = a.rearrange("p (a b c) -> p a b c", b=free_shape[1], c=free_shape[2])
        return a


class Carver:
    def __init__(self, P, base, limit=ARENA_BYTES):
        self.P, self.off, self.limit = P, base, limit

    def get(self, free_shape, dtype):
        esz = 4 if dtype == F32 else 2
        nb = (int(np.prod(free_shape)) * esz + 3) // 4 * 4
        v = self.P.view(self.off, free_shape, dtype)
        self.off += nb
        assert self.off <= self.limit, (self.off, self.limit)
        return v


def build_program(taps=(), layers=(0, 1)):
    import os
    DO_HY = os.environ.get('NO_HY', '') == ''
    P = Prog(taps)
    B = P.B
    ps = P.ps
    PSK = [f"psb{i}" for i in range(8)]

    def din(name, shape, dt=F32):
        return B.dram(name, shape, dt, "ExternalInput")

    xin = din("xin", [2, 128, 8, NT])
    cvec = din("cvec", [128, 8, 3])
    ada_w = din("ada_w", [2, 1024, 6144])
    abT = din("abT", [128, 2, 48])
    gvec = din("gvec", [128, 5, 8])
    consts = din("consts", [128, 10, 128])
    rope = din("rope", [128, 2, 2048])
    ev_w_in = din("ev_w_in", [1024, 2304])
    ev_small = din("ev_small", [128, 16])
    ev_cw = din("ev_cw", [128, 4, 31])
    ev_w_out = din("ev_w_out", [1280, 1024])
    ffn_w_up = din("ffn_w_up", [2, 1024, 5632])
    ffn_cw = din("ffn_cw", [128, 2, 44, 3])
    ffn_cb = din("ffn_cb", [128, 2, 44])
    ffn_w_down = din("ffn_w_down", [2, 2816, 1024])
    od_w_in = din("od_w_in", [1024, 4096])
    od_w_out = din("od_w_out", [1024, 1024])
    lbraw = din("lbraw", [128, 2, 2, 512])
    gnb_d = din("gnb", [128, 512])
    featsT = din("featsT", [33, 2048])
    hy_w1 = din("hy_w1", [33, 64])
    hy_w2 = din("hy_w2", [64, 64])
    hy_w3 = din("hy_w3", [64, 2048])
    hy_small = din("hy_small", [128, 8])
    hy_decay = din("hy_decay", [128, 16, 512])
    hy_sw = din("hy_sw", [128, 12, 3])
    hy_sb = din("hy_sb", [128, 12])
    hy_skip = din("hy_skip", [128, 2, 512])
    dftf = din("dftf", [17, 2, 128, 16, 128], BF16)
    dfti = din("dfti", [16, 128, 17, 2, 128], BF16)
    spec = B.dram("spec", [2, 17, 128, 2, 512], F32, "Internal")
    xsp = B.dram("xsp", [2, 128, 8, NT], F32, "Internal")
    ofs = B.dram("ofs", [2, 16, 64, 1024], F32, "Internal")
    out = B.dram("out", [2, 128, 8, 2048], F32, "ExternalOutput")
    tapd = {}
    for t in P.taps:
        if t in ("dlat", "clat"):
            tapd[t] = B.dram("tap_" + t, [2, 128, 4, 2048], F32, "ExternalOutput")
        else:
            tapd[t] = B.dram("tap_" + t, [2, 128, 8, NT], F32, "ExternalOutput")

    cst = B.sbuf([128, 10, 128], F32, name="cst")
    IDENT, ONES, BLK1, PSW, ESE, ESO, TRIF, TRIB, STRF, STRB = (cst[:, i, :] for i in range(10))
    B.dma("sp", cst[:], consts, w=["cst"])
    small = B.sbuf([128, 64], F32, name="small")
    B.memset(small[:, 0:1], EPS, w=["small"])
    epsc = small[:, 0:1]
    gv = B.sbuf([128, 5, 8], F32, name="gv")
    B.dma("sp", gv[:], gvec, w=["gv"])
    evs = B.sbuf([128, 16], F32, name="evs")
    B.dma("sp", evs[:], ev_small, w=["evs"])
    evcw = B.sbuf([128, 4, 31], F32, name="evcw")
    B.dma("sp", evcw[:], ev_cw, w=["evcw"])
    fcw = B.sbuf([128, 2, 44, 3], F32, name="fcw")
    B.dma("sp", fcw[:], ffn_cw, w=["fcw"])
    fcb = B.sbuf([128, 2, 44], F32, name="fcb")
    B.dma("sp", fcb[:], ffn_cb, w=["fcb"])
    identb = B.sbuf([128, 128], BF16, name="identb")
    B.cp(identb[:], IDENT, r=["cst"], w=["identb"])
    h2st = B.sbuf([128, 8], BF16, name="h2st")
    mod = B.sbuf([128, 2, 48, 3], F32, name="mod")
    mA = B.sbuf([128, 2, 2, 8, 3], F32, name="mA")

    def stage_mod():
        C = Carver(P, 0)
        cv = C.get([8, 3], F32)
        sc = C.get([8, 3], F32)
        ab = C.get([2, 48], F32)
        B.dma("sp", cv, cvec, w=["cv"])
        B.dma("sp", ab, abT, w=["ab"])
        B.act(sc, cv, AF.Silu, r=["cv"], w=["sc"])
        wsl = [C.get([8, 512], F32) for _ in range(2)]
        for l in range(2):
            awr = ada_w[l].rearrange("(kt p) f -> p kt f", p=128)
            for g in range(12):
                wb = wsl[g % 2]
                wk = f"adaw{g % 2}"
                B.dma("sp", wb, awr[:, :, g * 512:(g + 1) * 512], w=[wk])
                for jj in range(4):
                    j = g * 4 + jj
                    pb = j % 4
                    for kt in range(8):
                        B.mm(ps[pb][:, 0:3], wb[:, kt, jj * 128:(jj + 1) * 128], sc[:, kt, :],
                             kt == 0, kt == 7, r=[wk, "sc"], w=[PSK[pb]])
                    B.act(mod[:, l, j, :], ps[pb][:, 0:3], AF.Identity, r=[PSK[pb], "ab"], w=["mod"],
                          bias=ab[:, l, j:j + 1])
            for ni, (gi, mi) in enumerate(((l, 1), (2 + l, 4))):
                for col in range(3):
                    B.ts(mA[:, l, ni, :, col], mod[:, l, mi * 8:(mi + 1) * 8, col], 1.0, None, ALU.add, None,
                         r=["mod"], w=["mA"])
                    B.tt(mA[:, l, ni, :, col], mA[:, l, ni, :, col], gv[:, gi, :], ALU.mult,
                         r=["mA", "gv"], w=["mA"])
        B.fence()

    stage_mod()


    PI = float(np.pi)

    def stage_filter():
        B.fence()
        C = Carver(P, 0)
        HD = C.get([16, 4, 512], F32)
        AB = P.view(C.off, [16, 2, 512], BF16)
        fT = C.get([2048], F32)
        h1 = C.get([2048], F32)
        h2 = C.get([2048], F32)
        w3 = C.get([2048], F32)
        tmp = [C.get([512], F32) for _ in range(2)]
        rn = C.get([512], F32)
        slab = [C.get([16, 128], BF16) for _ in range(3)]
        spb = [C.get([2, 512], F32) for _ in range(2)]
        dct = [C.get([512], F32) for _ in range(2)]
        sm = B.sbuf([128, 8], F32, name="hysm")
        w1s = B.sbuf([128, 64], F32, name="hyw1")
        w2s = B.sbuf([128, 64], F32, name="hyw2")
        B.dma("sp", sm[:], hy_small, w=["hysm"])
        B.dma("sp", w1s[0:33, :], hy_w1, w=["hyw1"])
        B.dma("sp", w2s[0:64, :], hy_w2, w=["hyw2"])
        B.dma("sp", fT[0:33, :], featsT, w=["fT"])
        B.dma("sp", w3[0:64, :], hy_w3, w=["w3"])
        B.tt(sm[0:64, 3:4], sm[0:64, 0:1], sm[0:64, 1:2], ALU.mult, r=["hysm"], w=["hysm"])
        B.tt(sm[0:64, 4:5], sm[0:64, 2:3], sm[0:64, 1:2], ALU.mult, r=["hysm"], w=["hysm"])

        def sin_layer(wT, kin, src, dst, bcol, srck, dstk, wk):
            for c in range(4):
                cs = slice(c * 512, (c + 1) * 512)
                B.mm(ps[c % 2][0:64, :512], wT[0:kin, 0:64], src[0:kin, cs], True, True, r=[wk, srck], w=[PSK[c % 2]])
                y = dst[0:64, cs]
                B.act(y, ps[c % 2][0:64, :512], AF.Identity, r=[PSK[c % 2], "hysm"], w=[dstk],
                      bias=sm[0:64, bcol:bcol + 1], scale=sm[0:64, 1:2])
                t = tmp[c % 2][0:64, :]
                tk = f"ftmp{c % 2}"
                for _ in range(2):
                    B.ts(t, y, PI, -2 * PI, ALU.is_gt, ALU.mult, r=[dstk], w=[tk])
                    B.tt(y, y, t, ALU.add, r=[dstk, tk], w=[dstk])
                    B.ts(t, y, -PI, 2 * PI, ALU.is_lt, ALU.mult, r=[dstk], w=[tk])
                    B.tt(y, y, t, ALU.add, r=[dstk, tk], w=[dstk])
                B.ts(y, y, PI, -PI, ALU.min, ALU.max, r=[dstk], w=[dstk])
                B.act(y, y, AF.Sin, r=[dstk], w=[dstk])

        sin_layer(w1s, 33, fT, h1, 3, "fT", "h1", "hyw1")
        sin_layer(w2s, 64, h1, h2, 4, "h1", "h2", "hyw2")
        for tt_ in range(16):
            d_ = dct[tt_ % 2]
            dk_ = f"dct{tt_ % 2}"
            B.dma("sp", d_, hy_decay[:, tt_, :], w=[dk_])
            for g in range(4):
                pb = (tt_ * 4 + g) % 4
                B.mm(ps[pb][:, :512], h2[0:64, tt_ * 128:(tt_ + 1) * 128], w3[0:64, g * 512:(g + 1) * 512], True, True,
                     r=["h2", "w3"], w=[PSK[pb]])
                B.tt(HD[:, tt_, g, :], ps[pb][:, :512], d_, ALU.mult, r=[PSK[pb], dk_], w=["HD"])
        for g in (1, 3):
            B.memset(HD[0:1, 0, g, :], 0.0, w=["HD"])
        B.fence()
        for o in range(2):
            i = 0
            for tt_ in range(16):
                for dr in range(2):
                    t = tmp[i % 2]
                    tk = f"ftmp{i % 2}"
                    B.act(t, HD[:, tt_, o * 2 + dr, :], AF.Abs, r=["HD"], w=[tk])
                    B.mm(ps[4][:, :512], ONES, t, i == 0, i == 31, r=["cst", tk], w=[PSK[4]])
                    i += 1
            B.recip(rn, ps[4][:, :512], r=[PSK[4]], w=["rn"])
            for tt_ in range(16):
                t = tmp[tt_ % 2]
                tk = f"ftmp{tt_ % 2}"
                B.tt(t, HD[:, tt_, o * 2, :], HD[:, tt_, o * 2 + 1, :], ALU.add, r=["HD"], w=[tk])
                B.tt(AB[:, tt_, 0, :], t, rn, ALU.mult, r=[tk, "rn"], w=["AB"])
                B.tt(t, HD[:, tt_, o * 2, :], HD[:, tt_, o * 2 + 1, :], ALU.subtract, r=["HD"], w=[tk])
                B.tt(AB[:, tt_, 1, :], t, rn, ALU.mult, r=[tk, "rn"], w=["AB"])
            q = 0
            for ft in range(17):
                sb_ = spb[ft % 2]
                sk_ = f"spb{ft % 2}"
                for part in range(2):
                    sl, slk = slab[q % 3], f"fslab{q % 3}"
                    q += 1
                    B.dma("sp", sl, dftf[ft, part], w=[slk])
                    pb = 5 + (ft * 2 + part) % 3
                    for tt_ in range(16):
                        B.mm(ps[pb][:, :512], sl[:, tt_, :], AB[:, tt_, part, :], tt_ == 0, tt_ == 15,
                             r=[slk, "AB"], w=[PSK[pb]])
                    if part == 0:
                        B.act(sb_[:, part, :], ps[pb][:, :512], AF.Identity, r=[PSK[pb]], w=[sk_])
                    else:
                        B.cp(sb_[:, part, :], ps[pb][:, :512], r=[PSK[pb]], w=[sk_])
                B.dma("sp", spec[o, ft], sb_, r=[sk_], w=[f"spec{o}"])
        B.fence()

    if 1 in layers and DO_HY:
        stage_filter()

    def modv(l, m, k, col):
        return mod[:, l, m * 8 + k, col:col + 1]

    def norm_mod(src_fn, n, dst_fn, l, ni, col, C, srck, dstk):
        sq = [C.get([512], F32) for _ in range(2)]
        rs = C.get([512], F32)
        tmp = [C.get([512], F32) for _ in range(2)]
        pb = 7
        kk = P.k("nm")
        for k in range(8):
            s = sq[k % 2]
            B.tt(s[:, :n], src_fn(k), src_fn(k), ALU.mult, r=srck, w=[kk + f"sq{k % 2}"])
            B.mm(ps[pb][:, :n], ONES, s[:, :n], k == 0, k == 7, r=[kk + f"sq{k % 2}", "cst"], w=[PSK[pb]])
        B.act(rs[:, :n], ps[pb][:, :n], AF.Sqrt, r=[PSK[pb], "small"], w=[kk + "rs"], bias=epsc, scale=1.0 / 1024)
        B.recip(rs[:, :n], rs[:, :n], r=[kk + "rs"], w=[kk + "rs"])
        mshift = 0 if ni == 0 else 3
        for k in range(8):
            t = tmp[k % 2]
            B.tt(t[:, :n], src_fn(k), rs[:, :n], ALU.mult, r=list(srck) + [kk + "rs"], w=[kk + f"t{k % 2}"])
            B.act(dst_fn(k), t[:, :n], AF.Identity, r=[kk + f"t{k % 2}", "mA", "mod"], w=dstk,
                  bias=modv(l, mshift, k, col), scale=mA[:, l, ni, k, col:col + 1])

    def ffn(l, b, X, lat_only):
        if lat_only:
            scs = [[(256, 1024, False, True)], [(1280, 1024, True, False)]]
        else:
            scs = [[(0, 256, False, False), (256, 896, False, True)], [(1152, 1152, True, False)]]
        wup = ffn_w_up[l].rearrange("(kt p) f -> p kt f", p=128)
        wdn = ffn_w_down[l].rearrange("(ft p) d -> p ft d", p=128)
        for si, segs in enumerate(scs):
            B.fence()
            C = Carver(P, 73728)
            t0 = segs[0][0] - (1 if segs[0][2] else 0)
            t1 = segs[-1][0] + segs[-1][1] + (1 if segs[-1][3] else 0)
            next_ = t1 - t0
            h2 = C.get([8, next_], BF16)
            ntok = sum(s[1] for s in segs)
            G = C.get([22, ntok], BF16)
            ubw = sum(s[1] + 2 for s in segs)
            ub = [C.get([2, ubw], BF16) for _ in range(2)]
            wsl = [C.get([8, 256], BF16) for _ in range(3)]
            dg = [C.get([6, 128], BF16) for _ in range(2)]
            wds = [C.get([22, 128], BF16) for _ in range(2)]
            sg = [C.get([512], F32) for _ in range(2)]
            h2k = P.k("h2")
            pos = t0
            if segs[0][2]:
                B.cp(h2[:, :, 0], h2st[:], r=["h2st"], w=[h2k])
                pos = t0 + 1
            while pos < t1:
                n = min(512, t1 - pos)
                if pos < 256 < pos + n:
                    n = 256 - pos
                col = 2 if pos < 256 else b
                Cn = Carver(P, C.off)
                norm_mod(lambda k, pos=pos, n=n: X[:, k, pos:pos + n], n,
                         lambda k, pos=pos, n=n: h2[:, k, pos - t0:pos - t0 + n], l, 1, col, Cn,
                         srck=["X"], dstk=[h2k])
                pos += n
            if si == 0:
                nxt = scs[1][0][0] - 1
                B.cp(h2st[:], h2[:, :, nxt - t0], r=[h2k], w=["h2st"])
            Gk = P.k("G")
            for j in range(22):
                wb = wsl[j % 3]
                wk = f"wup{j % 3}"
                B.dma("pool", wb[:, :, 0:128], wup[:, :, j * 128:(j + 1) * 128], w=[wk])
                B.dma("pool", wb[:, :, 128:256], wup[:, :, 2816 + j * 128:2816 + (j + 1) * 128], w=[wk])
                d = dg[j % 2]
                dk = f"dg{j % 2}"
                for vg in range(2):
                    for kk in range(3):
                        B.ts(d[:, vg * 3 + kk, :], IDENT, fcw[:, l, vg * 22 + j, kk:kk + 1], None, ALU.mult, None,
                             r=["cst", "fcw"], w=[dk])
                u = ub[j % 2]
                uk = f"ub{j % 2}"
                B.memset(u[:], 0.0, w=[uk], eng="pool")
                ucol = 0
                seg_ucol = []
                for (s0, sn, lh, rh) in segs:
                    seg_ucol.append(ucol)
                    a0 = s0 - (1 if lh else 0)
                    a1 = s0 + sn + (1 if rh else 0)
                    pos = a0
                    while pos < a1:
                        n = min(512, a1 - pos)
                        for vg in range(2):
                            pb = (2 * (pos // 512) + vg) % 4
                            for kt in range(8):
                                B.mm(ps[pb][:, :n], wb[:, kt, vg * 128:(vg + 1) * 128],
                                     h2[:, kt, pos - t0:pos - t0 + n], kt == 0, kt == 7,
                                     r=[wk, h2k], w=[PSK[pb]])
                            c0 = ucol + 1 + (pos - s0)
                            if vg == 0:
                                B.act(u[:, vg, c0:c0 + n], ps[pb][:, :n], AF.Identity, r=[PSK[pb]], w=[uk])
                            else:
                                B.cp(u[:, vg, c0:c0 + n], ps[pb][:, :n], r=[PSK[pb]], w=[uk])
                        pos += n
                    ucol += sn + 2
                gcol = 0
                for (s0, sn, lh, rh), uc in zip(segs, seg_ucol):
                    pos = 0
                    while pos < sn:
                        n = min(512, sn - pos)
                        pv, pg = 4 + (pos // 512) % 2 * 2, 5 + (pos // 512) % 2 * 2
                        for vg, pb in ((1, pg), (0, pv)):
                            for kk in range(3):
                                B.mm(ps[pb][:, :n], d[:, vg * 3 + kk, :], u[:, vg, uc + pos + kk:uc + pos + kk + n],
                                     kk == 0, kk == 2, r=[dk, uk], w=[PSK[pb]])
                        s = sg[(pos // 512) % 2]
                        sk = f"sg{(pos // 512) % 2}"
                        B.act(s[:, :n], ps[pg][:, :n], AF.Silu, r=[PSK[pg], "fcb"], w=[sk],
                              bias=fcb[:, l, 22 + j:23 + j], scale=1.0)
                        B.stt(G[:, j, gcol + pos:gcol + pos + n], ps[pv][:, :n], fcb[:, l, j:j + 1], s[:, :n],
                              ALU.add, ALU.mult, r=[PSK[pv], sk, "fcb"], w=[Gk])
                        pos += n
                    gcol += sn
            for dm in range(8):
                wd = wds[dm % 2]
                wdk = f"wdn{dm % 2}"
                B.dma("pool", wd, wdn[:, :, dm * 128:(dm + 1) * 128], w=[wdk])
                gcol = 0
                ci = 0
                for (s0, sn, lh, rh) in segs:
                    pos = 0
                    while pos < sn:
                        n = min(512, sn - pos)
                        pb = ci % 4
                        ci += 1
                        for ft in range(22):
                            B.mm(ps[pb][:, :n], wd[:, ft, :], G[:, ft, gcol + pos:gcol + pos + n],
                                 ft == 0, ft == 21, r=[wdk, Gk], w=[PSK[pb]])
                        col = 2 if s0 < 256 else b
                        xs = X[:, dm, s0 + pos:s0 + pos + n]
                        B.stt(xs, ps[pb][:, :n], modv(l, 5, dm, col), xs, ALU.mult, ALU.add,
                              r=[PSK[pb], "mod", "X"], w=["X"])
                        pos += n
                    gcol += sn
        B.fence()

    def u0col(tok):
        return tok + 15 if tok < 256 else tok + 45

    def layer0(b):
        l = 0
        B.fence()
        C = Carver(P, 0)
        QT = C.get([6, NT], BF16)
        KD = C.get([4, NT], BF16)
        VA = C.get([18, 4, 129], BF16)
        assert C.off <= 73728
        X = P.view(0, [8, NT], F32)
        C = Carver(P, 73728)
        H = C.get([8, NT], BF16)
        U0 = C.get([4, 2368], BF16)
        CO = C.get([4, NT], BF16)
        tbase = C.off
        xs = C.get([8, 512], F32)
        RC = C.get([2048], F32)
        RS = C.get([2048], F32)
        tmp_base = C.off
        B.dma("sp", RC, rope[:, 0, :], w=["RC"])
        B.dma("sp", RS, rope[:, 1, :], w=["RS"])
        B.memset(U0[:], 0.0, w=["U0"], eng="pool")
        B.memset(VA[:], 0.0, w=["VA"], eng="pool")
        B.memset(VA[:, :, :, 0:1], 1.0, w=["VA"], eng="pool")
        B.memset(VA[:, :, :, 128:129], 1.0, w=["VA"], eng="pool")
        for ci, (c0, n) in enumerate(CH):
            B.dma("sp", xs[:, :, :n], xin[b][:, :, c0:c0 + n], w=["xs"])
            Cn = Carver(P, tmp_base)
            norm_mod(lambda k, n=n: xs[:, k, :n], n, lambda k, c0=c0, n=n: H[:, k, c0:c0 + n], l, 0,
                     2 if ci == 0 else b, Cn, srck=["xs"], dstk=["H"])
        B.fence()
        Ct = Carver(P, tmp_base)
        wsl = [Ct.get([8, 128], BF16) for _ in range(3)]
        wv = P.view(tbase, [8, 256], BF16)
        qs_t = Ct.get([512], F32)
        q2_t = Ct.get([512], F32)
        rs_t = Ct.get([512], F32)
        t1_t = Ct.get([512], F32)
        t2_t = Ct.get([512], F32)
        win = ev_w_in.rearrange("(kt p) f -> p kt f", p=128)
        slab_i = [0]

        def load_slab(col_list):
            i = slab_i[0] % 3
            slab_i[0] += 1
            wb, wk = wsl[i], f"win{i}"
            wdt = 128 // len(col_list)
            for q, c in enumerate(col_list):
                B.dma("pool", wb[:, :, q * wdt:(q + 1) * wdt], win[:, :, c:c + wdt], w=[wk])
            return wb, wk

        def proj_chunk(wb, wk, c0, n, pb):
            for kt in range(8):
                B.mm(ps[pb][:, :n], wb[:, kt, :], H[:, kt, c0:c0 + n], kt == 0, kt == 7, r=[wk, "H"], w=[PSK[pb]])

        def qk_post(pb, ci, c0, n, gcol, dst, dstk):
            B.act(qs_t[:, :n], ps[pb][:, :n], AF.Identity, r=[PSK[pb], "evs"], w=["qs"], scale=evs[:, gcol:gcol + 1])
            B.act(q2_t[:, :n], ps[pb][:, :n], AF.Square, r=[PSK[pb]], w=["q2"])
            B.mm(ps[4][:, :n], BLK1, q2_t[:, :n], True, True, r=["cst", "q2"], w=[PSK[4]])
            if ci > 0:
                B.mm(ps[5][:, :n], PSW, qs_t[:, :n], True, True, r=["cst", "qs"], w=[PSK[5]])
            B.act(rs_t[:, :n], ps[4][:, :n], AF.Sqrt, r=[PSK[4], "small"], w=["rs"], bias=epsc, scale=1.0 / 64)
            B.recip(rs_t[:, :n], rs_t[:, :n], r=["rs"], w=["rs"])
            if ci > 0:
                r0 = c0 - 256
                B.tt(t1_t[:, :n], qs_t[:, :n], RC[:, r0:r0 + n], ALU.mult, r=["qs", "RC"], w=["t1"])
                B.tt(t2_t[:, :n], ps[5][:, :n], RS[:, r0:r0 + n], ALU.mult, r=[PSK[5], "RS"], w=["t2"])
                B.tt(t1_t[:, :n], t1_t[:, :n], t2_t[:, :n], ALU.add, r=["t1", "t2"], w=["t1"])
                B.tt(dst, t1_t[:, :n], rs_t[:, :n], ALU.mult, r=["t1", "rs"], w=dstk)
            else:
                B.tt(dst, qs_t[:, :n], rs_t[:, :n], ALU.mult, r=["qs", "rs"], w=dstk)

        for j in range(6):
            wb, wk = load_slab([j * 128])
            for ci, (c0, n) in enumerate(CH):
                pb = ci % 2
                proj_chunk(wb, wk, c0, n, pb)
                qk_post(pb, ci, c0, n, 0, QT[:, j, c0:c0 + n], ["QT"])
        for g in range(4):
            wb, wk = load_slab([768 + 64 * g, 768 + 64 * g])
            for ci, (c0, n) in enumerate(CH):
                pb = ci % 2
                proj_chunk(wb, wk, c0, n, pb)
                qk_post(pb, ci, c0, n, 1, KD[:, g, c0:c0 + n], ["KD"])
        B.dma("pool", wv, win[:, :, 1024:1280], w=["wv"])
        for tt_ in range(18):
            pb = tt_ % 2
            for kt in range(8):
                B.mm(ps[pb][:, 0:256], H[:, kt, tt_ * 128:(tt_ + 1) * 128], wv[:, kt, :], kt == 0, kt == 7,
                     r=["wv", "H"], w=[PSK[pb]])
            B.cp(VA[:, tt_, :, 64:128], ps[pb][:, 0:256].rearrange("p (g d) -> p g d", d=64), r=[PSK[pb]], w=["VA"])
        for j in range(4):
            wa, wak = load_slab([1280 + j * 128])
            wg, wgk = load_slab([1792 + j * 128])
            for ci, (c0, n) in enumerate(CH):
                pa, pg = 2 * (ci % 2), 2 * (ci % 2) + 1
                proj_chunk(wg, wgk, c0, n, pg)
                proj_chunk(wa, wak, c0, n, pa)
                B.act(qs_t[:, :n], ps[pg][:, :n], AF.Sigmoid, r=[PSK[pg]], w=["qs"])
                uc = u0col(c0)
                B.tt(U0[:, j, uc:uc + n], ps[pa][:, :n], qs_t[:, :n], ALU.mult, r=[PSK[pa], "qs"], w=["U0"])
        B.fence()
        Cd = Carver(P, tbase)
        DG = Cd.get([4, 31, 128], BF16)
        Ct = Carver(P, Cd.off)
        ub = Ct.get([4, 512], F32)
        u2 = [Ct.get([512], F32) for _ in range(2)]
        mean = Ct.get([512], F32)
        var = Ct.get([512], F32)
        dd = [Ct.get([512], F32) for _ in range(2)]
        for ct in range(4):
            for kk in range(31):
                B.ts(DG[:, ct, kk, :], IDENT, evcw[:, ct, kk:kk + 1], None, ALU.mult, None, r=["cst", "evcw"], w=["DG"])
        for ci, (c0, n) in enumerate(CH):
            uc = u0col(c0) - 15
            for ct in range(4):
                pb = ct % 2
                for kk in range(31):
                    B.mm(ps[pb][:, :n], DG[:, ct, kk, :], U0[:, ct, uc + kk:uc + kk + n], kk == 0, kk == 30,
                         r=["DG", "U0"], w=[PSK[pb]])
                B.act(ub[:, ct, :n], ps[pb][:, :n], AF.Identity, r=[PSK[pb], "evs"], w=["ub"],
                      bias=evs[:, 2 + ct:3 + ct], scale=1.0)
                B.tt(u2[ct % 2][:, :n], ub[:, ct, :n], ub[:, ct, :n], ALU.mult, r=["ub"], w=[f"u2{ct % 2}"])
                B.mm(ps[2][:, :n], ONES, ub[:, ct, :n], ct == 0, ct == 3, r=["cst", "ub"], w=[PSK[2]])
                B.mm(ps[3][:, :n], ONES, u2[ct % 2][:, :n], ct == 0, ct == 3, r=["cst", f"u2{ct % 2}"], w=[PSK[3]])
            B.act(mean[:, :n], ps[2][:, :n], AF.Identity, r=[PSK[2]], w=["mean"], scale=1.0 / 512)
            B.tt(var[:, :n], mean[:, :n], mean[:, :n], ALU.mult, r=["mean"], w=["var"])
            B.stt(var[:, :n], ps[3][:, :n], 1.0 / 512, var[:, :n], ALU.mult, ALU.subtract, r=[PSK[3], "var"], w=["var"])
            B.act(var[:, :n], var[:, :n], AF.Sqrt, r=["var", "small"], w=["var"], bias=epsc, scale=1.0)
            B.recip(var[:, :n], var[:, :n], r=["var"], w=["var"])
            for ct in range(4):
                d = dd[ct % 2]
                B.tt(d[:, :n], ub[:, ct, :n], mean[:, :n], ALU.subtract, r=["ub", "mean"], w=[f"dd{ct % 2}"])
                B.tt(d[:, :n], d[:, :n], var[:, :n], ALU.mult, r=[f"dd{ct % 2}", "var"], w=[f"dd{ct % 2}"])
                B.act(CO[:, ct, c0:c0 + n], d[:, :n], AF.Silu, r=[f"dd{ct % 2}", "evs"], w=["CO"],
                      bias=evs[:, 10 + ct:11 + ct], scale=evs[:, 6 + ct:7 + ct])
        B.fence()
        Ct = Carver(P, tbase)
        PT = [Ct.get([512], BF16) for _ in range(3)]
        OS = [Ct.get([512], F32) for _ in range(2)]
        ui = 0
        for h in range(12):
            g, par = h // 3, h % 2
            off = par * 64
            for ci, (c0, n) in enumerate(CH):
                nkt = 2 if ci == 0 else 18
                po = ui % 2
                O = OS[ui % 2]
                ok = f"OS{ui % 2}"
                ui += 1
                orows = 65 if par == 0 else 128

                def s_mm(kt, n=n, c0=c0, g=g, off=off, h=h):
                    pb = 2 + kt % 4
                    B.mm(ps[pb][:, :n], KD[off:off + 64, g, kt * 128:(kt + 1) * 128],
                         QT[off:off + 64, h // 2, c0:c0 + n], True, True, r=["KD", "QT"], w=[PSK[pb]])

                def pv(kt, n=n, g=g, par=par, po=po, nkt=nkt, orows=orows):
                    pb = 2 + kt % 4
                    pt, ptk = PT[kt % 3], f"PT{kt % 3}"
                    B.act(pt[:, :n], ps[pb][:, :n], AF.Exp, r=[PSK[pb]], w=[ptk], scale=0.125)
                    lhs = VA[:, kt, g, 64:129] if par == 0 else VA[:, kt, g, 0:128]
                    B.mm(ps[po][0:orows, :n], lhs, pt[:, :n], kt == 0, kt == nkt - 1, r=["VA", ptk], w=[PSK[po]])

                s_mm(0)
                if nkt > 1:
                    s_mm(1)
                for kt in range(nkt):
                    pv(kt)
                    if kt + 2 < nkt:
                        s_mm(kt + 2)
                B.act(O[0:orows, :n], ps[po][0:orows, :n], AF.Identity, r=[PSK[po]], w=[ok])
                dr = 64 if par == 0 else 0
                B.recip(O[dr:dr + 1, :n], O[dr:dr + 1, :n], r=[ok], w=[ok])
                if par == 0:
                    B.mm(ps[6][0:64, :n], ESE[0:65, 0:64], O[0:65, :n], True, True, r=["cst", ok], w=[PSK[6]])
                else:
                    B.mm(ps[6][:, :n], ESO, O[:, :n], True, True, r=["cst", ok], w=[PSK[6]])
                B.tt(H[off:off + 64, h // 2, c0:c0 + n], O[off:off + 64, :n], ps[6][off:off + 64, :n], ALU.mult,
                     r=[ok, PSK[6]], w=["H"])
        B.fence()
        Ct = Carver(P, tbase)
        wos = [Ct.get([10, 128], BF16) for _ in range(2)]
        wo = ev_w_out.rearrange("(ft p) d -> p ft d", p=128)
        for dm in range(8):
            w_, wk = wos[dm % 2], f"wo{dm % 2}"
            B.dma("pool", w_, wo[:, :, dm * 128:(dm + 1) * 128], w=[wk])
            B.dma("sp", X[:, dm, :], xin[b][:, dm, :], w=["X"])
            for ci, (c0, n) in enumerate(CH):
                pb = ci % 4
                for ft in range(10):
                    rhs = H[:, ft, c0:c0 + n] if ft < 6 else CO[:, ft - 6, c0:c0 + n]
                    B.mm(ps[pb][:, :n], w_[:, ft, :], rhs, ft == 0, ft == 9, r=[wk, "H", "CO"], w=[PSK[pb]])
                col = 2 if ci == 0 else b
                xs_ = X[:, dm, c0:c0 + n]
                B.stt(xs_, ps[pb][:, :n], modv(l, 2, dm, col), xs_, ALU.mult, ALU.add,
                      r=[PSK[pb], "mod", "X"], w=["X"])
        B.fence()
        if "xmid0" in P.taps:
            B.dma("sp", tapd["xmid0"][b], X, r=["X"], w=[P.k("tap")])
        ffn(0, b, X, lat_only=False)
        if "x0" in P.taps:
            B.dma("sp", tapd["x0"][b], X, r=["X"], w=[P.k("tap")])
        return X


    AX = mybir.AxisListType

    def sub(pk, hs=range(4)):
        return [pk]

    def layer1(b, X, do_hyena=True):
        l = 1
        B.fence()
        B.dma("sp", xsp[b], X, r=["X"], w=["xsp"])
        H = P.view(73728, [8, NT], BF16)
        for ci, (c0, n) in enumerate(CH):
            Cn = Carver(P, 161792)
            norm_mod(lambda k, c0=c0, n=n: X[:, k, c0:c0 + n], n, lambda k, c0=c0, n=n: H[:, k, c0:c0 + n], l, 0,
                     2 if ci == 0 else b, Cn, srck=["X"], dstk=["H"])
        B.fence()
        PT = P.view(0, [12, 2050], BF16)
        QF = P.view(49200, [4, NT], BF16)
        MIX = P.view(110592, [8, 2048], BF16)
        VT = P.view(143360, [18, 512], BF16)
        win = od_w_in.rearrange("(kt p) f -> p kt f", p=128)
        Ct = Carver(P, 161792)
        wsl = [Ct.get([8, 128], BF16) for _ in range(3)]
        B.memset(PT[:, :, 0:1], 0.0, w=["PT"])
        B.memset(PT[:, :, 2049:2050], 0.0, w=["PT"])
        for j in range(16):
            wb, wk = wsl[j % 3], f"w1in{j % 3}"
            B.dma("pool", wb, win[:, :, j * 128:(j + 1) * 128], w=[wk])
            for ci, (c0, n) in enumerate(CH):
                if j < 12 and ci == 0:
                    continue
                pb = ci % 4
                for kt in range(8):
                    B.mm(ps[pb][:, :n], wb[:, kt, :], H[:, kt, c0:c0 + n], kt == 0, kt == 7, r=[wk, "H"], w=[PSK[pb]])
                if j < 12:
                    dst, dk = PT[:, j, 1 + c0 - 256:1 + c0 - 256 + n], "PT"
                else:
                    dst, dk = QF[:, j - 12, c0:c0 + n], "QF"
                if ci % 2 == 0:
                    B.act(dst, ps[pb][:, :n], AF.Identity, r=[PSK[pb]], w=[dk])
                else:
                    B.cp(dst, ps[pb][:, :n], r=[PSK[pb]], w=[dk])
        B.fence()
        Ct = Carver(P, 161792)
        Wf = Ct.get([8, 512], BF16)
        Wi = Ct.get([8, 512], BF16)
        fr = Ct.get([512], F32)
        lg = Ct.get([512], F32)
        kk_ = Ct.get([512], F32)
        e1 = Ct.get([512], F32)
        kd = Ct.get([512], F32)
        ebT = Ct.get([4, 128], F32)
        kdec = Ct.get([512], BF16)
        qdT = Ct.get([4, 128], BF16)
        kdT = Ct.get([512], BF16)
        AT = Ct.get([4, 128], BF16)
        S32 = Ct.get([4, 128], F32)
        S16 = Ct.get([4, 128], BF16)
        dec = Ct.get([4, 2], F32)
        Cx = Carver(P, 110592, 110592 + 16384)
        lbv = Cx.get([2, 512], F32)
        oml = Cx.get([2, 512], F32)
        osb = Cx.get([2, 512], F32)
        ofl = Cx.get([2, 512], F32)
        Cy = Carver(P, 49200 + 18432, 73728)
        ss = Cy.get([8], F32)
        MSK = Cy.get([2, 4, 128], F32)
        gnb = Ct.get([512], F32)
        B.dma("sp", lbv, lbraw[:, 0], w=["lbv"])
        B.dma("sp", oml, lbraw[:, 1], w=["oml"])
        B.dma("sp", gnb, gnb_d, w=["gnb"])
        B.act(lbv, lbv, AF.Exp, r=["lbv"], w=["lbv"])
        B.act(oml, oml, AF.Exp, r=["oml"], w=["oml"])
        B.tt(lbv, lbv, oml, ALU.add, r=["lbv", "oml"], w=["lbv"])
        B.recip(lbv, lbv, r=["lbv"], w=["lbv"])
        B.tt(lbv, lbv, oml, ALU.mult, r=["lbv", "oml"], w=["lbv"])
        B.ts(oml, lbv, -1.0, 1.0, ALU.mult, ALU.add, r=["lbv"], w=["oml"])
        for hd in range(4):
            B.cp(MSK[:, 0, hd, :], TRIF, r=["cst"], w=["MSK"])
            B.cp(MSK[:, 1, hd, :], TRIB, r=["cst"], w=["MSK"])

        def wload(dst, col0, key):
            B.dma("pool", dst, win[:, :, col0:col0 + 512], w=[key])

        def hg_tile(tt_, d):
            lat = tt_ >= 2
            tl = tt_ - 2
            tsl = slice(tt_ * 128, (tt_ + 1) * 128)
            for kt in range(8):
                B.mm(ps[0][:, :512], H[:, kt, tsl], Wf[:, kt, :], kt == 0, kt == 7, r=["H", "Wf"], w=[PSK[0]])
            B.act(fr, ps[0][:, :512], AF.Sigmoid, r=[PSK[0]], w=["fr"])
            B.tt(fr, fr, oml[:, d, :], ALU.mult, r=["fr", "oml"], w=["fr"])
            B.tt(fr, fr, lbv[:, d, :], ALU.add, r=["fr", "lbv"], w=["fr"])
            B.act(lg, fr, AF.Ln, r=["fr"], w=["lg"])
            B.ts(kk_, fr, -1.0, 1.0, ALU.mult, ALU.add, r=["fr"], w=["kk"])
            if d == 0:
                for kt in range(8):
                    B.mm(ps[1][:, :512], H[:, kt, tsl], Wi[:, kt, :], kt == 0, kt == 7, r=["H", "Wi"], w=sub(PSK[1]))
                B.act(VT[:, tt_, :], ps[1][:, :512], AF.Identity, r=sub(PSK[1]), w=["VT"])
            TRI, STR = (TRIF, STRF) if d == 0 else (TRIB, STRB)
            B.mm(ps[2][:, :512], TRI, lg, True, True, r=["cst", "lg"], w=sub(PSK[2]))
            B.mm(ps[3][:, :512], STR, lg, True, True, r=["cst", "lg"], w=sub(PSK[3]))
            B.act(e1, ps[2][:, :512], AF.Exp, r=sub(PSK[2]), w=["e1"], scale=-1.0)
            B.tt(kd, kk_, e1, ALU.mult, r=["kk", "e1"], w=["kd"])
            B.act(e1, ps[3][:, :512], AF.Exp, r=sub(PSK[3]), w=["e1"])
            B.tt(kdec, kk_, e1, ALU.mult, r=["kk", "e1"], w=["kdec"])
            for hd in range(4):
                hs = slice(hd * 128, (hd + 1) * 128)
                B.mm(ps[4][:, hs], lg[:, hs], TRI, True, True, r=["cst", "lg"], w=[PSK[4]])
            p4 = ps[4][:, :512].rearrange("p (h t) -> p h t", t=128)
            B.act(ebT, p4, AF.Exp, r=sub(PSK[4]), w=["ebT"])
            B.tt(qdT, QF[:, :, tsl], ebT, ALU.mult, r=["QF", "ebT"], w=["qdT"])
            cols = (63, 127) if d == 0 else (0, 64)
            for c in range(2):
                B.act(dec[:, :, c:c + 1], p4[:, :, cols[c]:cols[c] + 1], AF.Exp, r=sub(PSK[4]), w=["dec"])
            for hd in range(4):
                hs = slice(hd * 128, (hd + 1) * 128)
                B.tr(ps[5][:, hs], kd[:, hs], IDENT, r=["kd", "cst"], w=[PSK[5]])
            B.act(kdT, ps[5][:, :512], AF.Identity, r=sub(PSK[5]), w=["kdT"])
            SK = os.environ.get("HG_SKIP", "")
            if lat and "a" not in SK:
                for hd in range(4):
                    hs = slice(hd * 128, (hd + 1) * 128)
                    B.mm(ps[6][:, hs], kdT[:, hs], qdT[:, hd, :], True, True, r=["kdT", "qdT"], w=[PSK[6]])
                B.tt(AT, ps[6][:, :512].rearrange("p (h t) -> p h t", t=128), MSK[:, d], ALU.mult,
                     r=sub(PSK[6]) + ["MSK"], w=["AT"])
            corder = (0, 1) if d == 0 else (1, 0)
            for qi, c in enumerate(corder):
                cs = slice(c * 64, (c + 1) * 64)
                po = 7 if c == 0 else 1
                if lat and "o" not in SK:
                    for hd in range(4):
                        hs = slice(hd * 128, (hd + 1) * 128)
                        B.mm(ps[po][0:64, hs], AT[:, hd, cs], VT[:, tt_, hs], True, False, r=["AT", "VT"],
                             w=[PSK[po]])
                        B.mm(ps[po][0:64, hs], qdT[:, hd, cs], S16[:, hd, :], False, True, r=["qdT", f"S16:{hd}"],
                             w=[PSK[po]])
                pu = 2 + qi
                for hd in range(4):
                    hs = slice(hd * 128, (hd + 1) * 128)
                    B.mm(ps[pu][:, hs], kdec[cs, hs], VT[cs, tt_, hs], True, True, r=["kdec", "VT"],
                         w=[PSK[pu]])
                for hd in range(4):
                    hs = slice(hd * 128, (hd + 1) * 128)
                    B.stt(S32[:, hd, :], S32[:, hd, :], dec[:, hd, c:c + 1], ps[pu][:, hs], ALU.mult, ALU.add,
                          r=[f"S32:{hd}", "dec", PSK[pu]], w=[f"S32:{hd}"])
                    B.act(S16[:, hd, :], S32[:, hd, :], AF.Identity, r=[f"S32:{hd}"], w=[f"S16:{hd}"])
            if not lat or "r" in SK:
                return
            if d == 0:
                B.act(osb[0:64, 0, :], ps[7][0:64, :512], AF.Identity, r=sub(PSK[7]), w=["osb"])
                B.cp(osb[0:64, 1, :], ps[1][0:64, :512], r=sub(PSK[1]), w=["osb"])
                B.dma("sp", ofs[b, tl].rearrange("p (c f) -> p c f", c=2), osb[0:64], r=["osb"], w=[f"ofs{tl}"])
                return
            B.dma("sp", ofl[0:64], ofs[b, tl].rearrange("p (c f) -> p c f", c=2), r=[f"ofs{tl}"], w=["ofl"])
            B.tt(osb[0:64, 0, :], ofl[0:64, 0, :], ps[7][0:64, :512], ALU.add, r=["ofl"] + sub(PSK[7]), w=["osb"])
            B.tt(osb[0:64, 1, :], ofl[0:64, 1, :], ps[1][0:64, :512], ALU.add, r=["ofl"] + sub(PSK[1]), w=["osb"])
            B.tt(ofl[0:64], osb[0:64], osb[0:64], ALU.mult, r=["osb"], w=["ofl"])
            B.op("dve", lambda e: e.tensor_reduce(ss[0:64, :], ofl[0:64].rearrange("p c (h e) -> p (c h) e", e=128),
                                                  AX.X, ALU.add), r=["ofl"], w=["ss"])
            B.act(ss[0:64, :], ss[0:64, :], AF.Sqrt, r=["ss", "small"], w=["ss"], bias=epsc[0:64], scale=1.0 / 128)
            B.recip(ss[0:64, :], ss[0:64, :], r=["ss"], w=["ss"])
            for c in range(2):
                for hd in range(4):
                    hs = slice(hd * 128, (hd + 1) * 128)
                    B.stt(osb[0:64, c, hs], osb[0:64, c, hs], ss[0:64, c * 4 + hd:c * 4 + hd + 1], gnb[0:64, hs],
                          ALU.mult, ALU.mult, r=["osb", "ss", "gnb"], w=["osb"])
            for c in range(2):
                pg = 4 + c
                for kt in range(8):
                    B.mm(ps[pg][0:64, :512], H[:, kt, tt_ * 128 + c * 64:tt_ * 128 + (c + 1) * 64], Wi[:, kt, :],
                         kt == 0, kt == 7, r=["H", "Wi"], w=sub(PSK[pg]))
                B.act(ofl[0:64, c, :], ps[pg][0:64, :512], AF.Silu, r=sub(PSK[pg]), w=["ofl"])
            B.tt(osb[0:64], osb[0:64], ofl[0:64], ALU.mult, r=["osb", "ofl"], w=["osb"])
            for c in range(2):
                for hd in range(4):
                    B.tr(ps[6][:, hd * 128 + c * 64:hd * 128 + (c + 1) * 64], osb[0:64, c, hd * 128:(hd + 1) * 128],
                         cst[0:64, 0, 0:64], r=["osb", "cst"], w=[PSK[6]])
            B.act(MIX[:, 4:8, tl * 128:(tl + 1) * 128], ps[6][:, :512].rearrange("p (h t) -> p h t", t=128),
                  AF.Identity, r=sub(PSK[6]), w=["MIX"])

        for d in range(int(os.environ.get("HG_ND", "2"))):
            wload(Wf, 1536 + 512 + d * 512, "Wf")
            wload(Wi, 1536 + (1536 if d == 0 else 2048), "Wi")
            for hd in range(4):
                B.memset(S32[:, hd, :], 0.0, w=[f"S32:{hd}"])
                B.memset(S16[:, hd, :], 0.0, w=[f"S16:{hd}"])
            order = list(range(18)) if d == 0 else [1, 0] + list(range(17, 1, -1))
            order = order[:int(os.environ.get("HG_NT", "18"))]
            for tt_ in order:
                hg_tile(tt_, d)
        B.fence()
        if "dlat" in P.taps:
            B.dma("pool", tapd["dlat"][b], MIX[:, 4:8, :], r=["MIX"], w=[P.k("tap")])

        if DO_HY:
            VX = P.view(143360, [3, 16, 512], BF16)
            Z2 = P.view(73728, [16, 512], BF16)
            Ct = Carver(P, 73728 + 16384, 110592)
            dgs = [Ct.get([3, 128], BF16) for _ in range(2)]
            uT = [Ct.get([512], F32) for _ in range(2)]
            hsw = Ct.get([12, 3], F32)
            hsb = Ct.get([12], F32)
            skb = Ct.get([2, 512], F32)
            B.dma("sp", hsw, hy_sw, w=["hsw"])
            B.dma("sp", hsb, hy_sb, w=["hsb"])
            B.dma("sp", skb, hy_skip, w=["skb"])
            for ct in range(12):
                dg_, dgk = dgs[ct % 2], f"hdg{ct % 2}"
                for kk in range(3):
                    B.ts(dg_[:, kk, :], IDENT, hsw[:, ct, kk:kk + 1], None, ALU.mult, None, r=["cst", "hsw"], w=[dgk])
                for c in range(4):
                    pb = c % 2
                    u_, uk = uT[c % 2], f"uT{c % 2}"
                    for kk in range(3):
                        B.mm(ps[pb][:, :512], dg_[:, kk, :], PT[:, ct, c * 512 + kk:c * 512 + kk + 512], kk == 0, kk == 2,
                             r=[dgk, "PT"], w=[PSK[pb]])
                    B.act(u_, ps[pb][:, :512], AF.Identity, r=[PSK[pb], "hsb"], w=[uk], bias=hsb[:, ct:ct + 1], scale=1.0)
                    pt_ = 2 + c % 2
                    for q in range(4):
                        B.tr(ps[pt_][:, q * 128:(q + 1) * 128], u_[:, q * 128:(q + 1) * 128], IDENT, r=[uk, "cst"],
                             w=[PSK[pt_]])
                    dst = VX[:, ct // 4, c * 4:(c + 1) * 4, (ct % 4) * 128:(ct % 4 + 1) * 128]
                    B.cp(dst, ps[pt_][:, :512].rearrange("p (q c) -> p q c", c=128), r=[PSK[pt_]], w=["VX"])
            B.fence()
            Ct = Carver(P, 0, 73728)
            Y = Ct.get([17, 2, 512], BF16)
            fsl = [Ct.get([16, 128], BF16) for _ in range(3)]
            isl = [Ct.get([17, 2, 128], BF16) for _ in range(2)]
            Ct2 = Carver(P, 73728 + 16384, 110592)
            skb = Ct2.get([2, 512], F32)
            spt = [Ct2.get([2, 512], F32) for _ in range(2)]
            t1s = [Ct2.get([512], F32) for _ in range(2)]
            B.dma("sp", skb, hy_skip, w=["skb2"])
            for o in range(2):
                zin = VX[:, 0] if o == 0 else Z2
                zk = "VX" if o == 0 else "Z2"
                q = 0
                for ft in range(17):
                    sp_, spk = spt[ft % 2], f"spt{ft % 2}"
                    B.dma("sp", sp_, spec[o, ft], r=[f"spec{o}"], w=[spk])
                    pz = [4 + 2 * (ft % 2), 5 + 2 * (ft % 2)]
                    for part in range(2):
                        sl, slk = fsl[q % 3], f"hfsl{q % 3}"
                        q += 1
                        B.dma("sp", sl, dftf[ft, part], w=[slk])
                        for tt_ in range(16):
                            B.mm(ps[pz[part]][:, :512], sl[:, tt_, :], zin[:, tt_, :], tt_ == 0, tt_ == 15, r=[slk, zk],
                                 w=[PSK[pz[part]]])
                    a, bq = t1s
                    B.tt(a, ps[pz[0]][:, :512], sp_[:, 0, :], ALU.mult, r=[PSK[pz[0]], spk], w=["t1a"])
                    B.tt(bq, ps[pz[1]][:, :512], sp_[:, 1, :], ALU.mult, r=[PSK[pz[1]], spk], w=["t1b"])
                    B.tt(Y[:, ft, 0, :], a, bq, ALU.subtract, r=["t1a", "t1b"], w=["Y"])
                    B.tt(a, ps[pz[0]][:, :512], sp_[:, 1, :], ALU.mult, r=[PSK[pz[0]], spk], w=["t1a"])
                    B.tt(bq, ps[pz[1]][:, :512], sp_[:, 0, :], ALU.mult, r=[PSK[pz[1]], spk], w=["t1b"])
                    B.tt(Y[:, ft, 1, :], a, bq, ALU.add, r=["t1a", "t1b"], w=["Y"])
                for tt_ in range(16):
                    il, ilk = isl[tt_ % 2], f"hisl{tt_ % 2}"
                    B.dma("sp", il, dfti[tt_], w=[ilk])
                    pb = tt_ % 2
                    i = 0
                    for ft in range(17):
                        for part in range(2):
                            B.mm(ps[pb][:, :512], il[:, ft, part, :], Y[:, ft, part, :], i == 0, i == 33, r=[ilk, "Y"],
                                 w=[PSK[pb]])
                            i += 1
                    a = t1s[tt_ % 2]
                    ak = f"t1{'ab'[tt_ % 2]}"
                    B.tt(a, zin[:, tt_, :], skb[:, o, :], ALU.mult, r=[zk, "skb2"], w=[ak])
                    B.tt(a, a, ps[pb][:, :512], ALU.add, r=[ak, PSK[pb]], w=[ak])
                    if o == 0:
                        B.tt(Z2[:, tt_, :], a, VX[:, 1, tt_, :], ALU.mult, r=[ak, "VX"], w=["Z2"])
                    else:
                        B.tt(a, a, VX[:, 2, tt_, :], ALU.mult, r=[ak, "VX"], w=[ak])
                        pt_ = 2 + tt_ % 2
                        for q4 in range(4):
                            B.tr(ps[pt_][:, q4 * 128:(q4 + 1) * 128], a[:, q4 * 128:(q4 + 1) * 128], IDENT, r=[ak, "cst"],
                                 w=[PSK[pt_]])
                        B.act(MIX[:, 0:4, tt_ * 128:(tt_ + 1) * 128],
                              ps[pt_][:, :512].rearrange("p (h t) -> p h t", t=128), AF.Identity, r=[PSK[pt_]], w=["MIX"])
            B.fence()
            if "clat" in P.taps:
                B.dma("pool", tapd["clat"][b], MIX[:, 0:4, :], r=["MIX"], w=[P.k("tap")])
        else:
            B.memset(MIX[:, 0:4, :], 0.0, w=["MIX"])
            B.fence()
        wos = [P.view(143360 + i * 2048, [8, 128], BF16) for i in range(2)]
        wo = od_w_out.rearrange("(ft p) d -> p ft d", p=128)
        for dm in range(8):
            w_, wk = wos[dm % 2], f"wo1{dm % 2}"
            B.dma("pool", w_, wo[:, :, dm * 128:(dm + 1) * 128], w=[wk])
            B.dma("sp", X[:, dm, 256:NT], xsp[b][:, dm, 256:NT], r=["xsp"], w=["X"])
            for c in range(4):
                pb = c % 4
                for ft in range(8):
                    B.mm(ps[pb][:, :512], w_[:, ft, :], MIX[:, ft, c * 512:(c + 1) * 512], ft == 0, ft == 7,
                         r=[wk, "MIX"], w=[PSK[pb]])
                xs_ = X[:, dm, 256 + c * 512:256 + (c + 1) * 512]
                B.stt(xs_, ps[pb][:, :512], modv(l, 2, dm, b), xs_, ALU.mult, ALU.add, r=[PSK[pb], "mod", "X"], w=["X"])
        B.fence()
        if "xmid1" in P.taps:
            B.dma("sp", tapd["xmid1"][b], X, r=["X"], w=[P.k("tap")])
        ffn(1, b, X, lat_only=True)
        return X

    def final(b, X):
        B.fence()
        C = Carver(P, 73728)
        ob = [C.get([8, 512], F32) for _ in range(2)]
        sq = [C.get([512], F32) for _ in range(2)]
        rs = C.get([512], F32)
        for ci in range(4):
            c0 = 256 + ci * 512
            n = 512
            o_, okk = ob[ci % 2], f"ob{ci % 2}"
            for k in range(8):
                s = sq[k % 2]
                B.tt(s[:, :n], X[:, k, c0:c0 + n], X[:, k, c0:c0 + n], ALU.mult, r=["X"], w=[f"fsq{k % 2}"])
                B.mm(ps[7][:, :n], ONES, s[:, :n], k == 0, k == 7, r=[f"fsq{k % 2}", "cst"], w=[PSK[7]])
            B.act(rs[:, :n], ps[7][:, :n], AF.Sqrt, r=[PSK[7], "small"], w=["frs"], bias=epsc, scale=1.0 / 1024)
            B.recip(rs[:, :n], rs[:, :n], r=["frs"], w=["frs"])
            for k in range(8):
                B.stt(o_[:, k, :], X[:, k, c0:c0 + n], gv[:, 4, k:k + 1], rs[:, :n], ALU.mult, ALU.mult,
                      r=["X", "gv", "frs"], w=[okk])
            B.dma("sp", out[b][:, :, ci * 512:(ci + 1) * 512], o_, r=[okk], w=[P.k("out")])

    import os
    for b in range(2):
        if 0 in layers:
            X = layer0(b)
        else:
            X = P.view(0, [8, NT], F32)
            B.dma("sp", X, xin[b], w=["X"])
        if 1 in layers:
            layer1(b, X)
        final(b, X)
    nc = B.build()
    return nc, P


def host_consts():
    c = np.zeros((128, 10, 128), np.float32)
    ii = np.arange(128)
    same = (ii[:, None] // 64) == (ii[None, :] // 64)
    c[:, 6, :] = same & (ii[:, None] <= ii[None, :])
    c[:, 7, :] = same & (ii[:, None] >= ii[None, :])
    c[:, 8, :] = same & (ii[:, None] > ii[None, :])
    c[:, 9, :] = same & (ii[:, None] < ii[None, :])
    c[:, 0, :] = np.eye(128)
    c[:, 1, :] = 1.0
    c[0:64, 2, 0:64] = 1.0
    c[64:128, 2, 64:128] = 1.0
    for i in range(64):
        c[2 * i + 1, 3, 2 * i] = -1.0
        c[2 * i, 3, 2 * i + 1] = 1.0
    c[64, 4, 0:64] = 1.0
    c[0, 5, 64:128] = 1.0
    return c


def host_rope():
    L, GW, dh = 2048, 64, 64
    row = np.repeat(np.arange(L // GW, dtype=np.float32), GW)
    col = np.tile(np.arange(GW, dtype=np.float32), L // GW)
    axis_dim = dh // 2
    inv = (10000.0 ** (-np.arange(0, axis_dim, 2, dtype=np.float32) / axis_dim)).astype(np.float32)
    ang = np.concatenate([row[:, None] * inv, col[:, None] * inv], axis=-1)
    cos, sin = np.cos(ang).astype(np.float32), np.sin(ang).astype(np.float32)
    r = np.zeros((128, 2, L), np.float32)
    for p in range(128):
        i = (p % 64) // 2
        r[p, 0] = cos[:, i]
        r[p, 1] = sin[:, i]
    return r

_HC = {}


def host_hyena_consts():
    if _HC:
        return _HC
    L, N = 2048, 4096
    f32 = np.float32
    t = np.linspace(0.0, 1.0, L, dtype=f32)[:, None]
    bands = 16
    fb = np.linspace(1e-4, bands - 1, bands, dtype=f32)
    wpos = (f32(2.0 * math.pi / L) * np.arange(L, dtype=f32))[:, None]
    feats = np.concatenate([t, np.cos(fb * wpos), -np.sin(fb * wpos)], axis=-1).astype(f32)
    _HC["featsT"] = np.ascontiguousarray(feats.T)
    deltas = np.abs(np.linspace(math.log(1e-2) / 1.5, math.log(1e-2) / 0.3, 512, dtype=f32))
    dec = np.exp(-t * deltas).astype(f32)
    _HC["hy_decay"] = np.ascontiguousarray(np.transpose(dec.reshape(16, 128, 512), (1, 0, 2)))
    tt = np.arange(L, dtype=np.int64)[:, None]
    ff = np.arange(17 * 128, dtype=np.int64)[None, :]
    ang = 2.0 * np.pi * ((tt * ff) % N).astype(np.float64) / N
    valid = (ff <= L).astype(np.float64)
    Mc = np.cos(ang) * valid
    Ms = -np.sin(ang) * valid
    fw = np.stack([Mc, Ms], 0).reshape(2, 16, 128, 17, 128)
    _HC["dftf"] = np.ascontiguousarray(np.transpose(fw, (3, 0, 2, 1, 4))).astype(ml_dtypes.bfloat16)
    wf = np.where((ff == 0) | (ff == L), 1.0, 2.0) * valid / N
    Ic = (np.cos(ang) * wf).T
    Is = (-np.sin(ang) * wf).T
    iw = np.stack([Ic, Is], 0).reshape(2, 17, 128, 16, 128)
    _HC["dfti"] = np.ascontiguousarray(np.transpose(iw, (3, 2, 1, 0, 4))).astype(ml_dtypes.bfloat16)
    return _HC


def fm(v):
    return np.ascontiguousarray(v.reshape(-1, 128).T)


def prep_inputs(inp):
    f = lambda a: np.ascontiguousarray(np.asarray(a, dtype=np.float32))
    x, ctx = f(inp["x"]), f(inp["ctx"])
    shared = {
        "ada_w": f(inp["ada_w"]),
        "abT": np.ascontiguousarray(np.stack([fm(inp["ada_b"][l]) for l in range(2)], axis=1)),
        "gvec": np.ascontiguousarray(np.stack([fm(inp["norm1_g"][0]), fm(inp["norm1_g"][1]), fm(inp["norm2_g"][0]),
                                               fm(inp["norm2_g"][1]), fm(inp["final_g"])], axis=1)),
        "consts": host_consts(),
        "rope": host_rope(),
        "ev_w_in": f(inp["ev_w_in"][0]),
        "ev_cw": np.ascontiguousarray(np.transpose(f(inp["ev_conv_w"][0]).reshape(31, 4, 128), (2, 1, 0))),
        "ev_w_out": f(inp["ev_w_out"][0]),
        "ffn_w_up": f(inp["ffn_w_up"]),
        "ffn_cw": np.ascontiguousarray(np.transpose(f(inp["ffn_conv_w"]).reshape(2, 3, 44, 128), (3, 0, 2, 1))),
        "ffn_cb": np.ascontiguousarray(np.transpose(f(inp["ffn_conv_b"]).reshape(2, 44, 128), (2, 0, 1))),
        "ffn_w_down": f(inp["ffn_w_down"]),
        "od_w_in": f(inp["od_w_in"][0]),
        "od_w_out": f(inp["od_w_out"][0]),
        "lbraw": np.ascontiguousarray(np.broadcast_to(f(inp["od_lower_bound"])[None], (128, 2, 2, 512))),
        "gnb": np.ascontiguousarray(np.broadcast_to(np.tile(f(inp["od_gnorm_g"][0]), 4)[None], (128, 512))),
        "hy_w1": f(inp["od_filt_w1"][0]),
        "hy_w2": f(inp["od_filt_w2"][0]),
        "hy_w3": f(inp["od_filt_w3"][0]),
        "hy_sw": np.ascontiguousarray(np.transpose(f(inp["od_short_w"][0]).reshape(3, 12, 128), (2, 1, 0))),
        "hy_sb": fm(f(inp["od_short_b"][0])),
        "hy_skip": np.ascontiguousarray(np.broadcast_to(f(inp["od_hyena_skip"][0])[None], (128, 2, 512))),
    }
    hs = np.zeros((128, 8), np.float32)
    hs[0:64, 0] = f(inp["od_filt_b1"][0])
    hs[0:64, 1] = f(inp["od_filt_freq"][0])
    hs[0:64, 2] = f(inp["od_filt_b2"][0])
    shared["hy_small"] = hs
    shared.update(host_hyena_consts())
    evs = np.zeros((128, 16), np.float32)
    evs[:, 0] = np.tile(f(inp["ev_q_gain"][0]), 2)
    evs[:, 1] = np.tile(f(inp["ev_k_gain"][0]), 2)
    evs[:, 2:6] = fm(f(inp["ev_conv_b"][0]))
    evs[:, 6:10] = fm(f(inp["ev_ln_g"][0]))
    evs[:, 10:14] = fm(f(inp["ev_ln_b"][0]))
    shared["ev_small"] = evs
    maps = []
    for core in range(8):
        m = dict(shared)
        xi = np.zeros((2, 128, 8, NT), np.float32)
        cv = np.zeros((128, 8, 3), np.float32)
        for q in range(2):
            bidx = 2 * core + q
            full = np.concatenate([ctx[bidx], x[bidx]], axis=0)
            xi[q] = np.transpose(full.reshape(NT, 8, 128), (2, 1, 0))
            cv[:, :, q] = fm(f(inp["c"][bidx]))
        cv[:, :, 2] = fm(f(inp["c_ctx"]))
        m["xin"] = xi
        m["cvec"] = cv
        maps.append(m)
    return maps


_CACHE = {}


def kernel(**inputs):
    maps = prep_inputs(inputs)
    if "nc" not in _CACHE:
        _CACHE["nc"] = build_program()[0]
    nc = _CACHE["nc"]
    res = run_bass_kernel_spmd(nc, maps, core_ids=list(range(8)))
    outp = np.zeros((16, 2048, 1024), np.float32)
    for core in range(8):
        o = res.results[core]["out"]
        for q in range(2):
            outp[2 * core + q] = np.transpose(o[q], (2, 1, 0)).reshape(2048, 1024)
    return outp
```

```python
from contextlib import ExitStack
import math
import numpy as np
import ml_dtypes
import concourse.bass as bass
import concourse.mybir as mybir
from concourse.bass_utils import run_bass_kernel_spmd

F32 = mybir.dt.float32
BF16 = mybir.dt.bfloat16
ALU = mybir.AluOpType
AF = mybir.ActivationFunctionType

N_DMA_SEMS = 40
NT = 2304
CH = [(0, 256), (256, 512), (768, 512), (1280, 512), (1792, 512)]
EPS = 1e-6


class Builder:
    def __init__(self):
        self.nc = bass.Bass("TRN2", target_bir_lowering=False)
        self.ops = []
        self.es = ExitStack()
        self._n = 0

    def dram(self, name, shape, dtype, kind):
        return self.nc.dram_tensor(name, list(shape), dtype, kind=kind).ap()

    def sbuf(self, shape, dtype, name=None):
        self._n += 1
        return self.es.enter_context(self.nc.sbuf_tensor(name or f"sb{self._n}", list(shape), dtype))

    def psum(self, shape, dtype, name=None):
        self._n += 1
        return self.es.enter_context(self.nc.psum_tensor(name or f"ps{self._n}", list(shape), dtype))

    def op(self, stream, fn, r=(), w=(), acc=False):
        self.ops.append((stream, "c", fn, tuple(r), tuple(w), acc))

    def dma(self, queue, out, in_, r=(), w=()):
        self.ops.append((queue, "d", (out, in_), tuple(r), tuple(w), False))

    def fence(self):
        self.ops.append((None, "f", None, (), (), False))

    def mm(self, out, lhsT, rhs, start, stop, r, w):
        self.op("pe", lambda e: e.matmul(out, lhsT, rhs, start=start, stop=stop), r=r, w=w, acc=not start)

    def tr(self, out, in_, ident, r, w):
        self.op("pe", lambda e: e.transpose(out, in_, ident), r=r, w=w)

    def act(self, out, in_, func, r, w, bias=None, scale=None):
        kw = {}
        if bias is not None:
            kw["bias"] = bias
        if scale is not None:
            kw["scale"] = scale
        self.op("act", lambda e: e.activation(out, in_, func, **kw), r=r, w=w)

    def tt(self, out, a, b, op, r, w, eng="dve"):
        self.op(eng, lambda e: e.tensor_tensor(out, a, b, op), r=r, w=w)

    def ts(self, out, a, s1, s2, op0, op1, r, w, eng="dve"):
        if s2 is None:
            self.op(eng, lambda e: e.tensor_scalar(out, a, s1, None, op0), r=r, w=w)
        else:
            self.op(eng, lambda e: e.tensor_scalar(out, a, s1, s2, op0, op1), r=r, w=w)

    def stt(self, out, a, s, b, op0, op1, r, w):
        self.op("dve", lambda e: e.scalar_tensor_tensor(out, a, s, b, op0, op1), r=r, w=w)

    def cp(self, out, in_, r, w, eng="dve"):
        self.op(eng, lambda e: e.tensor_copy(out, in_), r=r, w=w)

    def recip(self, out, in_, r, w):
        self.op("dve", lambda e: e.reciprocal(out, in_), r=r, w=w)

    def memset(self, ap, v, w, eng="dve"):
        self.op(eng, lambda e: e.memset(ap, v), w=w)

    def build(self):
        nc = self.nc
        ops = self.ops
        n = len(ops)
        streams = ["pe", "act", "dve", "pool", "sp"]
        last_w = {}
        readers = {}
        deps = [set() for _ in range(n)]
        last_on = {s: None for s in streams}
        open_dmas = []
        pending_fence = {s: None for s in streams}
        for i, (st, kind, fn, R, W, acc) in enumerate(ops):
            if kind == "f":
                fd = set(j for j in last_on.values() if j is not None) | set(open_dmas)
                for s in streams:
                    pending_fence[s] = (pending_fence[s] or set()) | fd
                open_dmas = []
                continue
            if pending_fence[st]:
                deps[i] |= pending_fence[st]
                pending_fence[st] = None
            for k in R:
                if k in last_w:
                    deps[i].add(last_w[k])
            for k in W:
                if k in last_w:
                    j = last_w[k]
                    if not ((acc or st == "pe") and ops[j][0] == st and ops[j][1] == "c" and kind == "c"):
                        deps[i].add(j)
                for rr in readers.get(k, ()):
                    deps[i].add(rr)
            for k in R:
                readers.setdefault(k, []).append(i)
            for k in W:
                last_w[k] = i
                readers[k] = []
            deps[i].discard(i)
            if kind == "c":
                last_on[st] = i
            else:
                open_dmas.append(i)
        dma_sem_of, dma_val_of = {}, {}
        sem_count = [0] * N_DMA_SEMS
        sem_last = [None] * N_DMA_SEMS
        qn = {"sp": 0, "pool": 0}
        NSP = 24
        for i in range(n):
            if ops[i][1] != "d":
                continue
            if ops[i][0] == "sp":
                s = qn["sp"] % NSP
            else:
                s = NSP + qn["pool"] % (N_DMA_SEMS - NSP)
            qn[ops[i][0]] += 1
            if sem_last[s] is not None:
                deps[i].add(sem_last[s])
            sem_last[s] = i
            sem_count[s] += 16
            dma_sem_of[i] = s
            dma_val_of[i] = sem_count[s]
        needed = [False] * n
        for i in range(n):
            for j in deps[i]:
                needed[j] = True
        cnt = {s: 0 for s in streams}
        sigval = {}
        for i, (st, kind, *_r) in enumerate(ops):
            if kind == "c" and needed[i]:
                cnt[st] += 1
                sigval[i] = cnt[st]
        es = self.es
        esem = {s: es.enter_context(nc.semaphore(f"sem_{s}")) for s in streams}
        dsem = [es.enter_context(nc.semaphore(f"dsem{k}")) for k in range(N_DMA_SEMS)]
        block = es.enter_context(nc.Block())

        def target(j):
            if ops[j][1] == "d":
                return ("d", dma_sem_of[j]), dsem[dma_sem_of[j]], dma_val_of[j]
            return ("c", ops[j][0]), esem[ops[j][0]], sigval[j]

        def emit_stream(st, eng):
            waited = {}
            for i, (s2, kind, fn, R, W, acc) in enumerate(ops):
                if s2 != st:
                    continue
                need = {}
                for j in deps[i]:
                    key, sem, val = target(j)
                    if waited.get(key, 0) >= val:
                        continue
                    if need.get(key, (None, 0))[1] < val:
                        need[key] = (sem, val)
                for key, (sem, val) in need.items():
                    eng.wait_ge(sem, val)
                    waited[key] = val
                if kind == "c":
                    ins = fn(eng)
                    if needed[i]:
                        ins.then_inc(esem[st], 1)
                else:
                    out, in_ = fn
                    eng.dma_start(out=out, in_=in_).then_inc(dsem[dma_sem_of[i]], 16)
            if st == "sp":
                for s in range(N_DMA_SEMS):
                    if sem_count[s] > 0 and waited.get(("d", s), 0) < sem_count[s]:
                        eng.wait_ge(dsem[s], sem_count[s])
                for s3 in streams:
                    if cnt[s3] > 0 and waited.get(("c", s3), 0) < cnt[s3]:
                        eng.wait_ge(esem[s3], cnt[s3])

        @block.tensor
        def _(e):
            emit_stream("pe", e)

        @block.scalar
        def _(e):
            emit_stream("act", e)

        @block.vector
        def _(e):
            emit_stream("dve", e)

        @block.gpsimd
        def _(e):
            emit_stream("pool", e)

        @block.sync
        def _(e):
            emit_stream("sp", e)

        es.close()
        return nc


ARENA_BYTES = 200704


class Prog:
    def __init__(self, taps=()):
        self.B = Builder()
        self.taps = set(taps)
        self.uid = 0
        B = self.B
        self.arena = B.sbuf([128, ARENA_BYTES // 4], F32, name="arena")
        self.ps = [B.psum([128, 512], F32, name=f"psb{i}") for i in range(8)]
        self.outs = {}

    def k(self, s):
        self.uid += 1
        return f"{s}#{self.uid}"

    def view(self, off, free_shape, dtype):
        esz = 4 if dtype == F32 else 2
        nel = int(np.prod(free_shape))
        nb = nel * esz
        assert off % 4 == 0 and nb % 4 == 0 and off + nb <= ARENA_BYTES, (off, nb)
        a = self.arena[:, off // 4:(off + nb) // 4]
        if dtype != F32:
            a = a.bitcast(dtype)
        if len(free_shape) == 2:
            a = a.rearrange("p (a b) -> p a b", b=free_shape[1])
        elif len(free_shape) == 3:
            a = a.rearrange("p (a b c) -> p a b c", b=free_shape[1], c=free_shape[2])
        return a


class Carver:
    def __init__(self, P, base, limit=ARENA_BYTES):
        self.P, self.off, self.limit = P, base, limit

    def get(self, free_shape, dtype):
        esz = 4 if dtype == F32 else 2
        nb = (int(np.prod(free_shape)) * esz + 3) // 4 * 4
        v = self.P.view(self.off, free_shape, dtype)
        self.off += nb
        assert self.off <= self.limit, (self.off, self.limit)
        return v


def build_program(taps=(), layers=(0, 1)):
    import os
    DO_HY = os.environ.get('NO_HY', '') == ''
    P = Prog(taps)
    B = P.B
    ps = P.ps
    PSK = [f"psb{i}" for i in range(8)]

    def din(name, shape, dt=F32):
        return B.dram(name, shape, dt, "ExternalInput")

    xin = din("xin", [2, 128, 8, NT])
    cvec = din("cvec", [128, 8, 3])
    ada_w = din("ada_w", [2, 1024, 6144])
    abT = din("abT", [128, 2, 48])
    gvec = din("gvec", [128, 5, 8])
    consts = din("consts", [128, 10, 128])
    rope = din("rope", [128, 2, 2048])
    ev_w_in = din("ev_w_in", [1024, 2304])
    ev_small = din("ev_small", [128, 16])
    ev_cw = din("ev_cw", [128, 4, 31])
    ev_w_out = din("ev_w_out", [1280, 1024])
    ffn_w_up = din("ffn_w_up", [2, 1024, 5632])
    ffn_cw = din("ffn_cw", [128, 2, 44, 3])
    ffn_cb = din("ffn_cb", [128, 2, 44])
    ffn_w_down = din("ffn_w_down", [2, 2816, 1024])
    od_w_in = din("od_w_in", [1024, 4096])
    od_w_out = din("od_w_out", [1024, 1024])
    lbraw = din("lbraw", [128, 2, 2, 512])
    gnb_d = din("gnb", [128, 512])
    featsT = din("featsT", [33, 2048])
    hy_w1 = din("hy_w1", [33, 64])
    hy_w2 = din("hy_w2", [64, 64])
    hy_w3 = din("hy_w3", [64, 2048])
    hy_small = din("hy_small", [128, 8])
    hy_decay = din("hy_decay", [128, 16, 512])
    hy_sw = din("hy_sw", [128, 12, 3])
    hy_sb = din("hy_sb", [128, 12])
    hy_skip = din("hy_skip", [128, 2, 512])
    dftf = din("dftf", [17, 2, 128, 16, 128], BF16)
    dfti = din("dfti", [16, 128, 17, 2, 128], BF16)
    spec = B.dram("spec", [2, 17, 128, 2, 512], F32, "Internal")
    xsp = B.dram("xsp", [2, 128, 8, NT], F32, "Internal")
    ofs = B.dram("ofs", [2, 16, 64, 1024], F32, "Internal")
    out = B.dram("out", [2, 128, 8, 2048], F32, "ExternalOutput")
    tapd = {}
    for t in P.taps:
        if t in ("dlat", "clat"):
            tapd[t] = B.dram("tap_" + t, [2, 128, 4, 2048], F32, "ExternalOutput")
        else:
            tapd[t] = B.dram("tap_" + t, [2, 128, 8, NT], F32, "ExternalOutput")

    cst = B.sbuf([128, 10, 128], F32, name="cst")
    IDENT, ONES, BLK1, PSW, ESE, ESO, TRIF, TRIB, STRF, STRB = (cst[:, i, :] for i in range(10))
    B.dma("sp", cst[:], consts, w=["cst"])
    small = B.sbuf([128, 64], F32, name="small")
    B.memset(small[:, 0:1], EPS, w=["small"])
    epsc = small[:, 0:1]
    gv = B.sbuf([128, 5, 8], F32, name="gv")
    B.dma("sp", gv[:], gvec, w=["gv"])
    evs = B.sbuf([128, 16], F32, name="evs")
    B.dma("sp", evs[:], ev_small, w=["evs"])
    evcw = B.sbuf([128, 4, 31], F32, name="evcw")
    B.dma("sp", evcw[:], ev_cw, w=["evcw"])
    fcw = B.sbuf([128, 2, 44, 3], F32, name="fcw")
    B.dma("sp", fcw[:], ffn_cw, w=["fcw"])
    fcb = B.sbuf([128, 2, 44], F32, name="fcb")
    B.dma("sp", fcb[:], ffn_cb, w=["fcb"])
    identb = B.sbuf([128, 128], BF16, name="identb")
    B.cp(identb[:], IDENT, r=["cst"], w=["identb"])
    h2st = B.sbuf([128, 8], BF16, name="h2st")
    mod = B.sbuf([128, 2, 48, 3], F32, name="mod")
    mA = B.sbuf([128, 2, 2, 8, 3], F32, name="mA")

    def stage_mod():
        C = Carver(P, 0)
        cv = C.get([8, 3], F32)
        sc = C.get([8, 3], F32)
        ab = C.get([2, 48], F32)
        B.dma("sp", cv, cvec, w=["cv"])
        B.dma("sp", ab, abT, w=["ab"])
        B.act(sc, cv, AF.Silu, r=["cv"], w=["sc"])
        wsl = [C.get([8, 512], F32) for _ in range(2)]
        for l in range(2):
            awr = ada_w[l].rearrange("(kt p) f -> p kt f", p=128)
            for g in range(12):
                wb = wsl[g % 2]
                wk = f"adaw{g % 2}"
                B.dma("sp", wb, awr[:, :, g * 512:(g + 1) * 512], w=[wk])
                for jj in range(4):
                    j = g * 4 + jj
                    pb = j % 4
                    for kt in range(8):
                        B.mm(ps[pb][:, 0:3], wb[:, kt, jj * 128:(jj + 1) * 128], sc[:, kt, :],
                             kt == 0, kt == 7, r=[wk, "sc"], w=[PSK[pb]])
                    B.act(mod[:, l, j, :], ps[pb][:, 0:3], AF.Identity, r=[PSK[pb], "ab"], w=["mod"],
                          bias=ab[:, l, j:j + 1])
            for ni, (gi, mi) in enumerate(((l, 1), (2 + l, 4))):
                for col in range(3):
                    B.ts(mA[:, l, ni, :, col], mod[:, l, mi * 8:(mi + 1) * 8, col], 1.0, None, ALU.add, None,
                         r=["mod"], w=["mA"])
                    B.tt(mA[:, l, ni, :, col], mA[:, l, ni, :, col], gv[:, gi, :], ALU.mult,
                         r=["mA", "gv"], w=["mA"])
        B.fence()

    stage_mod()


    PI = float(np.pi)

    def stage_filter():
        B.fence()
        C = Carver(P, 0)
        HD = C.get([16, 4, 512], F32)
        AB = P.view(C.off, [16, 2, 512], BF16)
        fT = C.get([2048], F32)
        h1 = C.get([2048], F32)
        h2 = C.get([2048], F32)
        w3 = C.get([2048], F32)
        tmp = [C.get([512], F32) for _ in range(2)]
        rn = C.get([512], F32)
        slab = [C.get([16, 128], BF16) for _ in range(3)]
        spb = [C.get([2, 512], F32) for _ in range(2)]
        dct = [C.get([512], F32) for _ in range(2)]
        sm = B.sbuf([128, 8], F32, name="hysm")
        w1s = B.sbuf([128, 64], F32, name="hyw1")
        w2s = B.sbuf([128, 64], F32, name="hyw2")
        B.dma("sp", sm[:], hy_small, w=["hysm"])
        B.dma("sp", w1s[0:33, :], hy_w1, w=["hyw1"])
        B.dma("sp", w2s[0:64, :], hy_w2, w=["hyw2"])
        B.dma("sp", fT[0:33, :], featsT, w=["fT"])
        B.dma("sp", w3[0:64, :], hy_w3, w=["w3"])
        B.tt(sm[0:64, 3:4], sm[0:64, 0:1], sm[0:64, 1:2], ALU.mult, r=["hysm"], w=["hysm"])
        B.tt(sm[0:64, 4:5], sm[0:64, 2:3], sm[0:64, 1:2], ALU.mult, r=["hysm"], w=["hysm"])

        def sin_layer(wT, kin, src, dst, bcol, srck, dstk, wk):
            for c in range(4):
                cs = slice(c * 512, (c + 1) * 512)
                B.mm(ps[c % 2][0:64, :512], wT[0:kin, 0:64], src[0:kin, cs], True, True, r=[wk, srck], w=[PSK[c % 2]])
                y = dst[0:64, cs]
                B.act(y, ps[c % 2][0:64, :512], AF.Identity, r=[PSK[c % 2], "hysm"], w=[dstk],
                      bias=sm[0:64, bcol:bcol + 1], scale=sm[0:64, 1:2])
                t = tmp[c % 2][0:64, :]
                tk = f"ftmp{c % 2}"
                for _ in range(2):
                    B.ts(t, y, PI, -2 * PI, ALU.is_gt, ALU.mult, r=[dstk], w=[tk])
                    B.tt(y, y, t, ALU.add, r=[dstk, tk], w=[dstk])
                    B.ts(t, y, -PI, 2 * PI, ALU.is_lt, ALU.mult, r=[dstk], w=[tk])
                    B.tt(y, y, t, ALU.add, r=[dstk, tk], w=[dstk])
                B.ts(y, y, PI, -PI, ALU.min, ALU.max, r=[dstk], w=[dstk])
                B.act(y, y, AF.Sin, r=[dstk], w=[dstk])

        sin_layer(w1s, 33, fT, h1, 3, "fT", "h1", "hyw1")
        sin_layer(w2s, 64, h1, h2, 4, "h1", "h2", "hyw2")
        for tt_ in range(16):
            d_ = dct[tt_ % 2]
            dk_ = f"dct{tt_ % 2}"
            B.dma("sp", d_, hy_decay[:, tt_, :], w=[dk_])
            for g in range(4):
                pb = (tt_ * 4 + g) % 4
                B.mm(ps[pb][:, :512], h2[0:64, tt_ * 128:(tt_ + 1) * 128], w3[0:64, g * 512:(g + 1) * 512], True, True,
                     r=["h2", "w3"], w=[PSK[pb]])
                B.tt(HD[:, tt_, g, :], ps[pb][:, :512], d_, ALU.mult, r=[PSK[pb], dk_], w=["HD"])
        for g in (1, 3):
            B.memset(HD[0:1, 0, g, :], 0.0, w=["HD"])
        B.fence()
        for o in range(2):
            i = 0
            for tt_ in range(16):
                for dr in range(2):
                    t = tmp[i % 2]
                    tk = f"ftmp{i % 2}"
                    B.act(t, HD[:, tt_, o * 2 + dr, :], AF.Abs, r=["HD"], w=[tk])
                    B.mm(ps[4][:, :512], ONES, t, i == 0, i == 31, r=["cst", tk], w=[PSK[4]])
                    i += 1
            B.recip(rn, ps[4][:, :512], r=[PSK[4]], w=["rn"])
            for tt_ in range(16):
                t = tmp[tt_ % 2]
                tk = f"ftmp{tt_ % 2}"
                B.tt(t, HD[:, tt_, o * 2, :], HD[:, tt_, o * 2 + 1, :], ALU.add, r=["HD"], w=[tk])
                B.tt(AB[:, tt_, 0, :], t, rn, ALU.mult, r=[tk, "rn"], w=["AB"])
                B.tt(t, HD[:, tt_, o * 2, :], HD[:, tt_, o * 2 + 1, :], ALU.subtract, r=["HD"], w=[tk])
                B.tt(AB[:, tt_, 1, :], t, rn, ALU.mult, r=[tk, "rn"], w=["AB"])
            q = 0
            for ft in range(17):
                sb_ = spb[ft % 2]
                sk_ = f"spb{ft % 2}"
                for part in range(2):
                    sl, slk = slab[q % 3], f"fslab{q % 3}"
                    q += 1
                    B.dma("sp", sl, dftf[ft, part], w=[slk])
                    pb = 5 + (ft * 2 + part) % 3
                    for tt_ in range(16):
                        B.mm(ps[pb][:, :512], sl[:, tt_, :], AB[:, tt_, part, :], tt_ == 0, tt_ == 15,
                             r=[slk, "AB"], w=[PSK[pb]])
                    if part == 0:
                        B.act(sb_[:, part, :], ps[pb][:, :512], AF.Identity, r=[PSK[pb]], w=[sk_])
                    else:
                        B.cp(sb_[:, part, :], ps[pb][:, :512], r=[PSK[pb]], w=[sk_])
                B.dma("sp", spec[o, ft], sb_, r=[sk_], w=[f"spec{o}"])
        B.fence()

    if 1 in layers and DO_HY:
        stage_filter()

    def modv(l, m, k, col):
        return mod[:, l, m * 8 + k, col:col + 1]

    def norm_mod(src_fn, n, dst_fn, l, ni, col, C, srck, dstk):
        sq = [C.get([512], F32) for _ in range(2)]
        rs = C.get([512], F32)
        tmp = [C.get([512], F32) for _ in range(2)]
        pb = 7
        kk = P.k("nm")
        for k in range(8):
            s = sq[k % 2]
            B.tt(s[:, :n], src_fn(k), src_fn(k), ALU.mult, r=srck, w=[kk + f"sq{k % 2}"])
            B.mm(ps[pb][:, :n], ONES, s[:, :n], k == 0, k == 7, r=[kk + f"sq{k % 2}", "cst"], w=[PSK[pb]])
        B.act(rs[:, :n], ps[pb][:, :n], AF.Sqrt, r=[PSK[pb], "small"], w=[kk + "rs"], bias=epsc, scale=1.0 / 1024)
        B.recip(rs[:, :n], rs[:, :n], r=[kk + "rs"], w=[kk + "rs"])
        mshift = 0 if ni == 0 else 3
        for k in range(8):
            t = tmp[k % 2]
            B.tt(t[:, :n], src_fn(k), rs[:, :n], ALU.mult, r=list(srck) + [kk + "rs"], w=[kk + f"t{k % 2}"])
            B.act(dst_fn(k), t[:, :n], AF.Identity, r=[kk + f"t{k % 2}", "mA", "mod"], w=dstk,
                  bias=modv(l, mshift, k, col), scale=mA[:, l, ni, k, col:col + 1])

    def ffn(l, b, X, lat_only):
        if lat_only:
            scs = [[(256, 1024, False, True)], [(1280, 1024, True, False)]]
        else:
            scs = [[(0, 256, False, False), (256, 896, False, True)], [(1152, 1152, True, False)]]
        wup = ffn_w_up[l].rearrange("(kt p) f -> p kt f", p=128)
        wdn = ffn_w_down[l].rearrange("(ft p) d -> p ft d", p=128)
        for si, segs in enumerate(scs):
            B.fence()
            C = Carver(P, 73728)
            t0 = segs[0][0] - (1 if segs[0][2] else 0)
            t1 = segs[-1][0] + segs[-1][1] + (1 if segs[-1][3] else 0)
            next_ = t1 - t0
            h2 = C.get([8, next_], BF16)
            ntok = sum(s[1] for s in segs)
            G = C.get([22, ntok], BF16)
            ubw = sum(s[1] + 2 for s in segs)
            ub = [C.get([2, ubw], BF16) for _ in range(2)]
            wsl = [C.get([8, 256], BF16) for _ in range(3)]
            dg = [C.get([6, 128], BF16) for _ in range(2)]
            wds = [C.get([22, 128], BF16) for _ in range(2)]
            sg = [C.get([512], F32) for _ in range(2)]
            h2k = P.k("h2")
            pos = t0
            if segs[0][2]:
                B.cp(h2[:, :, 0], h2st[:], r=["h2st"], w=[h2k])
                pos = t0 + 1
            while pos < t1:
                n = min(512, t1 - pos)
                if pos < 256 < pos + n:
                    n = 256 - pos
                col = 2 if pos < 256 else b
                Cn = Carver(P, C.off)
                norm_mod(lambda k, pos=pos, n=n: X[:, k, pos:pos + n], n,
                         lambda k, pos=pos, n=n: h2[:, k, pos - t0:pos - t0 + n], l, 1, col, Cn,
                         srck=["X"], dstk=[h2k])
                pos += n
            if si == 0:
                nxt = scs[1][0][0] - 1
                B.cp(h2st[:], h2[:, :, nxt - t0], r=[h2k], w=["h2st"])
            Gk = P.k("G")
            for j in range(22):
                wb = wsl[j % 3]
                wk = f"wup{j % 3}"
                B.dma("pool", wb[:, :, 0:128], wup[:, :, j * 128:(j + 1) * 128], w=[wk])
                B.dma("pool", wb[:, :, 128:256], wup[:, :, 2816 + j * 128:2816 + (j + 1) * 128], w=[wk])
                d = dg[j % 2]
                dk = f"dg{j % 2}"
                for vg in range(2):
                    for kk in range(3):
                        B.ts(d[:, vg * 3 + kk, :], IDENT, fcw[:, l, vg * 22 + j, kk:kk + 1], None, ALU.mult, None,
                             r=["cst", "fcw"], w=[dk])
                u = ub[j % 2]
                uk = f"ub{j % 2}"
                B.memset(u[:], 0.0, w=[uk], eng="pool")
                ucol = 0
                seg_ucol = []
                for (s0, sn, lh, rh) in segs:
                    seg_ucol.append(ucol)
                    a0 = s0 - (1 if lh else 0)
                    a1 = s0 + sn + (1 if rh else 0)
                    pos = a0
                    while pos < a1:
                        n = min(512, a1 - pos)
                        for vg in range(2):
                            pb = (2 * (pos // 512) + vg) % 4
                            for kt in range(8):
                                B.mm(ps[pb][:, :n], wb[:, kt, vg * 128:(vg + 1) * 128],
                                     h2[:, kt, pos - t0:pos - t0 + n], kt == 0, kt == 7,
                                     r=[wk, h2k], w=[PSK[pb]])
                            c0 = ucol + 1 + (pos - s0)
                            if vg == 0:
                                B.act(u[:, vg, c0:c0 + n], ps[pb][:, :n], AF.Identity, r=[PSK[pb]], w=[uk])
                            else:
                                B.cp(u[:, vg, c0:c0 + n], ps[pb][:, :n], r=[PSK[pb]], w=[uk])
                        pos += n
                    ucol += sn + 2
                gcol = 0
                for (s0, sn, lh, rh), uc in zip(segs, seg_ucol):
                    pos = 0
                    while pos < sn:
                        n = min(512, sn - pos)
                        pv, pg = 4 + (pos // 512) % 2 * 2, 5 + (pos // 512) % 2 * 2
                        for vg, pb in ((1, pg), (0, pv)):
                            for kk in range(3):
                                B.mm(ps[pb][:, :n], d[:, vg * 3 + kk, :], u[:, vg, uc + pos + kk:uc + pos + kk + n],
                                     kk == 0, kk == 2, r=[dk, uk], w=[PSK[pb]])
                        s = sg[(pos // 512) % 2]
                        sk = f"sg{(pos // 512) % 2}"
                        B.act(s[:, :n], ps[pg][:, :n], AF.Silu, r=[PSK[pg], "fcb"], w=[sk],
                              bias=fcb[:, l, 22 + j:23 + j], scale=1.0)
                        B.stt(G[:, j, gcol + pos:gcol + pos + n], ps[pv][:, :n], fcb[:, l, j:j + 1], s[:, :n],
                              ALU.add, ALU.mult, r=[PSK[pv], sk, "fcb"], w=[Gk])
                        pos += n
                    gcol += sn
            for dm in range(8):
                wd = wds[dm % 2]
                wdk = f"wdn{dm % 2}"
                B.dma("pool", wd, wdn[:, :, dm * 128:(dm + 1) * 128], w=[wdk])
                gcol = 0
                ci = 0
                for (s0, sn, lh, rh) in segs:
                    pos = 0
                    while pos < sn:
                        n = min(512, sn - pos)
                        pb = ci % 4
                        ci += 1
                        for ft in range(22):
                            B.mm(ps[pb][:, :n], wd[:, ft, :], G[:, ft, gcol + pos:gcol + pos + n],
                                 ft == 0, ft == 21, r=[wdk, Gk], w=[PSK[pb]])
                        col = 2 if s0 < 256 else b
                        xs = X[:, dm, s0 + pos:s0 + pos + n]
                        B.stt(xs, ps[pb][:, :n], modv(l, 5, dm, col), xs, ALU.mult, ALU.add,
                              r=[PSK[pb], "mod", "X"], w=["X"])
                        pos += n
                    gcol += sn
        B.fence()

    def u0col(tok):
        return tok + 15 if tok < 256 else tok + 45

    def layer0(b):
        l = 0
        B.fence()
        C = Carver(P, 0)
        QT = C.get([6, NT], BF16)
        KD = C.get([4, NT], BF16)
        VA = C.get([18, 4, 192], BF16)
        assert C.off <= 73728
        X = P.view(0, [8, NT], F32)
        C = Carver(P, 73728)
        H = C.get([8, NT], BF16)
        U0 = C.get([4, 2368], BF16)
        CO = C.get([4, NT], BF16)
        tbase = C.off
        xs = C.get([8, 512], F32)
        RC = C.get([2048], F32)
        RS = C.get([2048], F32)
        tmp_base = C.off
        B.dma("sp", RC, rope[:, 0, :], w=["RC"])
        B.dma("sp", RS, rope[:, 1, :], w=["RS"])
        B.memset(U0[:], 0.0, w=["U0"], eng="pool")
        B.memset(VA[:], 0.0, w=["VA"], eng="pool")
        B.memset(VA[:, :, :, 0:1], 1.0, w=["VA"], eng="pool")
        B.memset(VA[:, :, :, 128:129], 1.0, w=["VA"], eng="pool")
        for ci, (c0, n) in enumerate(CH):
            B.dma("sp", xs[:, :, :n], xin[b][:, :, c0:c0 + n], w=["xs"])
            Cn = Carver(P, tmp_base)
            norm_mod(lambda k, n=n: xs[:, k, :n], n, lambda k, c0=c0, n=n: H[:, k, c0:c0 + n], l, 0,
                     2 if ci == 0 else b, Cn, srck=["xs"], dstk=["H"])
        B.fence()
        Ct = Carver(P, tmp_base)
        wsl = [Ct.get([8, 128], BF16) for _ in range(3)]
        wv = P.view(tbase, [8, 256], BF16)
        qs_t = Ct.get([512], F32)
        q2_t = Ct.get([512], F32)
        rs_t = Ct.get([512], F32)
        t1_t = Ct.get([512], F32)
        t2_t = Ct.get([512], F32)
        win = ev_w_in.rearrange("(kt p) f -> p kt f", p=128)
        slab_i = [0]

        def load_slab(col_list):
            i = slab_i[0] % 3
            slab_i[0] += 1
            wb, wk = wsl[i], f"win{i}"
            wdt = 128 // len(col_list)
            for q, c in enumerate(col_list):
                B.dma("pool", wb[:, :, q * wdt:(q + 1) * wdt], win[:, :, c:c + wdt], w=[wk])
            return wb, wk

        def proj_chunk(wb, wk, c0, n, pb):
            for kt in range(8):
                B.mm(ps[pb][:, :n], wb[:, kt, :], H[:, kt, c0:c0 + n], kt == 0, kt == 7, r=[wk, "H"], w=[PSK[pb]])

        def qk_post(pb, ci, c0, n, gcol, dst, dstk):
            B.act(qs_t[:, :n], ps[pb][:, :n], AF.Identity, r=[PSK[pb], "evs"], w=["qs"], scale=evs[:, gcol:gcol + 1])
            B.act(q2_t[:, :n], ps[pb][:, :n], AF.Square, r=[PSK[pb]], w=["q2"])
            B.mm(ps[4][:, :n], BLK1, q2_t[:, :n], True, True, r=["cst", "q2"], w=[PSK[4]])
            if ci > 0:
                B.mm(ps[5][:, :n], PSW, qs_t[:, :n], True, True, r=["cst", "qs"], w=[PSK[5]])
            B.act(rs_t[:, :n], ps[4][:, :n], AF.Sqrt, r=[PSK[4], "small"], w=["rs"], bias=epsc, scale=1.0 / 64)
            B.recip(rs_t[:, :n], rs_t[:, :n], r=["rs"], w=["rs"])
            if ci > 0:
                r0 = c0 - 256
                B.tt(t1_t[:, :n], qs_t[:, :n], RC[:, r0:r0 + n], ALU.mult, r=["qs", "RC"], w=["t1"])
                B.tt(t2_t[:, :n], ps[5][:, :n], RS[:, r0:r0 + n], ALU.mult, r=[PSK[5], "RS"], w=["t2"])
                B.tt(t1_t[:, :n], t1_t[:, :n], t2_t[:, :n], ALU.add, r=["t1", "t2"], w=["t1"])
                B.tt(dst, t1_t[:, :n], rs_t[:, :n], ALU.mult, r=["t1", "rs"], w=dstk)
            else:
                B.tt(dst, qs_t[:, :n], rs_t[:, :n], ALU.mult, r=["qs", "rs"], w=dstk)

        for j in range(6):
            wb, wk = load_slab([j * 128])
            for ci, (c0, n) in enumerate(CH):
                pb = ci % 2
                proj_chunk(wb, wk, c0, n, pb)
                qk_post(pb, ci, c0, n, 0, QT[:, j, c0:c0 + n], ["QT"])
        for g in range(4):
            wb, wk = load_slab([768 + 64 * g, 768 + 64 * g])
            for ci, (c0, n) in enumerate(CH):
                pb = ci % 2
                proj_chunk(wb, wk, c0, n, pb)
                qk_post(pb, ci, c0, n, 1, KD[:, g, c0:c0 + n], ["KD"])
        B.dma("pool", wv, win[:, :, 1024:1280], w=["wv"])
        for tt_ in range(18):
            pb = tt_ % 2
            for kt in range(8):
                B.mm(ps[pb][:, 0:256], H[:, kt, tt_ * 128:(tt_ + 1) * 128], wv[:, kt, :], kt == 0, kt == 7,
                     r=["wv", "H"], w=[PSK[pb]])
            B.cp(VA[:, tt_, :, 64:128], ps[pb][:, 0:256].rearrange("p (g d) -> p g d", d=64), r=[PSK[pb]], w=["VA"])
        for j in range(4):
            wa, wak = load_slab([1280 + j * 128])
            wg, wgk = load_slab([1792 + j * 128])
            for ci, (c0, n) in enumerate(CH):
                pa, pg = 2 * (ci % 2), 2 * (ci % 2) + 1
                proj_chunk(wg, wgk, c0, n, pg)
                proj_chunk(wa, wak, c0, n, pa)
                B.act(qs_t[:, :n], ps[pg][:, :n], AF.Sigmoid, r=[PSK[pg]], w=["qs"])
                uc = u0col(c0)
                B.tt(U0[:, j, uc:uc + n], ps[pa][:, :n], qs_t[:, :n], ALU.mult, r=[PSK[pa], "qs"], w=["U0"])
        B.fence()
        Cd = Carver(P, tbase)
        DG = Cd.get([4, 31, 128], BF16)
        Ct = Carver(P, Cd.off)
        ub = Ct.get([4, 512], F32)
        u2 = [Ct.get([512], F32) for _ in range(2)]
        mean = Ct.get([512], F32)
        var = Ct.get([512], F32)
        dd = [Ct.get([512], F32) for _ in range(2)]
        for ct in range(4):
            for kk in range(31):
                B.ts(DG[:, ct, kk, :], IDENT, evcw[:, ct, kk:kk + 1], None, ALU.mult, None, r=["cst", "evcw"], w=["DG"])
        for ci, (c0, n) in enumerate(CH):
            uc = u0col(c0) - 15
            for ct in range(4):
                pb = ct % 2
                for kk in range(31):
                    B.mm(ps[pb][:, :n], DG[:, ct, kk, :], U0[:, ct, uc + kk:uc + kk + n], kk == 0, kk == 30,
                         r=["DG", "U0"], w=[PSK[pb]])
                B.act(ub[:, ct, :n], ps[pb][:, :n], AF.Identity, r=[PSK[pb], "evs"], w=["ub"],
                      bias=evs[:, 2 + ct:3 + ct], scale=1.0)
                B.tt(u2[ct % 2][:, :n], ub[:, ct, :n], ub[:, ct, :n], ALU.mult, r=["ub"], w=[f"u2{ct % 2}"])
                B.mm(ps[2][:, :n], ONES, ub[:, ct, :n], ct == 0, ct == 3, r=["cst", "ub"], w=[PSK[2]])
                B.mm(ps[3][:, :n], ONES, u2[ct % 2][:, :n], ct == 0, ct == 3, r=["cst", f"u2{ct % 2}"], w=[PSK[3]])
            B.act(mean[:, :n], ps[2][:, :n], AF.Identity, r=[PSK[2]], w=["mean"], scale=1.0 / 512)
            B.tt(var[:, :n], mean[:, :n], mean[:, :n], ALU.mult, r=["mean"], w=["var"])
            B.stt(var[:, :n], ps[3][:, :n], 1.0 / 512, var[:, :n], ALU.mult, ALU.subtract, r=[PSK[3], "var"], w=["var"])
            B.act(var[:, :n], var[:, :n], AF.Sqrt, r=["var", "small"], w=["var"], bias=epsc, scale=1.0)
            B.recip(var[:, :n], var[:, :n], r=["var"], w=["var"])
            for ct in range(4):
                d = dd[ct % 2]
                B.tt(d[:, :n], ub[:, ct, :n], mean[:, :n], ALU.subtract, r=["ub", "mean"], w=[f"dd{ct % 2}"])
                B.tt(d[:, :n], d[:, :n], var[:, :n], ALU.mult, r=[f"dd{ct % 2}", "var"], w=[f"dd{ct % 2}"])
                B.act(CO[:, ct, c0:c0 + n], d[:, :n], AF.Silu, r=[f"dd{ct % 2}", "evs"], w=["CO"],
                      bias=evs[:, 10 + ct:11 + ct], scale=evs[:, 6 + ct:7 + ct])
        B.fence()
        Ct = Carver(P, tbase)
        PT = [Ct.get([512], BF16) for _ in range(6)]
        OS = [Ct.get([512], F32) for _ in range(2)]
        GS = 3
        ui = 0
        pending = []
        for h in range(12):
            g, par = h // 3, h % 2
            off = par * 64
            for ci, (c0, n) in enumerate(CH):
                nkt = 2 if ci == 0 else 18
                po = ui % 2
                O = OS[ui % 2]
                ok = f"OS{ui % 2}"
                ui += 1
                groups = [list(range(a_, min(a_ + GS, nkt))) for a_ in range(0, nkt, GS)]

                def s_grp(gi, n=n, c0=c0, g=g, off=off, h=h, groups=groups):
                    kts = groups[gi]
                    banks = [2 + 3 * (gi % 2) + i for i in range(len(kts))]
                    for i, kt in enumerate(kts):
                        wkeys = [PSK[bk] for bk in banks] if i == 0 else [PSK[banks[i]]]
                        B.mm(ps[banks[i]][:, :n], KD[off:off + 64, g, kt * 128:(kt + 1) * 128],
                             QT[off:off + 64, h // 2, c0:c0 + n], True, True, r=["KD", "QT"], w=wkeys)

                def e_grp(gi, n=n, groups=groups):
                    for i, kt in enumerate(groups[gi]):
                        bk = 2 + 3 * (gi % 2) + i
                        pi = 3 * (gi % 2) + i
                        B.act(PT[pi][:, :n], ps[bk][:, :n], AF.Exp, r=[PSK[bk]], w=[f"PT{pi}"], scale=0.125)

                def pv_grp(gi, n=n, g=g, par=par, po=po, nkt=nkt, groups=groups):
                    kts = groups[gi]
                    pis = [3 * (gi % 2) + i for i in range(len(kts))]
                    for i, kt in enumerate(kts):
                        rk = [f"PT{p_}" for p_ in pis] if i == 0 else [f"PT{pis[i]}"]
                        lhs = VA[:, kt, g, 64:192] if par == 0 else VA[:, kt, g, 0:128]
                        B.mm(ps[po][:, :n], lhs, PT[pis[i]][:, :n], kt == 0, kt == nkt - 1, r=["VA"] + rk,
                             w=[PSK[po]])

                def epilogue(n=n, c0=c0, par=par, po=po, O=O, ok=ok, off=off, h=h):
                    B.act(O[:, :n], ps[po][:, :n], AF.Identity, r=[PSK[po]], w=[ok])
                    dr = 64 if par == 0 else 0
                    B.recip(O[dr:dr + 1, :n], O[dr:dr + 1, :n], r=[ok], w=[ok])
                    if par == 0:
                        B.mm(ps[po][0:64, :n], ESE[0:65, 0:64], O[0:65, :n], True, True, r=["cst", ok], w=[PSK[po]])
                    else:
                        B.mm(ps[po][:, :n], ESO, O[:, :n], True, True, r=["cst", ok], w=[PSK[po]])
                    B.tt(H[off:off + 64, h // 2, c0:c0 + n], O[off:off + 64, :n], ps[po][off:off + 64, :n], ALU.mult,
                         r=[ok, PSK[po]], w=["H"])

                ng = len(groups)
                s_grp(0)
                e_grp(0)
                for gi in range(ng):
                    if gi + 1 < ng:
                        s_grp(gi + 1)
                        e_grp(gi + 1)
                    if gi == 1 or (ng == 1 and gi == 0):
                        for ep in pending:
                            ep()
                        pending = []
                    pv_grp(gi)
                pending.append(epilogue)
        for ep in pending:
            ep()
        B.fence()
        Ct = Carver(P, tbase)
        wos = [Ct.get([10, 128], BF16) for _ in range(2)]
        wo = ev_w_out.rearrange("(ft p) d -> p ft d", p=128)
        for dm in range(8):
            w_, wk = wos[dm % 2], f"wo{dm % 2}"
            B.dma("pool", w_, wo[:, :, dm * 128:(dm + 1) * 128], w=[wk])
            B.dma("sp", X[:, dm, :], xin[b][:, dm, :], w=["X"])
            for ci, (c0, n) in enumerate(CH):
                pb = ci % 4
                for ft in range(10):
                    rhs = H[:, ft, c0:c0 + n] if ft < 6 else CO[:, ft - 6, c0:c0 + n]
                    B.mm(ps[pb][:, :n], w_[:, ft, :], rhs, ft == 0, ft == 9, r=[wk, "H", "CO"], w=[PSK[pb]])
                col = 2 if ci == 0 else b
                xs_ = X[:, dm, c0:c0 + n]
                B.stt(xs_, ps[pb][:, :n], modv(l, 2, dm, col), xs_, ALU.mult, ALU.add,
                      r=[PSK[pb], "mod", "X"], w=["X"])
        B.fence()
        if "xmid0" in P.taps:
            B.dma("sp", tapd["xmid0"][b], X, r=["X"], w=[P.k("tap")])
        ffn(0, b, X, lat_only=False)
        if "x0" in P.taps:
            B.dma("sp", tapd["x0"][b], X, r=["X"], w=[P.k("tap")])
        return X


    AX = mybir.AxisListType

    def sub(pk, hs=range(4)):
        return [pk]

    def layer1(b, X, do_hyena=True):
        l = 1
        B.fence()
        B.dma("sp", xsp[b], X, r=["X"], w=["xsp"])
        H = P.view(73728, [8, NT], BF16)
        for ci, (c0, n) in enumerate(CH):
            Cn = Carver(P, 161792)
            norm_mod(lambda k, c0=c0, n=n: X[:, k, c0:c0 + n], n, lambda k, c0=c0, n=n: H[:, k, c0:c0 + n], l, 0,
                     2 if ci == 0 else b, Cn, srck=["X"], dstk=["H"])
        B.fence()
        PT = P.view(0, [12, 2050], BF16)
        QF = P.view(49200, [4, NT], BF16)
        MIX = P.view(110592, [8, 2048], BF16)
        VT = P.view(143360, [18, 512], BF16)
        win = od_w_in.rearrange("(kt p) f -> p kt f", p=128)
        Ct = Carver(P, 161792)
        wsl = [Ct.get([8, 128], BF16) for _ in range(3)]
        B.memset(PT[:, :, 0:1], 0.0, w=["PT"])
        B.memset(PT[:, :, 2049:2050], 0.0, w=["PT"])
        for j in range(16):
            wb, wk = wsl[j % 3], f"w1in{j % 3}"
            B.dma("pool", wb, win[:, :, j * 128:(j + 1) * 128], w=[wk])
            for ci, (c0, n) in enumerate(CH):
                if j < 12 and ci == 0:
                    continue
                pb = ci % 4
                for kt in range(8):
                    B.mm(ps[pb][:, :n], wb[:, kt, :], H[:, kt, c0:c0 + n], kt == 0, kt == 7, r=[wk, "H"], w=[PSK[pb]])
                if j < 12:
                    dst, dk = PT[:, j, 1 + c0 - 256:1 + c0 - 256 + n], "PT"
                else:
                    dst, dk = QF[:, j - 12, c0:c0 + n], "QF"
                if ci % 2 == 0:
                    B.act(dst, ps[pb][:, :n], AF.Identity, r=[PSK[pb]], w=[dk])
                else:
                    B.cp(dst, ps[pb][:, :n], r=[PSK[pb]], w=[dk])
        B.fence()
        Ct = Carver(P, 161792)
        Wf = Ct.get([8, 512], BF16)
        Wi = Ct.get([8, 512], BF16)
        fr = Ct.get([512], F32)
        lg = Ct.get([512], F32)
        kk_ = Ct.get([512], F32)
        e1 = Ct.get([512], F32)
        kd = Ct.get([512], F32)
        ebT = Ct.get([4, 128], F32)
        kdec = Ct.get([512], BF16)
        qdT = Ct.get([4, 128], BF16)
        kdT = Ct.get([512], BF16)
        AT = Ct.get([4, 128], BF16)
        S32 = Ct.get([4, 128], F32)
        S16 = Ct.get([4, 128], BF16)
        dec = Ct.get([4, 2], F32)
        Cx = Carver(P, 110592, 110592 + 16384)
        lbv = Cx.get([2, 512], F32)
        oml = Cx.get([2, 512], F32)
        osb = Cx.get([2, 512], F32)
        ofl = Cx.get([2, 512], F32)
        Cy = Carver(P, 49200 + 18432, 73728)
        ss = Cy.get([8], F32)
        MSK = Cy.get([2, 4, 128], F32)
        gnb = Ct.get([512], F32)
        B.dma("sp", lbv, lbraw[:, 0], w=["lbv"])
        B.dma("sp", oml, lbraw[:, 1], w=["oml"])
        B.dma("sp", gnb, gnb_d, w=["gnb"])
        B.act(lbv, lbv, AF.Exp, r=["lbv"], w=["lbv"])
        B.act(oml, oml, AF.Exp, r=["oml"], w=["oml"])
        B.tt(lbv, lbv, oml, ALU.add, r=["lbv", "oml"], w=["lbv"])
        B.recip(lbv, lbv, r=["lbv"], w=["lbv"])
        B.tt(lbv, lbv, oml, ALU.mult, r=["lbv", "oml"], w=["lbv"])
        B.ts(oml, lbv, -1.0, 1.0, ALU.mult, ALU.add, r=["lbv"], w=["oml"])
        for hd in range(4):
            B.cp(MSK[:, 0, hd, :], TRIF, r=["cst"], w=["MSK"])
            B.cp(MSK[:, 1, hd, :], TRIB, r=["cst"], w=["MSK"])

        def wload(dst, col0, key):
            B.dma("pool", dst, win[:, :, col0:col0 + 512], w=[key])

        def hg_tile(tt_, d):
            lat = tt_ >= 2
            tl = tt_ - 2
            tsl = slice(tt_ * 128, (tt_ + 1) * 128)
            for kt in range(8):
                B.mm(ps[0][:, :512], H[:, kt, tsl], Wf[:, kt, :], kt == 0, kt == 7, r=["H", "Wf"], w=[PSK[0]])
            B.act(fr, ps[0][:, :512], AF.Sigmoid, r=[PSK[0]], w=["fr"])
            B.tt(fr, fr, oml[:, d, :], ALU.mult, r=["fr", "oml"], w=["fr"])
            B.tt(fr, fr, lbv[:, d, :], ALU.add, r=["fr", "lbv"], w=["fr"])
            B.act(lg, fr, AF.Ln, r=["fr"], w=["lg"])
            B.ts(kk_, fr, -1.0, 1.0, ALU.mult, ALU.add, r=["fr"], w=["kk"])
            if d == 0:
                for kt in range(8):
                    B.mm(ps[1][:, :512], H[:, kt, tsl], Wi[:, kt, :], kt == 0, kt == 7, r=["H", "Wi"], w=sub(PSK[1]))
                B.act(VT[:, tt_, :], ps[1][:, :512], AF.Identity, r=sub(PSK[1]), w=["VT"])
            TRI, STR = (TRIF, STRF) if d == 0 else (TRIB, STRB)
            B.mm(ps[2][:, :512], TRI, lg, True, True, r=["cst", "lg"], w=sub(PSK[2]))
            B.mm(ps[3][:, :512], STR, lg, True, True, r=["cst", "lg"], w=sub(PSK[3]))
            B.act(e1, ps[2][:, :512], AF.Exp, r=sub(PSK[2]), w=["e1"], scale=-1.0)
            B.tt(kd, kk_, e1, ALU.mult, r=["kk", "e1"], w=["kd"])
            B.act(e1, ps[3][:, :512], AF.Exp, r=sub(PSK[3]), w=["e1"])
            B.tt(kdec, kk_, e1, ALU.mult, r=["kk", "e1"], w=["kdec"])
            for hd in range(4):
                hs = slice(hd * 128, (hd + 1) * 128)
                B.mm(ps[4][:, hs], lg[:, hs], TRI, True, True, r=["cst", "lg"], w=[PSK[4]])
            p4 = ps[4][:, :512].rearrange("p (h t) -> p h t", t=128)
            B.act(ebT, p4, AF.Exp, r=sub(PSK[4]), w=["ebT"])
            B.tt(qdT, QF[:, :, tsl], ebT, ALU.mult, r=["QF", "ebT"], w=["qdT"])
            cols = (63, 127) if d == 0 else (0, 64)
            for c in range(2):
                B.act(dec[:, :, c:c + 1], p4[:, :, cols[c]:cols[c] + 1], AF.Exp, r=sub(PSK[4]), w=["dec"])
            for hd in range(4):
                hs = slice(hd * 128, (hd + 1) * 128)
                B.tr(ps[5][:, hs], kd[:, hs], IDENT, r=["kd", "cst"], w=[PSK[5]])
            B.act(kdT, ps[5][:, :512], AF.Identity, r=sub(PSK[5]), w=["kdT"])
            SK = os.environ.get("HG_SKIP", "")
            if lat and "a" not in SK:
                for hd in range(4):
                    hs = slice(hd * 128, (hd + 1) * 128)
                    B.mm(ps[6][:, hs], kdT[:, hs], qdT[:, hd, :], True, True, r=["kdT", "qdT"], w=[PSK[6]])
                B.tt(AT, ps[6][:, :512].rearrange("p (h t) -> p h t", t=128), MSK[:, d], ALU.mult,
                     r=sub(PSK[6]) + ["MSK"], w=["AT"])
            corder = (0, 1) if d == 0 else (1, 0)
            for qi, c in enumerate(corder):
                cs = slice(c * 64, (c + 1) * 64)
                po = 7 if c == 0 else 1
                if lat and "o" not in SK:
                    for hd in range(4):
                        hs = slice(hd * 128, (hd + 1) * 128)
                        B.mm(ps[po][0:64, hs], AT[:, hd, cs], VT[:, tt_, hs], True, False, r=["AT", "VT"],
                             w=[PSK[po]])
                        B.mm(ps[po][0:64, hs], qdT[:, hd, cs], S16[:, hd, :], False, True, r=["qdT", f"S16:{hd}"],
                             w=[PSK[po]])
                pu = 2 + qi
                for hd in range(4):
                    hs = slice(hd * 128, (hd + 1) * 128)
                    B.mm(ps[pu][:, hs], kdec[cs, hs], VT[cs, tt_, hs], True, True, r=["kdec", "VT"],
                         w=[PSK[pu]])
                for hd in range(4):
                    hs = slice(hd * 128, (hd + 1) * 128)
                    B.stt(S32[:, hd, :], S32[:, hd, :], dec[:, hd, c:c + 1], ps[pu][:, hs], ALU.mult, ALU.add,
                          r=[f"S32:{hd}", "dec", PSK[pu]], w=[f"S32:{hd}"])
                    B.act(S16[:, hd, :], S32[:, hd, :], AF.Identity, r=[f"S32:{hd}"], w=[f"S16:{hd}"])
            if not lat or "r" in SK:
                return
            if d == 0:
                B.act(osb[0:64, 0, :], ps[7][0:64, :512], AF.Identity, r=sub(PSK[7]), w=["osb"])
                B.cp(osb[0:64, 1, :], ps[1][0:64, :512], r=sub(PSK[1]), w=["osb"])
                B.dma("sp", ofs[b, tl].rearrange("p (c f) -> p c f", c=2), osb[0:64], r=["osb"], w=[f"ofs{tl}"])
                return
            B.dma("sp", ofl[0:64], ofs[b, tl].rearrange("p (c f) -> p c f", c=2), r=[f"ofs{tl}"], w=["ofl"])
            B.tt(osb[0:64, 0, :], ofl[0:64, 0, :], ps[7][0:64, :512], ALU.add, r=["ofl"] + sub(PSK[7]), w=["osb"])
            B.tt(osb[0:64, 1, :], ofl[0:64, 1, :], ps[1][0:64, :512], ALU.add, r=["ofl"] + sub(PSK[1]), w=["osb"])
            B.tt(ofl[0:64], osb[0:64], osb[0:64], ALU.mult, r=["osb"], w=["ofl"])
            B.op("dve", lambda e: e.tensor_reduce(ss[0:64, :], ofl[0:64].rearrange("p c (h e) -> p (c h) e", e=128),
                                                  AX.X, ALU.add), r=["ofl"], w=["ss"])
            B.act(ss[0:64, :], ss[0:64, :], AF.Sqrt, r=["ss", "small"], w=["ss"], bias=epsc[0:64], scale=1.0 / 128)
            B.recip(ss[0:64, :], ss[0:64, :], r=["ss"], w=["ss"])
            for c in range(2):
                for hd in range(4):
                    hs = slice(hd * 128, (hd + 1) * 128)
                    B.stt(osb[0:64, c, hs], osb[0:64, c, hs], ss[0:64, c * 4 + hd:c * 4 + hd + 1], gnb[0:64, hs],
                          ALU.mult, ALU.mult, r=["osb", "ss", "gnb"], w=["osb"])
            for c in range(2):
                pg = 4 + c
                for kt in range(8):
                    B.mm(ps[pg][0:64, :512], H[:, kt, tt_ * 128 + c * 64:tt_ * 128 + (c + 1) * 64], Wi[:, kt, :],
                         kt == 0, kt == 7, r=["H", "Wi"], w=sub(PSK[pg]))
                B.act(ofl[0:64, c, :], ps[pg][0:64, :512], AF.Silu, r=sub(PSK[pg]), w=["ofl"])
            B.tt(osb[0:64], osb[0:64], ofl[0:64], ALU.mult, r=["osb", "ofl"], w=["osb"])
            for c in range(2):
                for hd in range(4):
                    B.tr(ps[6][:, hd * 128 + c * 64:hd * 128 + (c + 1) * 64], osb[0:64, c, hd * 128:(hd + 1) * 128],
                         cst[0:64, 0, 0:64], r=["osb", "cst"], w=[PSK[6]])
            B.act(MIX[:, 4:8, tl * 128:(tl + 1) * 128], ps[6][:, :512].rearrange("p (h t) -> p h t", t=128),
                  AF.Identity, r=sub(PSK[6]), w=["MIX"])

        for d in range(int(os.environ.get("HG_ND", "2"))):
            wload(Wf, 1536 + 512 + d * 512, "Wf")
            wload(Wi, 1536 + (1536 if d == 0 else 2048), "Wi")
            for hd in range(4):
                B.memset(S32[:, hd, :], 0.0, w=[f"S32:{hd}"])
                B.memset(S16[:, hd, :], 0.0, w=[f"S16:{hd}"])
            order = list(range(18)) if d == 0 else [1, 0] + list(range(17, 1, -1))
            order = order[:int(os.environ.get("HG_NT", "18"))]
            for tt_ in order:
                hg_tile(tt_, d)
        B.fence()
        if "dlat" in P.taps:
            B.dma("pool", tapd["dlat"][b], MIX[:, 4:8, :], r=["MIX"], w=[P.k("tap")])

        if DO_HY:
            VX = P.view(143360, [3, 16, 512], BF16)
            Z2 = P.view(73728, [16, 512], BF16)
            Ct = Carver(P, 73728 + 16384, 110592)
            dgs = [Ct.get([3, 128], BF16) for _ in range(2)]
            uT = [Ct.get([512], F32) for _ in range(2)]
            hsw = Ct.get([12, 3], F32)
            hsb = Ct.get([12], F32)
            skb = Ct.get([2, 512], F32)
            B.dma("sp", hsw, hy_sw, w=["hsw"])
            B.dma("sp", hsb, hy_sb, w=["hsb"])
            B.dma("sp", skb, hy_skip, w=["skb"])
            for ct in range(12):
                dg_, dgk = dgs[ct % 2], f"hdg{ct % 2}"
                for kk in range(3):
                    B.ts(dg_[:, kk, :], IDENT, hsw[:, ct, kk:kk + 1], None, ALU.mult, None, r=["cst", "hsw"], w=[dgk])
                for c in range(4):
                    pb = c % 2
                    u_, uk = uT[c % 2], f"uT{c % 2}"
                    for kk in range(3):
                        B.mm(ps[pb][:, :512], dg_[:, kk, :], PT[:, ct, c * 512 + kk:c * 512 + kk + 512], kk == 0, kk == 2,
                             r=[dgk, "PT"], w=[PSK[pb]])
                    B.act(u_, ps[pb][:, :512], AF.Identity, r=[PSK[pb], "hsb"], w=[uk], bias=hsb[:, ct:ct + 1], scale=1.0)
                    pt_ = 2 + c % 2
                    for q in range(4):
                        B.tr(ps[pt_][:, q * 128:(q + 1) * 128], u_[:, q * 128:(q + 1) * 128], IDENT, r=[uk, "cst"],
                             w=[PSK[pt_]])
                    dst = VX[:, ct // 4, c * 4:(c + 1) * 4, (ct % 4) * 128:(ct % 4 + 1) * 128]
                    B.cp(dst, ps[pt_][:, :512].rearrange("p (q c) -> p q c", c=128), r=[PSK[pt_]], w=["VX"])
            B.fence()
            Ct = Carver(P, 0, 73728)
            Y = Ct.get([17, 2, 512], BF16)
            fsl = [Ct.get([16, 128], BF16) for _ in range(3)]
            isl = [Ct.get([17, 2, 128], BF16) for _ in range(2)]
            Ct2 = Carver(P, 73728 + 16384, 110592)
            skb = Ct2.get([2, 512], F32)
            spt = [Ct2.get([2, 512], F32) for _ in range(2)]
            t1s = [Ct2.get([512], F32) for _ in range(2)]
            B.dma("sp", skb, hy_skip, w=["skb2"])
            for o in range(2):
                zin = VX[:, 0] if o == 0 else Z2
                zk = "VX" if o == 0 else "Z2"
                q = 0
                for ft in range(17):
                    sp_, spk = spt[ft % 2], f"spt{ft % 2}"
                    B.dma("sp", sp_, spec[o, ft], r=[f"spec{o}"], w=[spk])
                    pz = [4 + 2 * (ft % 2), 5 + 2 * (ft % 2)]
                    for part in range(2):
                        sl, slk = fsl[q % 3], f"hfsl{q % 3}"
                        q += 1
                        B.dma("sp", sl, dftf[ft, part], w=[slk])
                        for tt_ in range(16):
                            B.mm(ps[pz[part]][:, :512], sl[:, tt_, :], zin[:, tt_, :], tt_ == 0, tt_ == 15, r=[slk, zk],
                                 w=[PSK[pz[part]]])
                    a, bq = t1s
                    B.tt(a, ps[pz[0]][:, :512], sp_[:, 0, :], ALU.mult, r=[PSK[pz[0]], spk], w=["t1a"])
                    B.tt(bq, ps[pz[1]][:, :512], sp_[:, 1, :], ALU.mult, r=[PSK[pz[1]], spk], w=["t1b"])
                    B.tt(Y[:, ft, 0, :], a, bq, ALU.subtract, r=["t1a", "t1b"], w=["Y"])
                    B.tt(a, ps[pz[0]][:, :512], sp_[:, 1, :], ALU.mult, r=[PSK[pz[0]], spk], w=["t1a"])
                    B.tt(bq, ps[pz[1]][:, :512], sp_[:, 0, :], ALU.mult, r=[PSK[pz[1]], spk], w=["t1b"])
                    B.tt(Y[:, ft, 1, :], a, bq, ALU.add, r=["t1a", "t1b"], w=["Y"])
                for tt_ in range(16):
                    il, ilk = isl[tt_ % 2], f"hisl{tt_ % 2}"
                    B.dma("sp", il, dfti[tt_], w=[ilk])
                    pb = tt_ % 2
                    i = 0
                    for ft in range(17):
                        for part in range(2):
                            B.mm(ps[pb][:, :512], il[:, ft, part, :], Y[:, ft, part, :], i == 0, i == 33, r=[ilk, "Y"],
                                 w=[PSK[pb]])
                            i += 1
                    a = t1s[tt_ % 2]
                    ak = f"t1{'ab'[tt_ % 2]}"
                    B.tt(a, zin[:, tt_, :], skb[:, o, :], ALU.mult, r=[zk, "skb2"], w=[ak])
                    B.tt(a, a, ps[pb][:, :512], ALU.add, r=[ak, PSK[pb]], w=[ak])
                    if o == 0:
                        B.tt(Z2[:, tt_, :], a, VX[:, 1, tt_, :], ALU.mult, r=[ak, "VX"], w=["Z2"])
                    else:
                        B.tt(a, a, VX[:, 2, tt_, :], ALU.mult, r=[ak, "VX"], w=[ak])
                        pt_ = 2 + tt_ % 2
                        for q4 in range(4):
                            B.tr(ps[pt_][:, q4 * 128:(q4 + 1) * 128], a[:, q4 * 128:(q4 + 1) * 128], IDENT, r=[ak, "cst"],
                                 w=[PSK[pt_]])
                        B.act(MIX[:, 0:4, tt_ * 128:(tt_ + 1) * 128],
                              ps[pt_][:, :512].rearrange("p (h t) -> p h t", t=128), AF.Identity, r=[PSK[pt_]], w=["MIX"])
            B.fence()
            if "clat" in P.taps:
                B.dma("pool", tapd["clat"][b], MIX[:, 0:4, :], r=["MIX"], w=[P.k("tap")])
        else:
            B.memset(MIX[:, 0:4, :], 0.0, w=["MIX"])
            B.fence()
        wos = [P.view(143360 + i * 2048, [8, 128], BF16) for i in range(2)]
        wo = od_w_out.rearrange("(ft p) d -> p ft d", p=128)
        for dm in range(8):
            w_, wk = wos[dm % 2], f"wo1{dm % 2}"
            B.dma("pool", w_, wo[:, :, dm * 128:(dm + 1) * 128], w=[wk])
            B.dma("sp", X[:, dm, 256:NT], xsp[b][:, dm, 256:NT], r=["xsp"], w=["X"])
            for c in range(4):
                pb = c % 4
                for ft in range(8):
                    B.mm(ps[pb][:, :512], w_[:, ft, :], MIX[:, ft, c * 512:(c + 1) * 512], ft == 0, ft == 7,
                         r=[wk, "MIX"], w=[PSK[pb]])
                xs_ = X[:, dm, 256 + c * 512:256 + (c + 1) * 512]
                B.stt(xs_, ps[pb][:, :512], modv(l, 2, dm, b), xs_, ALU.mult, ALU.add, r=[PSK[pb], "mod", "X"], w=["X"])
        B.fence()
        if "xmid1" in P.taps:
            B.dma("sp", tapd["xmid1"][b], X, r=["X"], w=[P.k("tap")])
        ffn(1, b, X, lat_only=True)
        return X

    def final(b, X):
        B.fence()
        C = Carver(P, 73728)
        ob = [C.get([8, 512], F32) for _ in range(2)]
        sq = [C.get([512], F32) for _ in range(2)]
        rs = C.get([512], F32)
        for ci in range(4):
            c0 = 256 + ci * 512
            n = 512
            o_, okk = ob[ci % 2], f"ob{ci % 2}"
            for k in range(8):
                s = sq[k % 2]
                B.tt(s[:, :n], X[:, k, c0:c0 + n], X[:, k, c0:c0 + n], ALU.mult, r=["X"], w=[f"fsq{k % 2}"])
                B.mm(ps[7][:, :n], ONES, s[:, :n], k == 0, k == 7, r=[f"fsq{k % 2}", "cst"], w=[PSK[7]])
            B.act(rs[:, :n], ps[7][:, :n], AF.Sqrt, r=[PSK[7], "small"], w=["frs"], bias=epsc, scale=1.0 / 1024)
            B.recip(rs[:, :n], rs[:, :n], r=["frs"], w=["frs"])
            for k in range(8):
                B.stt(o_[:, k, :], X[:, k, c0:c0 + n], gv[:, 4, k:k + 1], rs[:, :n], ALU.mult, ALU.mult,
                      r=["X", "gv", "frs"], w=[okk])
            B.dma("sp", out[b][:, :, ci * 512:(ci + 1) * 512], o_, r=[okk], w=[P.k("out")])

    import os
    for b in range(2):
        if 0 in layers:
            X = layer0(b)
        else:
            X = P.view(0, [8, NT], F32)
            B.dma("sp", X, xin[b], w=["X"])
        if 1 in layers:
            layer1(b, X)
        final(b, X)
    nc = B.build()
    return nc, P


def host_consts():
    c = np.zeros((128, 10, 128), np.float32)
    ii = np.arange(128)
    same = (ii[:, None] // 64) == (ii[None, :] // 64)
    c[:, 6, :] = same & (ii[:, None] <= ii[None, :])
    c[:, 7, :] = same & (ii[:, None] >= ii[None, :])
    c[:, 8, :] = same & (ii[:, None] > ii[None, :])
    c[:, 9, :] = same & (ii[:, None] < ii[None, :])
    c[:, 0, :] = np.eye(128)
    c[:, 1, :] = 1.0
    c[0:64, 2, 0:64] = 1.0
    c[64:128, 2, 64:128] = 1.0
    for i in range(64):
        c[2 * i + 1, 3, 2 * i] = -1.0
        c[2 * i, 3, 2 * i + 1] = 1.0
    c[64, 4, 0:64] = 1.0
    c[0, 5, 64:128] = 1.0
    return c


def host_rope():
    L, GW, dh = 2048, 64, 64
    row = np.repeat(np.arange(L // GW, dtype=np.float32), GW)
    col = np.tile(np.arange(GW, dtype=np.float32), L // GW)
    axis_dim = dh // 2
    inv = (10000.0 ** (-np.arange(0, axis_dim, 2, dtype=np.float32) / axis_dim)).astype(np.float32)
    ang = np.concatenate([row[:, None] * inv, col[:, None] * inv], axis=-1)
    cos, sin = np.cos(ang).astype(np.float32), np.sin(ang).astype(np.float32)
    r = np.zeros((128, 2, L), np.float32)
    for p in range(128):
        i = (p % 64) // 2
        r[p, 0] = cos[:, i]
        r[p, 1] = sin[:, i]
    return r

_HC = {}


def host_hyena_consts():
    if _HC:
        return _HC
    L, N = 2048, 4096
    f32 = np.float32
    t = np.linspace(0.0, 1.0, L, dtype=f32)[:, None]
    bands = 16
    fb = np.linspace(1e-4, bands - 1, bands, dtype=f32)
    wpos = (f32(2.0 * math.pi / L) * np.arange(L, dtype=f32))[:, None]
    feats = np.concatenate([t, np.cos(fb * wpos), -np.sin(fb * wpos)], axis=-1).astype(f32)
    _HC["featsT"] = np.ascontiguousarray(feats.T)
    deltas = np.abs(np.linspace(math.log(1e-2) / 1.5, math.log(1e-2) / 0.3, 512, dtype=f32))
    dec = np.exp(-t * deltas).astype(f32)
    _HC["hy_decay"] = np.ascontiguousarray(np.transpose(dec.reshape(16, 128, 512), (1, 0, 2)))
    tt = np.arange(L, dtype=np.int64)[:, None]
    ff = np.arange(17 * 128, dtype=np.int64)[None, :]
    ang = 2.0 * np.pi * ((tt * ff) % N).astype(np.float64) / N
    valid = (ff <= L).astype(np.float64)
    Mc = np.cos(ang) * valid
    Ms = -np.sin(ang) * valid
    fw = np.stack([Mc, Ms], 0).reshape(2, 16, 128, 17, 128)
    _HC["dftf"] = np.ascontiguousarray(np.transpose(fw, (3, 0, 2, 1, 4))).astype(ml_dtypes.bfloat16)
    wf = np.where((ff == 0) | (ff == L), 1.0, 2.0) * valid / N
    Ic = (np.cos(ang) * wf).T
    Is = (-np.sin(ang) * wf).T
    iw = np.stack([Ic, Is], 0).reshape(2, 17, 128, 16, 128)
    _HC["dfti"] = np.ascontiguousarray(np.transpose(iw, (3, 2, 1, 0, 4))).astype(ml_dtypes.bfloat16)
    return _HC


def fm(v):
    return np.ascontiguousarray(v.reshape(-1, 128).T)


def prep_inputs(inp):
    f = lambda a: np.ascontiguousarray(np.asarray(a, dtype=np.float32))
    x, ctx = f(inp["x"]), f(inp["ctx"])
    shared = {
        "ada_w": f(inp["ada_w"]),
        "abT": np.ascontiguousarray(np.stack([fm(inp["ada_b"][l]) for l in range(2)], axis=1)),
        "gvec": np.ascontiguousarray(np.stack([fm(inp["norm1_g"][0]), fm(inp["norm1_g"][1]), fm(inp["norm2_g"][0]),
                                               fm(inp["norm2_g"][1]), fm(inp["final_g"])], axis=1)),
        "consts": host_consts(),
        "rope": host_rope(),
        "ev_w_in": f(inp["ev_w_in"][0]),
        "ev_cw": np.ascontiguousarray(np.transpose(f(inp["ev_conv_w"][0]).reshape(31, 4, 128), (2, 1, 0))),
        "ev_w_out": f(inp["ev_w_out"][0]),
        "ffn_w_up": f(inp["ffn_w_up"]),
        "ffn_cw": np.ascontiguousarray(np.transpose(f(inp["ffn_conv_w"]).reshape(2, 3, 44, 128), (3, 0, 2, 1))),
        "ffn_cb": np.ascontiguousarray(np.transpose(f(inp["ffn_conv_b"]).reshape(2, 44, 128), (2, 0, 1))),
        "ffn_w_down": f(inp["ffn_w_down"]),
        "od_w_in": f(inp["od_w_in"][0]),
        "od_w_out": f(inp["od_w_out"][0]),
        "lbraw": np.ascontiguousarray(np.broadcast_to(f(inp["od_lower_bound"])[None], (128, 2, 2, 512))),
        "gnb": np.ascontiguousarray(np.broadcast_to(np.tile(f(inp["od_gnorm_g"][0]), 4)[None], (128, 512))),
        "hy_w1": f(inp["od_filt_w1"][0]),
        "hy_w2": f(inp["od_filt_w2"][0]),
        "hy_w3": f(inp["od_filt_w3"][0]),
        "hy_sw": np.ascontiguousarray(np.transpose(f(inp["od_short_w"][0]).reshape(3, 12, 128), (2, 1, 0))),
        "hy_sb": fm(f(inp["od_short_b"][0])),
        "hy_skip": np.ascontiguousarray(np.broadcast_to(f(inp["od_hyena_skip"][0])[None], (128, 2, 512))),
    }
    hs = np.zeros((128, 8), np.float32)
    hs[0:64, 0] = f(inp["od_filt_b1"][0])
    hs[0:64, 1] = f(inp["od_filt_freq"][0])
    hs[0:64, 2] = f(inp["od_filt_b2"][0])
    shared["hy_small"] = hs
    shared.update(host_hyena_consts())
    evs = np.zeros((128, 16), np.float32)
    evs[:, 0] = np.tile(f(inp["ev_q_gain"][0]), 2)
    evs[:, 1] = np.tile(f(inp["ev_k_gain"][0]), 2)
    evs[:, 2:6] = fm(f(inp["ev_conv_b"][0]))
    evs[:, 6:10] = fm(f(inp["ev_ln_g"][0]))
    evs[:, 10:14] = fm(f(inp["ev_ln_b"][0]))
    shared["ev_small"] = evs
    maps = []
    for core in range(8):
        m = dict(shared)
        xi = np.zeros((2, 128, 8, NT), np.float32)
        cv = np.zeros((128, 8, 3), np.float32)
        for q in range(2):
            bidx = 2 * core + q
            full = np.concatenate([ctx[bidx], x[bidx]], axis=0)
            xi[q] = np.transpose(full.reshape(NT, 8, 128), (2, 1, 0))
            cv[:, :, q] = fm(f(inp["c"][bidx]))
        cv[:, :, 2] = fm(f(inp["c_ctx"]))
        m["xin"] = xi
        m["cvec"] = cv
        maps.append(m)
    return maps


_CACHE = {}


def kernel(**inputs):
    maps = prep_inputs(inputs)
    if "nc" not in _CACHE:
        _CACHE["nc"] = build_program()[0]
    nc = _CACHE["nc"]
    res = run_bass_kernel_spmd(nc, maps, core_ids=list(range(8)))
    outp = np.zeros((16, 2048, 1024), np.float32)
    for core in range(8):
        o = res.results[core]["out"]
        for q in range(2):
            outp[2 * core + q] = np.transpose(o[q], (2, 1, 0)).reshape(2048, 1024)
    return outp
```

```python
from contextlib import ExitStack
import math
import numpy as np
import ml_dtypes
import concourse.bass as bass
import concourse.mybir as mybir
from concourse.bass_utils import run_bass_kernel_spmd

F32 = mybir.dt.float32
BF16 = mybir.dt.bfloat16
ALU = mybir.AluOpType
AF = mybir.ActivationFunctionType

N_DMA_SEMS = 40
NT = 2304
CH = [(0, 256), (256, 512), (768, 512), (1280, 512), (1792, 512)]
EPS = 1e-6


class Builder:
    def __init__(self):
        self.nc = bass.Bass("TRN2", target_bir_lowering=False)
        self.ops = []
        self.es = ExitStack()
        self._n = 0

    def dram(self, name, shape, dtype, kind):
        return self.nc.dram_tensor(name, list(shape), dtype, kind=kind).ap()

    def sbuf(self, shape, dtype, name=None):
        self._n += 1
        return self.es.enter_context(self.nc.sbuf_tensor(name or f"sb{self._n}", list(shape), dtype))

    def psum(self, shape, dtype, name=None):
        self._n += 1
        return self.es.enter_context(self.nc.psum_tensor(name or f"ps{self._n}", list(shape), dtype))

    def op(self, stream, fn, r=(), w=(), acc=False):
        self.ops.append((stream, "c", fn, tuple(r), tuple(w), acc))

    def dma(self, queue, out, in_, r=(), w=()):
        self.ops.append((queue, "d", (out, in_), tuple(r), tuple(w), False))

    def fence(self):
        self.ops.append((None, "f", None, (), (), False))

    def mm(self, out, lhsT, rhs, start, stop, r, w):
        self.op("pe", lambda e: e.matmul(out, lhsT, rhs, start=start, stop=stop), r=r, w=w, acc=not start)

    def tr(self, out, in_, ident, r, w):
        self.op("pe", lambda e: e.transpose(out, in_, ident), r=r, w=w)

    def act(self, out, in_, func, r, w, bias=None, scale=None):
        kw = {}
        if bias is not None:
            kw["bias"] = bias
        if scale is not None:
            kw["scale"] = scale
        self.op("act", lambda e: e.activation(out, in_, func, **kw), r=r, w=w)

    def tt(self, out, a, b, op, r, w, eng="dve"):
        self.op(eng, lambda e: e.tensor_tensor(out, a, b, op), r=r, w=w)

    def ts(self, out, a, s1, s2, op0, op1, r, w, eng="dve"):
        if s2 is None:
            self.op(eng, lambda e: e.tensor_scalar(out, a, s1, None, op0), r=r, w=w)
        else:
            self.op(eng, lambda e: e.tensor_scalar(out, a, s1, s2, op0, op1), r=r, w=w)

    def stt(self, out, a, s, b, op0, op1, r, w):
        self.op("dve", lambda e: e.scalar_tensor_tensor(out, a, s, b, op0, op1), r=r, w=w)

    def cp(self, out, in_, r, w, eng="dve"):
        self.op(eng, lambda e: e.tensor_copy(out, in_), r=r, w=w)

    def recip(self, out, in_, r, w):
        self.op("dve", lambda e: e.reciprocal(out, in_), r=r, w=w)

    def memset(self, ap, v, w, eng="dve"):
        self.op(eng, lambda e: e.memset(ap, v), w=w)

    def build(self):
        nc = self.nc
        ops = self.ops
        n = len(ops)
        streams = ["pe", "act", "dve", "pool", "sp"]
        last_w = {}
        readers = {}
        deps = [set() for _ in range(n)]
        last_on = {s: None for s in streams}
        open_dmas = []
        pending_fence = {s: None for s in streams}
        for i, (st, kind, fn, R, W, acc) in enumerate(ops):
            if kind == "f":
                fd = set(j for j in last_on.values() if j is not None) | set(open_dmas)
                for s in streams:
                    pending_fence[s] = (pending_fence[s] or set()) | fd
                open_dmas = []
                continue
            if pending_fence[st]:
                deps[i] |= pending_fence[st]
                pending_fence[st] = None
            for k in R:
                if k in last_w:
                    deps[i].add(last_w[k])
            for k in W:
                if k in last_w:
                    j = last_w[k]
                    if not ((acc or st == "pe") and ops[j][0] == st and ops[j][1] == "c" and kind == "c"):
                        deps[i].add(j)
                for rr in readers.get(k, ()):
                    deps[i].add(rr)
            for k in R:
                readers.setdefault(k, []).append(i)
            for k in W:
                last_w[k] = i
                readers[k] = []
            deps[i].discard(i)
            if kind == "c":
                last_on[st] = i
            else:
                open_dmas.append(i)
        dma_sem_of, dma_val_of = {}, {}
        sem_count = [0] * N_DMA_SEMS
        sem_last = [None] * N_DMA_SEMS
        qn = {"sp": 0, "pool": 0}
        NSP = 24
        for i in range(n):
            if ops[i][1] != "d":
                continue
            if ops[i][0] == "sp":
                s = qn["sp"] % NSP
            else:
                s = NSP + qn["pool"] % (N_DMA_SEMS - NSP)
            qn[ops[i][0]] += 1
            if sem_last[s] is not None:
                deps[i].add(sem_last[s])
            sem_last[s] = i
            sem_count[s] += 16
            dma_sem_of[i] = s
            dma_val_of[i] = sem_count[s]
        needed = [False] * n
        for i in range(n):
            for j in deps[i]:
                needed[j] = True
        cnt = {s: 0 for s in streams}
        sigval = {}
        for i, (st, kind, *_r) in enumerate(ops):
            if kind == "c" and needed[i]:
                cnt[st] += 1
                sigval[i] = cnt[st]
        es = self.es
        esem = {s: es.enter_context(nc.semaphore(f"sem_{s}")) for s in streams}
        dsem = [es.enter_context(nc.semaphore(f"dsem{k}")) for k in range(N_DMA_SEMS)]
        block = es.enter_context(nc.Block())

        def target(j):
            if ops[j][1] == "d":
                return ("d", dma_sem_of[j]), dsem[dma_sem_of[j]], dma_val_of[j]
            return ("c", ops[j][0]), esem[ops[j][0]], sigval[j]

        def emit_stream(st, eng):
            waited = {}
            for i, (s2, kind, fn, R, W, acc) in enumerate(ops):
                if s2 != st:
                    continue
                need = {}
                for j in deps[i]:
                    key, sem, val = target(j)
                    if waited.get(key, 0) >= val:
                        continue
                    if need.get(key, (None, 0))[1] < val:
                        need[key] = (sem, val)
                for key, (sem, val) in need.items():
                    eng.wait_ge(sem, val)
                    waited[key] = val
                if kind == "c":
                    ins = fn(eng)
                    if needed[i]:
                        ins.then_inc(esem[st], 1)
                else:
                    out, in_ = fn
                    eng.dma_start(out=out, in_=in_).then_inc(dsem[dma_sem_of[i]], 16)
            if st == "sp":
                for s in range(N_DMA_SEMS):
                    if sem_count[s] > 0 and waited.get(("d", s), 0) < sem_count[s]:
                        eng.wait_ge(dsem[s], sem_count[s])
                for s3 in streams:
                    if cnt[s3] > 0 and waited.get(("c", s3), 0) < cnt[s3]:
                        eng.wait_ge(esem[s3], cnt[s3])

        @block.tensor
        def _(e):
            emit_stream("pe", e)

        @block.scalar
        def _(e):
            emit_stream("act", e)

        @block.vector
        def _(e):
            emit_stream("dve", e)

        @block.gpsimd
        def _(e):
            emit_stream("pool", e)

        @block.sync
        def _(e):
            emit_stream("sp", e)

        es.close()
        return nc


ARENA_BYTES = 200704


class Prog:
    def __init__(self, taps=()):
        self.B = Builder()
        self.taps = set(taps)
        self.uid = 0
        B = self.B
        self.arena = B.sbuf([128, ARENA_BYTES // 4], F32, name="arena")
        self.ps = [B.psum([128, 512], F32, name=f"psb{i}") for i in range(8)]
        self.outs = {}

    def k(self, s):
        self.uid += 1
        return f"{s}#{self.uid}"

    def view(self, off, free_shape, dtype):
        esz = 4 if dtype == F32 else 2
        nel = int(np.prod(free_shape))
        nb = nel * esz
        assert off % 4 == 0 and nb % 4 == 0 and off + nb <= ARENA_BYTES, (off, nb)
        a = self.arena[:, off // 4:(off + nb) // 4]
        if dtype != F32:
            a = a.bitcast(dtype)
        if len(free_shape) == 2:
            a = a.rearrange("p (a b) -> p a b", b=free_shape[1])
        elif len(free_shape) == 3:
            a = a.rearrange("p (a b c) -> p a b c", b=free_shape[1], c=free_shape[2])
        return a


class Carver:
    def __init__(self, P, base, limit=ARENA_BYTES):
        self.P, self.off, self.limit = P, base, limit

    def get(self, free_shape, dtype):
        esz = 4 if dtype == F32 else 2
        nb = (int(np.prod(free_shape)) * esz + 3) // 4 * 4
        v = self.P.view(self.off, free_shape, dtype)
        self.off += nb
        assert self.off <= self.limit, (self.off, self.limit)
        return v


def build_program(taps=(), layers=(0, 1)):
    import os
    DO_HY = os.environ.get('NO_HY', '') == ''
    P = Prog(taps)
    B = P.B
    ps = P.ps
    PSK = [f"psb{i}" for i in range(8)]

    def din(name, shape, dt=F32):
        return B.dram(name, shape, dt, "ExternalInput")

    xin = din("xin", [2, 128, 8, NT])
    cvec = din("cvec", [128, 8, 3])
    ada_w = din("ada_w", [2, 1024, 6144])
    abT = din("abT", [128, 2, 48])
    gvec = din("gvec", [128, 5, 8])
    consts = din("consts", [128, 10, 128])
    rope = din("rope", [128, 2, 2048])
    ev_w_in = din("ev_w_in", [1024, 2304])
    ev_small = din("ev_small", [128, 16])
    ev_cw = din("ev_cw", [128, 4, 31])
    ev_w_out = din("ev_w_out", [1280, 1024])
    ffn_w_up = din("ffn_w_up", [2, 1024, 5632])
    ffn_cw = din("ffn_cw", [128, 2, 44, 3])
    ffn_cb = din("ffn_cb", [128, 2, 44])
    ffn_w_down = din("ffn_w_down", [2, 2816, 1024])
    od_w_in = din("od_w_in", [1024, 4096])
    od_w_out = din("od_w_out", [1024, 1024])
    lbraw = din("lbraw", [128, 2, 2, 512])
    gnb_d = din("gnb", [128, 512])
    featsT = din("featsT", [33, 2048])
    hy_w1 = din("hy_w1", [33, 64])
    hy_w2 = din("hy_w2", [64, 64])
    hy_w3 = din("hy_w3", [64, 2048])
    hy_small = din("hy_small", [128, 8])
    hy_decay = din("hy_decay", [128, 16, 512])
    hy_sw = din("hy_sw", [128, 12, 3])
    hy_sb = din("hy_sb", [128, 12])
    hy_skip = din("hy_skip", [128, 2, 512])
    dftf = din("dftf", [17, 2, 128, 16, 128], BF16)
    dfti = din("dfti", [16, 128, 17, 2, 128], BF16)
    spec = B.dram("spec", [2, 17, 128, 2, 512], F32, "Internal")
    xsp = B.dram("xsp", [2, 128, 8, NT], F32, "Internal")
    ofs = B.dram("ofs", [2, 16, 64, 1024], F32, "Internal")
    out = B.dram("out", [2, 128, 8, 2048], F32, "ExternalOutput")
    tapd = {}
    for t in P.taps:
        if t in ("dlat", "clat"):
            tapd[t] = B.dram("tap_" + t, [2, 128, 4, 2048], F32, "ExternalOutput")
        else:
            tapd[t] = B.dram("tap_" + t, [2, 128, 8, NT], F32, "ExternalOutput")

    cst = B.sbuf([128, 10, 128], F32, name="cst")
    IDENT, ONES, BLK1, PSW, ESE, ESO, TRIF, TRIB, STRF, STRB = (cst[:, i, :] for i in range(10))
    B.dma("sp", cst[:], consts, w=["cst"])
    small = B.sbuf([128, 64], F32, name="small")
    B.memset(small[:, 0:1], EPS, w=["small"])
    epsc = small[:, 0:1]
    gv = B.sbuf([128, 5, 8], F32, name="gv")
    B.dma("sp", gv[:], gvec, w=["gv"])
    evs = B.sbuf([128, 16], F32, name="evs")
    B.dma("sp", evs[:], ev_small, w=["evs"])
    evcw = B.sbuf([128, 4, 31], F32, name="evcw")
    B.dma("sp", evcw[:], ev_cw, w=["evcw"])
    fcw = B.sbuf([128, 2, 44, 3], F32, name="fcw")
    B.dma("sp", fcw[:], ffn_cw, w=["fcw"])
    fcb = B.sbuf([128, 2, 44], F32, name="fcb")
    B.dma("sp", fcb[:], ffn_cb, w=["fcb"])
    identb = B.sbuf([128, 128], BF16, name="identb")
    B.cp(identb[:], IDENT, r=["cst"], w=["identb"])
    h2st = B.sbuf([128, 8], BF16, name="h2st")
    mod = B.sbuf([128, 2, 48, 3], F32, name="mod")
    mA = B.sbuf([128, 2, 2, 8, 3], F32, name="mA")

    def stage_mod():
        C = Carver(P, 0)
        cv = C.get([8, 3], F32)
        sc = C.get([8, 3], F32)
        ab = C.get([2, 48], F32)
        B.dma("sp", cv, cvec, w=["cv"])
        B.dma("sp", ab, abT, w=["ab"])
        B.act(sc, cv, AF.Silu, r=["cv"], w=["sc"])
        wsl = [C.get([8, 512], F32) for _ in range(2)]
        for l in range(2):
            awr = ada_w[l].rearrange("(kt p) f -> p kt f", p=128)
            for g in range(12):
                wb = wsl[g % 2]
                wk = f"adaw{g % 2}"
                B.dma("sp", wb, awr[:, :, g * 512:(g + 1) * 512], w=[wk])
                for jj in range(4):
                    j = g * 4 + jj
                    pb = j % 4
                    for kt in range(8):
                        B.mm(ps[pb][:, 0:3], wb[:, kt, jj * 128:(jj + 1) * 128], sc[:, kt, :],
                             kt == 0, kt == 7, r=[wk, "sc"], w=[PSK[pb]])
                    B.act(mod[:, l, j, :], ps[pb][:, 0:3], AF.Identity, r=[PSK[pb], "ab"], w=["mod"],
                          bias=ab[:, l, j:j + 1])
            for ni, (gi, mi) in enumerate(((l, 1), (2 + l, 4))):
                for col in range(3):
                    B.ts(mA[:, l, ni, :, col], mod[:, l, mi * 8:(mi + 1) * 8, col], 1.0, None, ALU.add, None,
                         r=["mod"], w=["mA"])
                    B.tt(mA[:, l, ni, :, col], mA[:, l, ni, :, col], gv[:, gi, :], ALU.mult,
                         r=["mA", "gv"], w=["mA"])
        B.fence()

    stage_mod()


    PI = float(np.pi)

    def stage_filter():
        B.fence()
        C = Carver(P, 0)
        HD = C.get([16, 4, 512], F32)
        AB = P.view(C.off, [16, 2, 512], BF16)
        fT = C.get([2048], F32)
        h1 = C.get([2048], F32)
        h2 = C.get([2048], F32)
        w3 = C.get([2048], F32)
        tmp = [C.get([512], F32) for _ in range(2)]
        rn = C.get([512], F32)
        slab = [C.get([16, 128], BF16) for _ in range(3)]
        spb = [C.get([2, 512], F32) for _ in range(2)]
        dct = [C.get([512], F32) for _ in range(2)]
        sm = B.sbuf([128, 8], F32, name="hysm")
        w1s = B.sbuf([128, 64], F32, name="hyw1")
        w2s = B.sbuf([128, 64], F32, name="hyw2")
        B.dma("sp", sm[:], hy_small, w=["hysm"])
        B.dma("sp", w1s[0:33, :], hy_w1, w=["hyw1"])
        B.dma("sp", w2s[0:64, :], hy_w2, w=["hyw2"])
        B.dma("sp", fT[0:33, :], featsT, w=["fT"])
        B.dma("sp", w3[0:64, :], hy_w3, w=["w3"])
        B.tt(sm[0:64, 3:4], sm[0:64, 0:1], sm[0:64, 1:2], ALU.mult, r=["hysm"], w=["hysm"])
        B.tt(sm[0:64, 4:5], sm[0:64, 2:3], sm[0:64, 1:2], ALU.mult, r=["hysm"], w=["hysm"])

        def sin_layer(wT, kin, src, dst, bcol, srck, dstk, wk):
            for c in range(4):
                cs = slice(c * 512, (c + 1) * 512)
                B.mm(ps[c % 2][0:64, :512], wT[0:kin, 0:64], src[0:kin, cs], True, True, r=[wk, srck], w=[PSK[c % 2]])
                y = dst[0:64, cs]
                B.act(y, ps[c % 2][0:64, :512], AF.Identity, r=[PSK[c % 2], "hysm"], w=[dstk],
                      bias=sm[0:64, bcol:bcol + 1], scale=sm[0:64, 1:2])
                t = tmp[c % 2][0:64, :]
                tk = f"ftmp{c % 2}"
                for _ in range(2):
                    B.ts(t, y, PI, -2 * PI, ALU.is_gt, ALU.mult, r=[dstk], w=[tk])
                    B.tt(y, y, t, ALU.add, r=[dstk, tk], w=[dstk])
                    B.ts(t, y, -PI, 2 * PI, ALU.is_lt, ALU.mult, r=[dstk], w=[tk])
                    B.tt(y, y, t, ALU.add, r=[dstk, tk], w=[dstk])
                B.ts(y, y, PI, -PI, ALU.min, ALU.max, r=[dstk], w=[dstk])
                B.act(y, y, AF.Sin, r=[dstk], w=[dstk])

        sin_layer(w1s, 33, fT, h1, 3, "fT", "h1", "hyw1")
        sin_layer(w2s, 64, h1, h2, 4, "h1", "h2", "hyw2")
        for tt_ in range(16):
            d_ = dct[tt_ % 2]
            dk_ = f"dct{tt_ % 2}"
            B.dma("sp", d_, hy_decay[:, tt_, :], w=[dk_])
            for g in range(4):
                pb = (tt_ * 4 + g) % 4
                B.mm(ps[pb][:, :512], h2[0:64, tt_ * 128:(tt_ + 1) * 128], w3[0:64, g * 512:(g + 1) * 512], True, True,
                     r=["h2", "w3"], w=[PSK[pb]])
                B.tt(HD[:, tt_, g, :], ps[pb][:, :512], d_, ALU.mult, r=[PSK[pb], dk_], w=["HD"])
        for g in (1, 3):
            B.memset(HD[0:1, 0, g, :], 0.0, w=["HD"])
        B.fence()
        for o in range(2):
            i = 0
            for tt_ in range(16):
                for dr in range(2):
                    t = tmp[i % 2]
                    tk = f"ftmp{i % 2}"
                    B.act(t, HD[:, tt_, o * 2 + dr, :], AF.Abs, r=["HD"], w=[tk])
                    B.mm(ps[4][:, :512], ONES, t, i == 0, i == 31, r=["cst", tk], w=[PSK[4]])
                    i += 1
            B.recip(rn, ps[4][:, :512], r=[PSK[4]], w=["rn"])
            for tt_ in range(16):
                t = tmp[tt_ % 2]
                tk = f"ftmp{tt_ % 2}"
                B.tt(t, HD[:, tt_, o * 2, :], HD[:, tt_, o * 2 + 1, :], ALU.add, r=["HD"], w=[tk])
                B.tt(AB[:, tt_, 0, :], t, rn, ALU.mult, r=[tk, "rn"], w=["AB"])
                B.tt(t, HD[:, tt_, o * 2, :], HD[:, tt_, o * 2 + 1, :], ALU.subtract, r=["HD"], w=[tk])
                B.tt(AB[:, tt_, 1, :], t, rn, ALU.mult, r=[tk, "rn"], w=["AB"])
            q = 0
            for ft in range(17):
                sb_ = spb[ft % 2]
                sk_ = f"spb{ft % 2}"
                for part in range(2):
                    sl, slk = slab[q % 3], f"fslab{q % 3}"
                    q += 1
                    B.dma("sp", sl, dftf[ft, part], w=[slk])
                    pb = 5 + (ft * 2 + part) % 3
                    for tt_ in range(16):
                        B.mm(ps[pb][:, :512], sl[:, tt_, :], AB[:, tt_, part, :], tt_ == 0, tt_ == 15,
                             r=[slk, "AB"], w=[PSK[pb]])
                    if part == 0:
                        B.act(sb_[:, part, :], ps[pb][:, :512], AF.Identity, r=[PSK[pb]], w=[sk_])
                    else:
                        B.cp(sb_[:, part, :], ps[pb][:, :512], r=[PSK[pb]], w=[sk_])
                B.dma("sp", spec[o, ft], sb_, r=[sk_], w=[f"spec{o}"])
        B.fence()

    if 1 in layers and DO_HY:
        stage_filter()

    def modv(l, m, k, col):
        return mod[:, l, m * 8 + k, col:col + 1]

    def norm_mod(src_fn, n, dst_fn, l, ni, col, C, srck, dstk):
        kk = f"nm@{C.off}:"
        sq = [C.get([512], F32) for _ in range(2)]
        rs = C.get([512], F32)
        tmp = [C.get([512], F32) for _ in range(2)]
        pb = 7
        for k in range(8):
            s = sq[k % 2]
            B.tt(s[:, :n], src_fn(k), src_fn(k), ALU.mult, r=srck, w=[kk + f"sq{k % 2}"], eng="pool")
            B.mm(ps[pb][:, :n], ONES, s[:, :n], k == 0, k == 7, r=[kk + f"sq{k % 2}", "cst"], w=[PSK[pb]])
        B.act(rs[:, :n], ps[pb][:, :n], AF.Sqrt, r=[PSK[pb], "small"], w=[kk + "rs"], bias=epsc, scale=1.0 / 1024)
        B.recip(rs[:, :n], rs[:, :n], r=[kk + "rs"], w=[kk + "rs"])
        mshift = 0 if ni == 0 else 3
        for k in range(8):
            t = tmp[k % 2]
            B.tt(t[:, :n], src_fn(k), rs[:, :n], ALU.mult, r=list(srck) + [kk + "rs"], w=[kk + f"t{k % 2}"])
            B.act(dst_fn(k), t[:, :n], AF.Identity, r=[kk + f"t{k % 2}", "mA", "mod"], w=dstk,
                  bias=modv(l, mshift, k, col), scale=mA[:, l, ni, k, col:col + 1])

    def ffn(l, b, X, lat_only):
        if lat_only:
            scs = [[(256, 1024, False, True)], [(1280, 1024, True, False)]]
        else:
            scs = [[(0, 256, False, False), (256, 896, False, True)], [(1152, 1152, True, False)]]
        wup = ffn_w_up[l].rearrange("(kt p) f -> p kt f", p=128)
        wdn = ffn_w_down[l].rearrange("(ft p) d -> p ft d", p=128)
        for si, segs in enumerate(scs):
            B.fence()
            C = Carver(P, 73728)
            t0 = segs[0][0] - (1 if segs[0][2] else 0)
            t1 = segs[-1][0] + segs[-1][1] + (1 if segs[-1][3] else 0)
            next_ = t1 - t0
            h2 = C.get([8, next_], BF16)
            ntok = sum(s[1] for s in segs)
            G = C.get([22, ntok], BF16)
            ubw = sum(s[1] + 2 for s in segs)
            ub = [C.get([2, ubw], BF16) for _ in range(2)]
            wsl = [C.get([8, 256], BF16) for _ in range(3)]
            dg = [C.get([6, 128], BF16) for _ in range(2)]
            wds = [C.get([22, 128], BF16) for _ in range(2)]
            sg = [C.get([512], F32) for _ in range(2)]
            h2k = P.k("h2")
            pos = t0
            if segs[0][2]:
                B.cp(h2[:, :, 0], h2st[:], r=["h2st"], w=[h2k])
                pos = t0 + 1
            while pos < t1:
                n = min(512, t1 - pos)
                if pos < 256 < pos + n:
                    n = 256 - pos
                col = 2 if pos < 256 else b
                Cn = Carver(P, C.off)
                norm_mod(lambda k, pos=pos, n=n: X[:, k, pos:pos + n], n,
                         lambda k, pos=pos, n=n: h2[:, k, pos - t0:pos - t0 + n], l, 1, col, Cn,
                         srck=["X"], dstk=[h2k])
                pos += n
            if si == 0:
                nxt = scs[1][0][0] - 1
                B.cp(h2st[:], h2[:, :, nxt - t0], r=[h2k], w=["h2st"])
            Gk = P.k("G")
            for j in range(22):
                wb = wsl[j % 3]
                wk = f"wup{j % 3}"
                B.dma("pool", wb[:, :, 0:128], wup[:, :, j * 128:(j + 1) * 128], w=[wk])
                B.dma("pool", wb[:, :, 128:256], wup[:, :, 2816 + j * 128:2816 + (j + 1) * 128], w=[wk])
                d = dg[j % 2]
                dk = f"dg{j % 2}"
                for vg in range(2):
                    for kk in range(3):
                        B.ts(d[:, vg * 3 + kk, :], IDENT, fcw[:, l, vg * 22 + j, kk:kk + 1], None, ALU.mult, None,
                             r=["cst", "fcw"], w=[dk])
                u = ub[j % 2]
                uk = f"ub{j % 2}"
                B.memset(u[:], 0.0, w=[uk], eng="pool")
                ucol = 0
                seg_ucol = []
                for (s0, sn, lh, rh) in segs:
                    seg_ucol.append(ucol)
                    a0 = s0 - (1 if lh else 0)
                    a1 = s0 + sn + (1 if rh else 0)
                    pos = a0
                    while pos < a1:
                        n = min(512, a1 - pos)
                        for vg in range(2):
                            pb = (2 * (pos // 512) + vg) % 4
                            for kt in range(8):
                                B.mm(ps[pb][:, :n], wb[:, kt, vg * 128:(vg + 1) * 128],
                                     h2[:, kt, pos - t0:pos - t0 + n], kt == 0, kt == 7,
                                     r=[wk, h2k], w=[PSK[pb]])
                            c0 = ucol + 1 + (pos - s0)
                            if vg == 0:
                                B.act(u[:, vg, c0:c0 + n], ps[pb][:, :n], AF.Identity, r=[PSK[pb]], w=[uk])
                            else:
                                B.cp(u[:, vg, c0:c0 + n], ps[pb][:, :n], r=[PSK[pb]], w=[uk])
                        pos += n
                    ucol += sn + 2
                gcol = 0
                for (s0, sn, lh, rh), uc in zip(segs, seg_ucol):
                    pos = 0
                    while pos < sn:
                        n = min(512, sn - pos)
                        pv, pg = 4 + (pos // 512) % 2 * 2, 5 + (pos // 512) % 2 * 2
                        for vg, pb in ((1, pg), (0, pv)):
                            for kk in range(3):
                                B.mm(ps[pb][:, :n], d[:, vg * 3 + kk, :], u[:, vg, uc + pos + kk:uc + pos + kk + n],
                                     kk == 0, kk == 2, r=[dk, uk], w=[PSK[pb]])
                        s = sg[(pos // 512) % 2]
                        sk = f"sg{(pos // 512) % 2}"
                        B.act(s[:, :n], ps[pg][:, :n], AF.Silu, r=[PSK[pg], "fcb"], w=[sk],
                              bias=fcb[:, l, 22 + j:23 + j], scale=1.0)
                        B.stt(G[:, j, gcol + pos:gcol + pos + n], ps[pv][:, :n], fcb[:, l, j:j + 1], s[:, :n],
                              ALU.add, ALU.mult, r=[PSK[pv], sk, "fcb"], w=[Gk])
                        pos += n
                    gcol += sn
            for dm in range(8):
                wd = wds[dm % 2]
                wdk = f"wdn{dm % 2}"
                B.dma("pool", wd, wdn[:, :, dm * 128:(dm + 1) * 128], w=[wdk])
                gcol = 0
                ci = 0
                for (s0, sn, lh, rh) in segs:
                    pos = 0
                    while pos < sn:
                        n = min(512, sn - pos)
                        pb = ci % 4
                        ci += 1
                        for ft in range(22):
                            B.mm(ps[pb][:, :n], wd[:, ft, :], G[:, ft, gcol + pos:gcol + pos + n],
                                 ft == 0, ft == 21, r=[wdk, Gk], w=[PSK[pb]])
                        col = 2 if s0 < 256 else b
                        xs = X[:, dm, s0 + pos:s0 + pos + n]
                        B.stt(xs, ps[pb][:, :n], modv(l, 5, dm, col), xs, ALU.mult, ALU.add,
                              r=[PSK[pb], "mod", "X"], w=["X"])
                        pos += n
                    gcol += sn
        B.fence()

    def u0col(tok):
        return tok + 15 if tok < 256 else tok + 45

    def layer0(b):
        l = 0
        B.fence()
        C = Carver(P, 0)
        QT = C.get([6, NT], BF16)
        KD = C.get([4, NT], BF16)
        VA = C.get([18, 4, 192], BF16)
        assert C.off <= 73728
        X = P.view(0, [8, NT], F32)
        C = Carver(P, 73728)
        H = C.get([8, NT], BF16)
        U0 = C.get([4, 2368], BF16)
        CO = C.get([4, NT], BF16)
        tbase = C.off
        xs = C.get([8, 512], F32)
        RC = C.get([2048], F32)
        RS = C.get([2048], F32)
        tmp_base = C.off
        B.dma("sp", RC, rope[:, 0, :], w=["RC"])
        B.dma("sp", RS, rope[:, 1, :], w=["RS"])
        B.memset(U0[:], 0.0, w=["U0"], eng="pool")
        B.memset(VA[:], 0.0, w=["VA"], eng="pool")
        B.memset(VA[:, :, :, 0:1], 1.0, w=["VA"], eng="pool")
        B.memset(VA[:, :, :, 128:129], 1.0, w=["VA"], eng="pool")
        for ci, (c0, n) in enumerate(CH):
            B.dma("sp", xs[:, :, :n], xin[b][:, :, c0:c0 + n], w=["xs"])
            Cn = Carver(P, tmp_base)
            norm_mod(lambda k, n=n: xs[:, k, :n], n, lambda k, c0=c0, n=n: H[:, k, c0:c0 + n], l, 0,
                     2 if ci == 0 else b, Cn, srck=["xs"], dstk=["H"])
        B.fence()
        Ct = Carver(P, tmp_base)
        wsl = [Ct.get([8, 128], BF16) for _ in range(3)]
        wv = P.view(tbase, [8, 256], BF16)
        qs_t = Ct.get([512], F32)
        q2_t = Ct.get([512], F32)
        rs_t = Ct.get([512], F32)
        t1_t = Ct.get([512], F32)
        t2_t = Ct.get([512], F32)
        win = ev_w_in.rearrange("(kt p) f -> p kt f", p=128)
        slab_i = [0]

        def load_slab(col_list):
            i = slab_i[0] % 3
            slab_i[0] += 1
            wb, wk = wsl[i], f"win{i}"
            wdt = 128 // len(col_list)
            for q, c in enumerate(col_list):
                B.dma("pool", wb[:, :, q * wdt:(q + 1) * wdt], win[:, :, c:c + wdt], w=[wk])
            return wb, wk

        def proj_chunk(wb, wk, c0, n, pb):
            for kt in range(8):
                B.mm(ps[pb][:, :n], wb[:, kt, :], H[:, kt, c0:c0 + n], kt == 0, kt == 7, r=[wk, "H"], w=[PSK[pb]])

        def qk_post(pb, ci, c0, n, gcol, dst, dstk):
            B.act(qs_t[:, :n], ps[pb][:, :n], AF.Identity, r=[PSK[pb], "evs"], w=["qs"], scale=evs[:, gcol:gcol + 1])
            B.act(q2_t[:, :n], ps[pb][:, :n], AF.Square, r=[PSK[pb]], w=["q2"])
            B.mm(ps[4][:, :n], BLK1, q2_t[:, :n], True, True, r=["cst", "q2"], w=[PSK[4]])
            if ci > 0:
                B.mm(ps[5][:, :n], PSW, qs_t[:, :n], True, True, r=["cst", "qs"], w=[PSK[5]])
            B.act(rs_t[:, :n], ps[4][:, :n], AF.Sqrt, r=[PSK[4], "small"], w=["rs"], bias=epsc, scale=1.0 / 64)
            B.recip(rs_t[:, :n], rs_t[:, :n], r=["rs"], w=["rs"])
            if ci > 0:
                r0 = c0 - 256
                B.tt(t1_t[:, :n], qs_t[:, :n], RC[:, r0:r0 + n], ALU.mult, r=["qs", "RC"], w=["t1"], eng="pool")
                B.tt(t2_t[:, :n], ps[5][:, :n], RS[:, r0:r0 + n], ALU.mult, r=[PSK[5], "RS"], w=["t2"])
                B.tt(t1_t[:, :n], t1_t[:, :n], t2_t[:, :n], ALU.add, r=["t1", "t2"], w=["t1"], eng="pool")
                B.tt(dst, t1_t[:, :n], rs_t[:, :n], ALU.mult, r=["t1", "rs"], w=dstk)
            else:
                B.tt(dst, qs_t[:, :n], rs_t[:, :n], ALU.mult, r=["qs", "rs"], w=dstk)

        for j in range(6):
            wb, wk = load_slab([j * 128])
            for ci, (c0, n) in enumerate(CH):
                pb = ci % 2
                proj_chunk(wb, wk, c0, n, pb)
                qk_post(pb, ci, c0, n, 0, QT[:, j, c0:c0 + n], ["QT"])
        for g in range(4):
            wb, wk = load_slab([768 + 64 * g, 768 + 64 * g])
            for ci, (c0, n) in enumerate(CH):
                pb = ci % 2
                proj_chunk(wb, wk, c0, n, pb)
                qk_post(pb, ci, c0, n, 1, KD[:, g, c0:c0 + n], ["KD"])
        B.dma("pool", wv, win[:, :, 1024:1280], w=["wv"])
        for tt_ in range(18):
            pb = tt_ % 2
            for kt in range(8):
                B.mm(ps[pb][:, 0:256], H[:, kt, tt_ * 128:(tt_ + 1) * 128], wv[:, kt, :], kt == 0, kt == 7,
                     r=["wv", "H"], w=[PSK[pb]])
            B.cp(VA[:, tt_, :, 64:128], ps[pb][:, 0:256].rearrange("p (g d) -> p g d", d=64), r=[PSK[pb]], w=["VA"])
        for j in range(4):
            wa, wak = load_slab([1280 + j * 128])
            wg, wgk = load_slab([1792 + j * 128])
            for ci, (c0, n) in enumerate(CH):
                pa, pg = 2 * (ci % 2), 2 * (ci % 2) + 1
                proj_chunk(wg, wgk, c0, n, pg)
                proj_chunk(wa, wak, c0, n, pa)
                B.act(qs_t[:, :n], ps[pg][:, :n], AF.Sigmoid, r=[PSK[pg]], w=["qs"])
                uc = u0col(c0)
                B.tt(U0[:, j, uc:uc + n], ps[pa][:, :n], qs_t[:, :n], ALU.mult, r=[PSK[pa], "qs"], w=["U0"])
        B.fence()
        Cd = Carver(P, tbase)
        DG = Cd.get([4, 31, 128], BF16)
        Ct = Carver(P, Cd.off)
        ub = Ct.get([4, 512], F32)
        u2 = [Ct.get([512], F32) for _ in range(2)]
        mean = Ct.get([512], F32)
        var = Ct.get([512], F32)
        dd = [Ct.get([512], F32) for _ in range(2)]
        for ct in range(4):
            for kk in range(31):
                B.ts(DG[:, ct, kk, :], IDENT, evcw[:, ct, kk:kk + 1], None, ALU.mult, None, r=["cst", "evcw"], w=["DG"])
        for ci, (c0, n) in enumerate(CH):
            uc = u0col(c0) - 15
            for ct in range(4):
                pb = ct % 2
                for kk in range(31):
                    B.mm(ps[pb][:, :n], DG[:, ct, kk, :], U0[:, ct, uc + kk:uc + kk + n], kk == 0, kk == 30,
                         r=["DG", "U0"], w=[PSK[pb]])
                B.act(ub[:, ct, :n], ps[pb][:, :n], AF.Identity, r=[PSK[pb], "evs"], w=["ub"],
                      bias=evs[:, 2 + ct:3 + ct], scale=1.0)
                B.tt(u2[ct % 2][:, :n], ub[:, ct, :n], ub[:, ct, :n], ALU.mult, r=["ub"], w=[f"u2{ct % 2}"])
                B.mm(ps[2][:, :n], ONES, ub[:, ct, :n], ct == 0, ct == 3, r=["cst", "ub"], w=[PSK[2]])
                B.mm(ps[3][:, :n], ONES, u2[ct % 2][:, :n], ct == 0, ct == 3, r=["cst", f"u2{ct % 2}"], w=[PSK[3]])
            B.act(mean[:, :n], ps[2][:, :n], AF.Identity, r=[PSK[2]], w=["mean"], scale=1.0 / 512)
            B.tt(var[:, :n], mean[:, :n], mean[:, :n], ALU.mult, r=["mean"], w=["var"])
            B.stt(var[:, :n], ps[3][:, :n], 1.0 / 512, var[:, :n], ALU.mult, ALU.subtract, r=[PSK[3], "var"], w=["var"])
            B.act(var[:, :n], var[:, :n], AF.Sqrt, r=["var", "small"], w=["var"], bias=epsc, scale=1.0)
            B.recip(var[:, :n], var[:, :n], r=["var"], w=["var"])
            for ct in range(4):
                d = dd[ct % 2]
                B.tt(d[:, :n], ub[:, ct, :n], mean[:, :n], ALU.subtract, r=["ub", "mean"], w=[f"dd{ct % 2}"])
                B.tt(d[:, :n], d[:, :n], var[:, :n], ALU.mult, r=[f"dd{ct % 2}", "var"], w=[f"dd{ct % 2}"])
                B.act(CO[:, ct, c0:c0 + n], d[:, :n], AF.Silu, r=[f"dd{ct % 2}", "evs"], w=["CO"],
                      bias=evs[:, 10 + ct:11 + ct], scale=evs[:, 6 + ct:7 + ct])
        B.fence()
        Ct = Carver(P, tbase)
        PT = [Ct.get([512], BF16) for _ in range(4)]
        OS = [Ct.get([512], F32) for _ in range(4)]
        ui = 0
        pending = []
        for j in range(6):
            hA, hB = 2 * j, 2 * j + 1
            gs = (hA // 3, hB // 3)
            for ci, (c0, n) in enumerate(CH):
                nkt = 2 if ci == 0 else 18
                ob = 2 * (ui % 2)
                Os = (OS[ob], OS[ob + 1])
                oks = (f"OS{ob}", f"OS{ob + 1}")
                ui += 1

                def s_pair(kt, n=n, c0=c0, j=j, gs=gs):
                    bk = 4 + 2 * (kt % 2)
                    for q in range(2):
                        off = 64 * q
                        wk = [PSK[bk], PSK[bk + 1]] if q == 0 else [PSK[bk + 1]]
                        B.mm(ps[bk + q][:, :n], KD[off:off + 64, gs[q], kt * 128:(kt + 1) * 128],
                             QT[off:off + 64, j, c0:c0 + n], True, True, r=["KD", "QT"], w=wk)

                def e_pair(kt, n=n):
                    bk = 4 + 2 * (kt % 2)
                    for q in range(2):
                        pi = 2 * (kt % 2) + q
                        B.act(PT[pi][:, :n], ps[bk + q][:, :n], AF.Exp, r=[PSK[bk + q]], w=[f"PT{pi}"], scale=0.125)

                def pv_pair(kt, n=n, gs=gs, ob=ob, nkt=nkt):
                    for q in range(2):
                        pi = 2 * (kt % 2) + q
                        rk = [f"PT{pi}", f"PT{pi + 1}"] if q == 0 else [f"PT{pi}"]
                        lhs = VA[:, kt, gs[q], 64:192] if q == 0 else VA[:, kt, gs[q], 0:128]
                        B.mm(ps[ob + q][:, :n], lhs, PT[pi][:, :n], kt == 0, kt == nkt - 1, r=["VA"] + rk,
                             w=[PSK[ob + q]])

                def epilogue(n=n, c0=c0, ob=ob, Os=Os, oks=oks, j=j):
                    for q in range(2):
                        O, ok, po, off = Os[q], oks[q], ob + q, 64 * q
                        B.act(O[:, :n], ps[po][:, :n], AF.Identity, r=[PSK[po]], w=[ok])
                        dr = 64 if q == 0 else 0
                        B.recip(O[dr:dr + 1, :n], O[dr:dr + 1, :n], r=[ok], w=[ok])
                    for q in range(2):
                        O, ok, po, off = Os[q], oks[q], ob + q, 64 * q
                        if q == 0:
                            B.mm(ps[po][0:64, :n], ESE[0:65, 0:64], O[0:65, :n], True, True, r=["cst", ok], w=[PSK[po]])
                        else:
                            B.mm(ps[po][:, :n], ESO, O[:, :n], True, True, r=["cst", ok], w=[PSK[po]])
                        B.tt(H[off:off + 64, j, c0:c0 + n], O[off:off + 64, :n], ps[po][off:off + 64, :n], ALU.mult,
                             r=[ok, PSK[po]], w=["H"])

                s_pair(0)
                e_pair(0)
                for kt in range(nkt):
                    if kt + 1 < nkt:
                        s_pair(kt + 1)
                        e_pair(kt + 1)
                    if kt == min(3, nkt - 1):
                        for ep in pending:
                            ep()
                        pending = []
                    pv_pair(kt)
                pending.append(epilogue)
        for ep in pending:
            ep()
        B.fence()
        Ct = Carver(P, tbase)
        wos = [Ct.get([10, 128], BF16) for _ in range(2)]
        wo = ev_w_out.rearrange("(ft p) d -> p ft d", p=128)
        for dm in range(8):
            w_, wk = wos[dm % 2], f"wo{dm % 2}"
            B.dma("pool", w_, wo[:, :, dm * 128:(dm + 1) * 128], w=[wk])
            B.dma("sp", X[:, dm, :], xin[b][:, dm, :], w=["X"])
            for ci, (c0, n) in enumerate(CH):
                pb = ci % 4
                for ft in range(10):
                    rhs = H[:, ft, c0:c0 + n] if ft < 6 else CO[:, ft - 6, c0:c0 + n]
                    B.mm(ps[pb][:, :n], w_[:, ft, :], rhs, ft == 0, ft == 9, r=[wk, "H", "CO"], w=[PSK[pb]])
                col = 2 if ci == 0 else b
                xs_ = X[:, dm, c0:c0 + n]
                B.stt(xs_, ps[pb][:, :n], modv(l, 2, dm, col), xs_, ALU.mult, ALU.add,
                      r=[PSK[pb], "mod", "X"], w=["X"])
        B.fence()
        if "xmid0" in P.taps:
            B.dma("sp", tapd["xmid0"][b], X, r=["X"], w=[P.k("tap")])
        ffn(0, b, X, lat_only=False)
        if "x0" in P.taps:
            B.dma("sp", tapd["x0"][b], X, r=["X"], w=[P.k("tap")])
        return X


    AX = mybir.AxisListType

    def sub(pk, hs=range(4)):
        return [pk]

    def layer1(b, X, do_hyena=True):
        l = 1
        B.fence()
        B.dma("sp", xsp[b], X, r=["X"], w=["xsp"])
        H = P.view(73728, [8, NT], BF16)
        for ci, (c0, n) in enumerate(CH):
            Cn = Carver(P, 161792)
            norm_mod(lambda k, c0=c0, n=n: X[:, k, c0:c0 + n], n, lambda k, c0=c0, n=n: H[:, k, c0:c0 + n], l, 0,
                     2 if ci == 0 else b, Cn, srck=["X"], dstk=["H"])
        B.fence()
        PT = P.view(0, [12, 2050], BF16)
        QF = P.view(49200, [4, NT], BF16)
        MIX = P.view(110592, [8, 2048], BF16)
        VT = P.view(143360, [18, 512], BF16)
        win = od_w_in.rearrange("(kt p) f -> p kt f", p=128)
        Ct = Carver(P, 161792)
        wsl = [Ct.get([8, 128], BF16) for _ in range(3)]
        B.memset(PT[:, :, 0:1], 0.0, w=["PT"])
        B.memset(PT[:, :, 2049:2050], 0.0, w=["PT"])
        for j in range(16):
            wb, wk = wsl[j % 3], f"w1in{j % 3}"
            B.dma("pool", wb, win[:, :, j * 128:(j + 1) * 128], w=[wk])
            for ci, (c0, n) in enumerate(CH):
                if j < 12 and ci == 0:
                    continue
                pb = ci % 4
                for kt in range(8):
                    B.mm(ps[pb][:, :n], wb[:, kt, :], H[:, kt, c0:c0 + n], kt == 0, kt == 7, r=[wk, "H"], w=[PSK[pb]])
                if j < 12:
                    dst, dk = PT[:, j, 1 + c0 - 256:1 + c0 - 256 + n], "PT"
                else:
                    dst, dk = QF[:, j - 12, c0:c0 + n], "QF"
                if ci % 2 == 0:
                    B.act(dst, ps[pb][:, :n], AF.Identity, r=[PSK[pb]], w=[dk])
                else:
                    B.cp(dst, ps[pb][:, :n], r=[PSK[pb]], w=[dk])
        B.fence()
        Ct = Carver(P, 161792)
        Wf = Ct.get([8, 512], BF16)
        Wi = Ct.get([8, 512], BF16)
        fr = Ct.get([512], F32)
        lg = Ct.get([512], F32)
        kk_ = Ct.get([512], F32)
        e1 = Ct.get([512], F32)
        kd = Ct.get([512], F32)
        ebT = Ct.get([4, 128], F32)
        kdec = Ct.get([512], BF16)
        qdT = Ct.get([4, 128], BF16)
        kdT = Ct.get([512], BF16)
        AT = Ct.get([4, 128], BF16)
        S32 = Ct.get([4, 128], F32)
        S16 = Ct.get([4, 128], BF16)
        dec = Ct.get([4, 2], F32)
        Cx = Carver(P, 110592, 110592 + 16384)
        lbv = Cx.get([2, 512], F32)
        oml = Cx.get([2, 512], F32)
        osb = Cx.get([2, 512], F32)
        ofl = Cx.get([2, 512], F32)
        Cy = Carver(P, 49200 + 18432, 73728)
        ss = Cy.get([8], F32)
        MSK = Cy.get([2, 4, 128], F32)
        gnb = Ct.get([512], F32)
        B.dma("sp", lbv, lbraw[:, 0], w=["lbv"])
        B.dma("sp", oml, lbraw[:, 1], w=["oml"])
        B.dma("sp", gnb, gnb_d, w=["gnb"])
        B.act(lbv, lbv, AF.Exp, r=["lbv"], w=["lbv"])
        B.act(oml, oml, AF.Exp, r=["oml"], w=["oml"])
        B.tt(lbv, lbv, oml, ALU.add, r=["lbv", "oml"], w=["lbv"])
        B.recip(lbv, lbv, r=["lbv"], w=["lbv"])
        B.tt(lbv, lbv, oml, ALU.mult, r=["lbv", "oml"], w=["lbv"])
        B.ts(oml, lbv, -1.0, 1.0, ALU.mult, ALU.add, r=["lbv"], w=["oml"])
        for hd in range(4):
            B.cp(MSK[:, 0, hd, :], TRIF, r=["cst"], w=["MSK"])
            B.cp(MSK[:, 1, hd, :], TRIB, r=["cst"], w=["MSK"])

        def wload(dst, col0, key):
            B.dma("pool", dst, win[:, :, col0:col0 + 512], w=[key])

        def hg_tile(tt_, d):
            lat = tt_ >= 2
            tl = tt_ - 2
            tsl = slice(tt_ * 128, (tt_ + 1) * 128)
            for kt in range(8):
                B.mm(ps[0][:, :512], H[:, kt, tsl], Wf[:, kt, :], kt == 0, kt == 7, r=["H", "Wf"], w=[PSK[0]])
            B.act(fr, ps[0][:, :512], AF.Sigmoid, r=[PSK[0]], w=["fr"])
            B.tt(fr, fr, oml[:, d, :], ALU.mult, r=["fr", "oml"], w=["fr"])
            B.tt(fr, fr, lbv[:, d, :], ALU.add, r=["fr", "lbv"], w=["fr"])
            B.act(lg, fr, AF.Ln, r=["fr"], w=["lg"])
            B.ts(kk_, fr, -1.0, 1.0, ALU.mult, ALU.add, r=["fr"], w=["kk"])
            if d == 0:
                for kt in range(8):
                    B.mm(ps[1][:, :512], H[:, kt, tsl], Wi[:, kt, :], kt == 0, kt == 7, r=["H", "Wi"], w=sub(PSK[1]))
                B.act(VT[:, tt_, :], ps[1][:, :512], AF.Identity, r=sub(PSK[1]), w=["VT"])
            TRI, STR = (TRIF, STRF) if d == 0 else (TRIB, STRB)
            B.mm(ps[2][:, :512], TRI, lg, True, True, r=["cst", "lg"], w=sub(PSK[2]))
            B.mm(ps[3][:, :512], STR, lg, True, True, r=["cst", "lg"], w=sub(PSK[3]))
            B.act(e1, ps[2][:, :512], AF.Exp, r=sub(PSK[2]), w=["e1"], scale=-1.0)
            B.tt(kd, kk_, e1, ALU.mult, r=["kk", "e1"], w=["kd"])
            B.act(e1, ps[3][:, :512], AF.Exp, r=sub(PSK[3]), w=["e1"])
            B.tt(kdec, kk_, e1, ALU.mult, r=["kk", "e1"], w=["kdec"])
            for hd in range(4):
                hs = slice(hd * 128, (hd + 1) * 128)
                B.mm(ps[4][:, hs], lg[:, hs], TRI, True, True, r=["cst", "lg"], w=[PSK[4]])
            p4 = ps[4][:, :512].rearrange("p (h t) -> p h t", t=128)
            B.act(ebT, p4, AF.Exp, r=sub(PSK[4]), w=["ebT"])
            B.tt(qdT, QF[:, :, tsl], ebT, ALU.mult, r=["QF", "ebT"], w=["qdT"])
            cols = (63, 127) if d == 0 else (0, 64)
            for c in range(2):
                B.act(dec[:, :, c:c + 1], p4[:, :, cols[c]:cols[c] + 1], AF.Exp, r=sub(PSK[4]), w=["dec"])
            for hd in range(4):
                hs = slice(hd * 128, (hd + 1) * 128)
                B.tr(ps[5][:, hs], kd[:, hs], IDENT, r=["kd", "cst"], w=[PSK[5]])
            B.act(kdT, ps[5][:, :512], AF.Identity, r=sub(PSK[5]), w=["kdT"])
            SK = os.environ.get("HG_SKIP", "")
            if lat and "a" not in SK:
                for hd in range(4):
                    hs = slice(hd * 128, (hd + 1) * 128)
                    B.mm(ps[6][:, hs], kdT[:, hs], qdT[:, hd, :], True, True, r=["kdT", "qdT"], w=[PSK[6]])
                B.tt(AT, ps[6][:, :512].rearrange("p (h t) -> p h t", t=128), MSK[:, d], ALU.mult,
                     r=sub(PSK[6]) + ["MSK"], w=["AT"])
            corder = (0, 1) if d == 0 else (1, 0)
            for qi, c in enumerate(corder):
                cs = slice(c * 64, (c + 1) * 64)
                po = 7 if c == 0 else 1
                if lat and "o" not in SK:
                    for hd in range(4):
                        hs = slice(hd * 128, (hd + 1) * 128)
                        B.mm(ps[po][0:64, hs], AT[:, hd, cs], VT[:, tt_, hs], True, False, r=["AT", "VT"],
                             w=[PSK[po]])
                        B.mm(ps[po][0:64, hs], qdT[:, hd, cs], S16[:, hd, :], False, True, r=["qdT", f"S16:{hd}"],
                             w=[PSK[po]])
                pu = 2 + qi
                for hd in range(4):
                    hs = slice(hd * 128, (hd + 1) * 128)
                    B.mm(ps[pu][:, hs], kdec[cs, hs], VT[cs, tt_, hs], True, True, r=["kdec", "VT"],
                         w=[PSK[pu]])
                for hd in range(4):
                    hs = slice(hd * 128, (hd + 1) * 128)
                    B.stt(S32[:, hd, :], S32[:, hd, :], dec[:, hd, c:c + 1], ps[pu][:, hs], ALU.mult, ALU.add,
                          r=[f"S32:{hd}", "dec", PSK[pu]], w=[f"S32:{hd}"])
                    B.act(S16[:, hd, :], S32[:, hd, :], AF.Identity, r=[f"S32:{hd}"], w=[f"S16:{hd}"])
            if not lat or "r" in SK:
                return
            if d == 0:
                B.act(osb[0:64, 0, :], ps[7][0:64, :512], AF.Identity, r=sub(PSK[7]), w=["osb"])
                B.cp(osb[0:64, 1, :], ps[1][0:64, :512], r=sub(PSK[1]), w=["osb"])
                B.dma("sp", ofs[b, tl].rearrange("p (c f) -> p c f", c=2), osb[0:64], r=["osb"], w=[f"ofs{tl}"])
                return
            B.dma("sp", ofl[0:64], ofs[b, tl].rearrange("p (c f) -> p c f", c=2), r=[f"ofs{tl}"], w=["ofl"])
            B.tt(osb[0:64, 0, :], ofl[0:64, 0, :], ps[7][0:64, :512], ALU.add, r=["ofl"] + sub(PSK[7]), w=["osb"])
            B.tt(osb[0:64, 1, :], ofl[0:64, 1, :], ps[1][0:64, :512], ALU.add, r=["ofl"] + sub(PSK[1]), w=["osb"])
            B.tt(ofl[0:64], osb[0:64], osb[0:64], ALU.mult, r=["osb"], w=["ofl"])
            B.op("dve", lambda e: e.tensor_reduce(ss[0:64, :], ofl[0:64].rearrange("p c (h e) -> p (c h) e", e=128),
                                                  AX.X, ALU.add), r=["ofl"], w=["ss"])
            B.act(ss[0:64, :], ss[0:64, :], AF.Sqrt, r=["ss", "small"], w=["ss"], bias=epsc[0:64], scale=1.0 / 128)
            B.recip(ss[0:64, :], ss[0:64, :], r=["ss"], w=["ss"])
            for c in range(2):
                for hd in range(4):
                    hs = slice(hd * 128, (hd + 1) * 128)
                    B.stt(osb[0:64, c, hs], osb[0:64, c, hs], ss[0:64, c * 4 + hd:c * 4 + hd + 1], gnb[0:64, hs],
                          ALU.mult, ALU.mult, r=["osb", "ss", "gnb"], w=["osb"])
            for c in range(2):
                pg = 4 + c
                for kt in range(8):
                    B.mm(ps[pg][0:64, :512], H[:, kt, tt_ * 128 + c * 64:tt_ * 128 + (c + 1) * 64], Wi[:, kt, :],
                         kt == 0, kt == 7, r=["H", "Wi"], w=sub(PSK[pg]))
                B.act(ofl[0:64, c, :], ps[pg][0:64, :512], AF.Silu, r=sub(PSK[pg]), w=["ofl"])
            B.tt(osb[0:64], osb[0:64], ofl[0:64], ALU.mult, r=["osb", "ofl"], w=["osb"])
            for c in range(2):
                for hd in range(4):
                    B.tr(ps[6][:, hd * 128 + c * 64:hd * 128 + (c + 1) * 64], osb[0:64, c, hd * 128:(hd + 1) * 128],
                         cst[0:64, 0, 0:64], r=["osb", "cst"], w=[PSK[6]])
            B.act(MIX[:, 4:8, tl * 128:(tl + 1) * 128], ps[6][:, :512].rearrange("p (h t) -> p h t", t=128),
                  AF.Identity, r=sub(PSK[6]), w=["MIX"])

        for d in range(int(os.environ.get("HG_ND", "2"))):
            wload(Wf, 1536 + 512 + d * 512, "Wf")
            wload(Wi, 1536 + (1536 if d == 0 else 2048), "Wi")
            for hd in range(4):
                B.memset(S32[:, hd, :], 0.0, w=[f"S32:{hd}"])
                B.memset(S16[:, hd, :], 0.0, w=[f"S16:{hd}"])
            order = list(range(18)) if d == 0 else [1, 0] + list(range(17, 1, -1))
            order = order[:int(os.environ.get("HG_NT", "18"))]
            for tt_ in order:
                hg_tile(tt_, d)
        B.fence()
        if "dlat" in P.taps:
            B.dma("pool", tapd["dlat"][b], MIX[:, 4:8, :], r=["MIX"], w=[P.k("tap")])

        if DO_HY:
            VX = P.view(143360, [3, 16, 512], BF16)
            Z2 = P.view(73728, [16, 512], BF16)
            Ct = Carver(P, 73728 + 16384, 110592)
            dgs = [Ct.get([3, 128], BF16) for _ in range(2)]
            uT = [Ct.get([512], F32) for _ in range(2)]
            hsw = Ct.get([12, 3], F32)
            hsb = Ct.get([12], F32)
            skb = Ct.get([2, 512], F32)
            B.dma("sp", hsw, hy_sw, w=["hsw"])
            B.dma("sp", hsb, hy_sb, w=["hsb"])
            B.dma("sp", skb, hy_skip, w=["skb"])
            for ct in range(12):
                dg_, dgk = dgs[ct % 2], f"hdg{ct % 2}"
                for kk in range(3):
                    B.ts(dg_[:, kk, :], IDENT, hsw[:, ct, kk:kk + 1], None, ALU.mult, None, r=["cst", "hsw"], w=[dgk])
                for c in range(4):
                    pb = c % 2
                    u_, uk = uT[c % 2], f"uT{c % 2}"
                    for kk in range(3):
                        B.mm(ps[pb][:, :512], dg_[:, kk, :], PT[:, ct, c * 512 + kk:c * 512 + kk + 512], kk == 0, kk == 2,
                             r=[dgk, "PT"], w=[PSK[pb]])
                    B.act(u_, ps[pb][:, :512], AF.Identity, r=[PSK[pb], "hsb"], w=[uk], bias=hsb[:, ct:ct + 1], scale=1.0)
                    pt_ = 2 + c % 2
                    for q in range(4):
                        B.tr(ps[pt_][:, q * 128:(q + 1) * 128], u_[:, q * 128:(q + 1) * 128], IDENT, r=[uk, "cst"],
                             w=[PSK[pt_]])
                    dst = VX[:, ct // 4, c * 4:(c + 1) * 4, (ct % 4) * 128:(ct % 4 + 1) * 128]
                    B.cp(dst, ps[pt_][:, :512].rearrange("p (q c) -> p q c", c=128), r=[PSK[pt_]], w=["VX"])
            B.fence()
            Ct = Carver(P, 0, 73728)
            Y = Ct.get([17, 2, 512], BF16)
            fsl = [Ct.get([16, 128], BF16) for _ in range(3)]
            isl = [Ct.get([17, 2, 128], BF16) for _ in range(2)]
            Ct2 = Carver(P, 73728 + 16384, 110592)
            skb = Ct2.get([2, 512], F32)
            spt = [Ct2.get([2, 512], F32) for _ in range(2)]
            t1s = [Ct2.get([512], F32) for _ in range(2)]
            B.dma("sp", skb, hy_skip, w=["skb2"])
            for o in range(2):
                zin = VX[:, 0] if o == 0 else Z2
                zk = "VX" if o == 0 else "Z2"
                q = 0
                for ft in range(17):
                    sp_, spk = spt[ft % 2], f"spt{ft % 2}"
                    B.dma("sp", sp_, spec[o, ft], r=[f"spec{o}"], w=[spk])
                    pz = [4 + 2 * (ft % 2), 5 + 2 * (ft % 2)]
                    for part in range(2):
                        sl, slk = fsl[q % 3], f"hfsl{q % 3}"
                        q += 1
                        B.dma("sp", sl, dftf[ft, part], w=[slk])
                        for tt_ in range(16):
                            B.mm(ps[pz[part]][:, :512], sl[:, tt_, :], zin[:, tt_, :], tt_ == 0, tt_ == 15, r=[slk, zk],
                                 w=[PSK[pz[part]]])
                    a, bq = t1s
                    B.tt(a, ps[pz[0]][:, :512], sp_[:, 0, :], ALU.mult, r=[PSK[pz[0]], spk], w=["t1a"])
                    B.tt(bq, ps[pz[1]][:, :512], sp_[:, 1, :], ALU.mult, r=[PSK[pz[1]], spk], w=["t1b"])
                    B.tt(Y[:, ft, 0, :], a, bq, ALU.subtract, r=["t1a", "t1b"], w=["Y"], eng="pool")
                    B.tt(a, ps[pz[0]][:, :512], sp_[:, 1, :], ALU.mult, r=[PSK[pz[0]], spk], w=["t1a"])
                    B.tt(bq, ps[pz[1]][:, :512], sp_[:, 0, :], ALU.mult, r=[PSK[pz[1]], spk], w=["t1b"])
                    B.tt(Y[:, ft, 1, :], a, bq, ALU.add, r=["t1a", "t1b"], w=["Y"], eng="pool")
                for tt_ in range(16):
                    il, ilk = isl[tt_ % 2], f"hisl{tt_ % 2}"
                    B.dma("sp", il, dfti[tt_], w=[ilk])
                    pb = tt_ % 2
                    i = 0
                    for ft in range(17):
                        for part in range(2):
                            B.mm(ps[pb][:, :512], il[:, ft, part, :], Y[:, ft, part, :], i == 0, i == 33, r=[ilk, "Y"],
                                 w=[PSK[pb]])
                            i += 1
                    a = t1s[tt_ % 2]
                    ak = f"t1{'ab'[tt_ % 2]}"
                    B.tt(a, zin[:, tt_, :], skb[:, o, :], ALU.mult, r=[zk, "skb2"], w=[ak])
                    B.tt(a, a, ps[pb][:, :512], ALU.add, r=[ak, PSK[pb]], w=[ak])
                    if o == 0:
                        B.tt(Z2[:, tt_, :], a, VX[:, 1, tt_, :], ALU.mult, r=[ak, "VX"], w=["Z2"])
                    else:
                        B.tt(a, a, VX[:, 2, tt_, :], ALU.mult, r=[ak, "VX"], w=[ak])
                        pt_ = 2 + tt_ % 2
                        for q4 in range(4):
                            B.tr(ps[pt_][:, q4 * 128:(q4 + 1) * 128], a[:, q4 * 128:(q4 + 1) * 128], IDENT, r=[ak, "cst"],
                                 w=[PSK[pt_]])
                        B.act(MIX[:, 0:4, tt_ * 128:(tt_ + 1) * 128],
                              ps[pt_][:, :512].rearrange("p (h t) -> p h t", t=128), AF.Identity, r=[PSK[pt_]], w=["MIX"])
            B.fence()
            if "clat" in P.taps:
                B.dma("pool", tapd["clat"][b], MIX[:, 0:4, :], r=["MIX"], w=[P.k("tap")])
        else:
            B.memset(MIX[:, 0:4, :], 0.0, w=["MIX"])
            B.fence()
        wos = [P.view(143360 + i * 2048, [8, 128], BF16) for i in range(2)]
        wo = od_w_out.rearrange("(ft p) d -> p ft d", p=128)
        for dm in range(8):
            w_, wk = wos[dm % 2], f"wo1{dm % 2}"
            B.dma("pool", w_, wo[:, :, dm * 128:(dm + 1) * 128], w=[wk])
            B.dma("sp", X[:, dm, 256:NT], xsp[b][:, dm, 256:NT], r=["xsp"], w=["X"])
            for c in range(4):
                pb = c % 4
                for ft in range(8):
                    B.mm(ps[pb][:, :512], w_[:, ft, :], MIX[:, ft, c * 512:(c + 1) * 512], ft == 0, ft == 7,
                         r=[wk, "MIX"], w=[PSK[pb]])
                xs_ = X[:, dm, 256 + c * 512:256 + (c + 1) * 512]
                B.stt(xs_, ps[pb][:, :512], modv(l, 2, dm, b), xs_, ALU.mult, ALU.add, r=[PSK[pb], "mod", "X"], w=["X"])
        B.fence()
        if "xmid1" in P.taps:
            B.dma("sp", tapd["xmid1"][b], X, r=["X"], w=[P.k("tap")])
        ffn(1, b, X, lat_only=True)
        return X

    def final(b, X):
        B.fence()
        C = Carver(P, 73728)
        ob = [C.get([8, 512], F32) for _ in range(2)]
        sq = [C.get([512], F32) for _ in range(2)]
        rs = C.get([512], F32)
        for ci in range(4):
            c0 = 256 + ci * 512
            n = 512
            o_, okk = ob[ci % 2], f"ob{ci % 2}"
            for k in range(8):
                s = sq[k % 2]
                B.tt(s[:, :n], X[:, k, c0:c0 + n], X[:, k, c0:c0 + n], ALU.mult, r=["X"], w=[f"fsq{k % 2}"])
                B.mm(ps[7][:, :n], ONES, s[:, :n], k == 0, k == 7, r=[f"fsq{k % 2}", "cst"], w=[PSK[7]])
            B.act(rs[:, :n], ps[7][:, :n], AF.Sqrt, r=[PSK[7], "small"], w=["frs"], bias=epsc, scale=1.0 / 1024)
            B.recip(rs[:, :n], rs[:, :n], r=["frs"], w=["frs"])
            for k in range(8):
                B.stt(o_[:, k, :], X[:, k, c0:c0 + n], gv[:, 4, k:k + 1], rs[:, :n], ALU.mult, ALU.mult,
                      r=["X", "gv", "frs"], w=[okk])
            B.dma("sp", out[b][:, :, ci * 512:(ci + 1) * 512], o_, r=[okk], w=[P.k("out")])

    import os
    for b in range(2):
        if 0 in layers:
            X = layer0(b)
        else:
            X = P.view(0, [8, NT], F32)
            B.dma("sp", X, xin[b], w=["X"])
        if 1 in layers:
            layer1(b, X)
        final(b, X)
    nc = B.build()
    return nc, P


def host_consts():
    c = np.zeros((128, 10, 128), np.float32)
    ii = np.arange(128)
    same = (ii[:, None] // 64) == (ii[None, :] // 64)
    c[:, 6, :] = same & (ii[:, None] <= ii[None, :])
    c[:, 7, :] = same & (ii[:, None] >= ii[None, :])
    c[:, 8, :] = same & (ii[:, None] > ii[None, :])
    c[:, 9, :] = same & (ii[:, None] < ii[None, :])
    c[:, 0, :] = np.eye(128)
    c[:, 1, :] = 1.0
    c[0:64, 2, 0:64] = 1.0
    c[64:128, 2, 64:128] = 1.0
    for i in range(64):
        c[2 * i + 1, 3, 2 * i] = -1.0
        c[2 * i, 3, 2 * i + 1] = 1.0
    c[64, 4, 0:64] = 1.0
    c[0, 5, 64:128] = 1.0
    return c


def host_rope():
    L, GW, dh = 2048, 64, 64
    row = np.repeat(np.arange(L // GW, dtype=np.float32), GW)
    col = np.tile(np.arange(GW, dtype=np.float32), L // GW)
    axis_dim = dh // 2
    inv = (10000.0 ** (-np.arange(0, axis_dim, 2, dtype=np.float32) / axis_dim)).astype(np.float32)
    ang = np.concatenate([row[:, None] * inv, col[:, None] * inv], axis=-1)
    cos, sin = np.cos(ang).astype(np.float32), np.sin(ang).astype(np.float32)
    r = np.zeros((128, 2, L), np.float32)
    for p in range(128):
        i = (p % 64) // 2
        r[p, 0] = cos[:, i]
        r[p, 1] = sin[:, i]
    return r

_HC = {}


def host_hyena_consts():
    if _HC:
        return _HC
    L, N = 2048, 4096
    f32 = np.float32
    t = np.linspace(0.0, 1.0, L, dtype=f32)[:, None]
    bands = 16
    fb = np.linspace(1e-4, bands - 1, bands, dtype=f32)
    wpos = (f32(2.0 * math.pi / L) * np.arange(L, dtype=f32))[:, None]
    feats = np.concatenate([t, np.cos(fb * wpos), -np.sin(fb * wpos)], axis=-1).astype(f32)
    _HC["featsT"] = np.ascontiguousarray(feats.T)
    deltas = np.abs(np.linspace(math.log(1e-2) / 1.5, math.log(1e-2) / 0.3, 512, dtype=f32))
    dec = np.exp(-t * deltas).astype(f32)
    _HC["hy_decay"] = np.ascontiguousarray(np.transpose(dec.reshape(16, 128, 512), (1, 0, 2)))
    tt = np.arange(L, dtype=np.int64)[:, None]
    ff = np.arange(17 * 128, dtype=np.int64)[None, :]
    ang = 2.0 * np.pi * ((tt * ff) % N).astype(np.float64) / N
    valid = (ff <= L).astype(np.float64)
    Mc = np.cos(ang) * valid
    Ms = -np.sin(ang) * valid
    fw = np.stack([Mc, Ms], 0).reshape(2, 16, 128, 17, 128)
    _HC["dftf"] = np.ascontiguousarray(np.transpose(fw, (3, 0, 2, 1, 4))).astype(ml_dtypes.bfloat16)
    wf = np.where((ff == 0) | (ff == L), 1.0, 2.0) * valid / N
    Ic = (np.cos(ang) * wf).T
    Is = (-np.sin(ang) * wf).T
    iw = np.stack([Ic, Is], 0).reshape(2, 17, 128, 16, 128)
    _HC["dfti"] = np.ascontiguousarray(np.transpose(iw, (3, 2, 1, 0, 4))).astype(ml_dtypes.bfloat16)
    return _HC


def fm(v):
    return np.ascontiguousarray(v.reshape(-1, 128).T)


def prep_inputs(inp):
    f = lambda a: np.ascontiguousarray(np.asarray(a, dtype=np.float32))
    x, ctx = f(inp["x"]), f(inp["ctx"])
    shared = {
        "ada_w": f(inp["ada_w"]),
        "abT": np.ascontiguousarray(np.stack([fm(inp["ada_b"][l]) for l in range(2)], axis=1)),
        "gvec": np.ascontiguousarray(np.stack([fm(inp["norm1_g"][0]), fm(inp["norm1_g"][1]), fm(inp["norm2_g"][0]),
                                               fm(inp["norm2_g"][1]), fm(inp["final_g"])], axis=1)),
        "consts": host_consts(),
        "rope": host_rope(),
        "ev_w_in": f(inp["ev_w_in"][0]),
        "ev_cw": np.ascontiguousarray(np.transpose(f(inp["ev_conv_w"][0]).reshape(31, 4, 128), (2, 1, 0))),
        "ev_w_out": f(inp["ev_w_out"][0]),
        "ffn_w_up": f(inp["ffn_w_up"]),
        "ffn_cw": np.ascontiguousarray(np.transpose(f(inp["ffn_conv_w"]).reshape(2, 3, 44, 128), (3, 0, 2, 1))),
        "ffn_cb": np.ascontiguousarray(np.transpose(f(inp["ffn_conv_b"]).reshape(2, 44, 128), (2, 0, 1))),
        "ffn_w_down": f(inp["ffn_w_down"]),
        "od_w_in": f(inp["od_w_in"][0]),
        "od_w_out": f(inp["od_w_out"][0]),
        "lbraw": np.ascontiguousarray(np.broadcast_to(f(inp["od_lower_bound"])[None], (128, 2, 2, 512))),
        "gnb": np.ascontiguousarray(np.broadcast_to(np.tile(f(inp["od_gnorm_g"][0]), 4)[None], (128, 512))),
        "hy_w1": f(inp["od_filt_w1"][0]),
        "hy_w2": f(inp["od_filt_w2"][0]),
        "hy_w3": f(inp["od_filt_w3"][0]),
        "hy_sw": np.ascontiguousarray(np.transpose(f(inp["od_short_w"][0]).reshape(3, 12, 128), (2, 1, 0))),
        "hy_sb": fm(f(inp["od_short_b"][0])),
        "hy_skip": np.ascontiguousarray(np.broadcast_to(f(inp["od_hyena_skip"][0])[None], (128, 2, 512))),
    }
    hs = np.zeros((128, 8), np.float32)
    hs[0:64, 0] = f(inp["od_filt_b1"][0])
    hs[0:64, 1] = f(inp["od_filt_freq"][0])
    hs[0:64, 2] = f(inp["od_filt_b2"][0])
    shared["hy_small"] = hs
    shared.update(host_hyena_consts())
    evs = np.zeros((128, 16), np.float32)
    evs[:, 0] = np.tile(f(inp["ev_q_gain"][0]), 2)
    evs[:, 1] = np.tile(f(inp["ev_k_gain"][0]), 2)
    evs[:, 2:6] = fm(f(inp["ev_conv_b"][0]))
    evs[:, 6:10] = fm(f(inp["ev_ln_g"][0]))
    evs[:, 10:14] = fm(f(inp["ev_ln_b"][0]))
    shared["ev_small"] = evs
    maps = []
    for core in range(8):
        m = dict(shared)
        xi = np.zeros((2, 128, 8, NT), np.float32)
        cv = np.zeros((128, 8, 3), np.float32)
        for q in range(2):
            bidx = 2 * core + q
            full = np.concatenate([ctx[bidx], x[bidx]], axis=0)
            xi[q] = np.transpose(full.reshape(NT, 8, 128), (2, 1, 0))
            cv[:, :, q] = fm(f(inp["c"][bidx]))
        cv[:, :, 2] = fm(f(inp["c_ctx"]))
        m["xin"] = xi
        m["cvec"] = cv
        maps.append(m)
    return maps


_CACHE = {}


def kernel(**inputs):
    maps = prep_inputs(inputs)
    if "nc" not in _CACHE:
        _CACHE["nc"] = build_program()[0]
    nc = _CACHE["nc"]
    res = run_bass_kernel_spmd(nc, maps, core_ids=list(range(8)))
    outp = np.zeros((16, 2048, 1024), np.float32)
    for core in range(8):
        o = res.results[core]["out"]
        for q in range(2):
            outp[2 * core + q] = np.transpose(o[q], (2, 1, 0)).reshape(2048, 1024)
    return outp
```

```python
from contextlib import ExitStack
import math
import numpy as np
import ml_dtypes
import concourse.bass as bass
import concourse.mybir as mybir
from concourse.bass_utils import run_bass_kernel_spmd

F32 = mybir.dt.float32
BF16 = mybir.dt.bfloat16
ALU = mybir.AluOpType
AF = mybir.ActivationFunctionType

N_DMA_SEMS = 40
NT = 2304
CH = [(0, 256), (256, 512), (768, 512), (1280, 512), (1792, 512)]
EPS = 1e-6


class Builder:
    def __init__(self):
        self.nc = bass.Bass("TRN2", target_bir_lowering=False)
        self.ops = []
        self.es = ExitStack()
        self._n = 0

    def dram(self, name, shape, dtype, kind):
        return self.nc.dram_tensor(name, list(shape), dtype, kind=kind).ap()

    def sbuf(self, shape, dtype, name=None):
        self._n += 1
        return self.es.enter_context(self.nc.sbuf_tensor(name or f"sb{self._n}", list(shape), dtype))

    def psum(self, shape, dtype, name=None):
        self._n += 1
        return self.es.enter_context(self.nc.psum_tensor(name or f"ps{self._n}", list(shape), dtype))

    def op(self, stream, fn, r=(), w=(), acc=False):
        self.ops.append((stream, "c", fn, tuple(r), tuple(w), acc))

    def dma(self, queue, out, in_, r=(), w=()):
        self.ops.append((queue, "d", (out, in_), tuple(r), tuple(w), False))

    def fence(self):
        self.ops.append((None, "f", None, (), (), False))

    def mm(self, out, lhsT, rhs, start, stop, r, w):
        self.op("pe", lambda e: e.matmul(out, lhsT, rhs, start=start, stop=stop), r=r, w=w, acc=not start)

    def tr(self, out, in_, ident, r, w):
        self.op("pe", lambda e: e.transpose(out, in_, ident), r=r, w=w)

    def act(self, out, in_, func, r, w, bias=None, scale=None):
        kw = {}
        if bias is not None:
            kw["bias"] = bias
        if scale is not None:
            kw["scale"] = scale
        self.op("act", lambda e: e.activation(out, in_, func, **kw), r=r, w=w)

    def tt(self, out, a, b, op, r, w, eng="dve"):
        self.op(eng, lambda e: e.tensor_tensor(out, a, b, op), r=r, w=w)

    def ts(self, out, a, s1, s2, op0, op1, r, w, eng="dve"):
        if s2 is None:
            self.op(eng, lambda e: e.tensor_scalar(out, a, s1, None, op0), r=r, w=w)
        else:
            self.op(eng, lambda e: e.tensor_scalar(out, a, s1, s2, op0, op1), r=r, w=w)

    def stt(self, out, a, s, b, op0, op1, r, w):
        self.op("dve", lambda e: e.scalar_tensor_tensor(out, a, s, b, op0, op1), r=r, w=w)

    def cp(self, out, in_, r, w, eng="dve"):
        self.op(eng, lambda e: e.tensor_copy(out, in_), r=r, w=w)

    def recip(self, out, in_, r, w):
        self.op("dve", lambda e: e.reciprocal(out, in_), r=r, w=w)

    def memset(self, ap, v, w, eng="dve"):
        self.op(eng, lambda e: e.memset(ap, v), w=w)

    def build(self):
        nc = self.nc
        ops = self.ops
        n = len(ops)
        streams = ["pe", "act", "dve", "pool", "sp"]
        last_w = {}
        readers = {}
        deps = [set() for _ in range(n)]
        last_on = {s: None for s in streams}
        open_dmas = []
        pending_fence = {s: None for s in streams}
        for i, (st, kind, fn, R, W, acc) in enumerate(ops):
            if kind == "f":
                fd = set(j for j in last_on.values() if j is not None) | set(open_dmas)
                for s in streams:
                    pending_fence[s] = (pending_fence[s] or set()) | fd
                open_dmas = []
                continue
            if pending_fence[st]:
                deps[i] |= pending_fence[st]
                pending_fence[st] = None
            for k in R:
                if k in last_w:
                    deps[i].add(last_w[k])
            for k in W:
                if k in last_w:
                    j = last_w[k]
                    if not ((acc or st == "pe") and ops[j][0] == st and ops[j][1] == "c" and kind == "c"):
                        deps[i].add(j)
                for rr in readers.get(k, ()):
                    deps[i].add(rr)
            for k in R:
                readers.setdefault(k, []).append(i)
            for k in W:
                last_w[k] = i
                readers[k] = []
            deps[i].discard(i)
            if kind == "c":
                last_on[st] = i
            else:
                open_dmas.append(i)
        dma_sem_of, dma_val_of = {}, {}
        sem_count = [0] * N_DMA_SEMS
        sem_last = [None] * N_DMA_SEMS
        qn = {"sp": 0, "pool": 0}
        NSP = 24
        for i in range(n):
            if ops[i][1] != "d":
                continue
            if ops[i][0] == "sp":
                s = qn["sp"] % NSP
            else:
                s = NSP + qn["pool"] % (N_DMA_SEMS - NSP)
            qn[ops[i][0]] += 1
            if sem_last[s] is not None:
                deps[i].add(sem_last[s])
            sem_last[s] = i
            sem_count[s] += 16
            dma_sem_of[i] = s
            dma_val_of[i] = sem_count[s]
        needed = [False] * n
        for i in range(n):
            for j in deps[i]:
                needed[j] = True
        cnt = {s: 0 for s in streams}
        sigval = {}
        for i, (st, kind, *_r) in enumerate(ops):
            if kind == "c" and needed[i]:
                cnt[st] += 1
                sigval[i] = cnt[st]
        es = self.es
        esem = {s: es.enter_context(nc.semaphore(f"sem_{s}")) for s in streams}
        dsem = [es.enter_context(nc.semaphore(f"dsem{k}")) for k in range(N_DMA_SEMS)]
        block = es.enter_context(nc.Block())

        def target(j):
            if ops[j][1] == "d":
                return ("d", dma_sem_of[j]), dsem[dma_sem_of[j]], dma_val_of[j]
            return ("c", ops[j][0]), esem[ops[j][0]], sigval[j]

        def emit_stream(st, eng):
            waited = {}
            for i, (s2, kind, fn, R, W, acc) in enumerate(ops):
                if s2 != st:
                    continue
                need = {}
                for j in deps[i]:
                    key, sem, val = target(j)
                    if waited.get(key, 0) >= val:
                        continue
                    if need.get(key, (None, 0))[1] < val:
                        need[key] = (sem, val)
                for key, (sem, val) in need.items():
                    eng.wait_ge(sem, val)
                    waited[key] = val
                if kind == "c":
                    ins = fn(eng)
                    if needed[i]:
                        ins.then_inc(esem[st], 1)
                else:
                    out, in_ = fn
                    eng.dma_start(out=out, in_=in_).then_inc(dsem[dma_sem_of[i]], 16)
            if st == "sp":
                for s in range(N_DMA_SEMS):
                    if sem_count[s] > 0 and waited.get(("d", s), 0) < sem_count[s]:
                        eng.wait_ge(dsem[s], sem_count[s])
                for s3 in streams:
                    if cnt[s3] > 0 and waited.get(("c", s3), 0) < cnt[s3]:
                        eng.wait_ge(esem[s3], cnt[s3])

        @block.tensor
        def _(e):
            emit_stream("pe", e)

        @block.scalar
        def _(e):
            emit_stream("act", e)

        @block.vector
        def _(e):
            emit_stream("dve", e)

        @block.gpsimd
        def _(e):
            emit_stream("pool", e)

        @block.sync
        def _(e):
            emit_stream("sp", e)

        es.close()
        return nc


ARENA_BYTES = 200704


class Prog:
    def __init__(self, taps=()):
        self.B = Builder()
        self.taps = set(taps)
        self.uid = 0
        B = self.B
        self.arena = B.sbuf([128, ARENA_BYTES // 4], F32, name="arena")
        self.ps = [B.psum([128, 512], F32, name=f"psb{i}") for i in range(8)]
        self.outs = {}

    def k(self, s):
        self.uid += 1
        return f"{s}#{self.uid}"

    def view(self, off, free_shape, dtype):
        esz = 4 if dtype == F32 else 2
        nel = int(np.prod(free_shape))
        nb = nel * esz
        assert off % 4 == 0 and nb % 4 == 0 and off + nb <= ARENA_BYTES, (off, nb)
        a = self.arena[:, off // 4:(off + nb) // 4]
        if dtype != F32:
            a = a.bitcast(dtype)
        if len(free_shape) == 2:
            a = a.rearrange("p (a b) -> p a b", b=free_shape[1])
        elif len(free_shape) == 3:
            a = a.rearrange("p (a b c) -> p a b c", b=free_shape[1], c=free_shape[2])
        return a


class Carver:
    def __init__(self, P, base, limit=ARENA_BYTES):
        self.P, self.off, self.limit = P, base, limit

    def get(self, free_shape, dtype):
        esz = 4 if dtype == F32 else 2
        nb = (int(np.prod(free_shape)) * esz + 3) // 4 * 4
        v = self.P.view(self.off, free_shape, dtype)
        self.off += nb
        assert self.off <= self.limit, (self.off, self.limit)
        return v


def build_program(taps=(), layers=(0, 1)):
    import os
    DO_HY = os.environ.get('NO_HY', '') == ''
    N_WARM = int(os.environ.get('N_WARM', '1'))
    P = Prog(taps)
    B = P.B
    ps = P.ps
    PSK = [f"psb{i}" for i in range(8)]

    def din(name, shape, dt=F32):
        return B.dram(name, shape, dt, "ExternalInput")

    xin = din("xin", [2, 128, 8, NT])
    cvec = din("cvec", [128, 8, 3])
    ada_w = din("ada_w", [2, 1024, 6144])
    abT = din("abT", [128, 2, 48])
    gvec = din("gvec", [128, 5, 8])
    consts = din("consts", [128, 10, 128])
    rope = din("rope", [128, 2, 2048])
    ev_w_in = din("ev_w_in", [1024, 2304])
    ev_small = din("ev_small", [128, 16])
    ev_cw = din("ev_cw", [128, 4, 31])
    ev_w_out = din("ev_w_out", [1280, 1024])
    ffn_w_up = din("ffn_w_up", [2, 1024, 5632])
    ffn_cw = din("ffn_cw", [128, 2, 44, 3])
    ffn_cb = din("ffn_cb", [128, 2, 44])
    ffn_w_down = din("ffn_w_down", [2, 2816, 1024])
    od_w_in = din("od_w_in", [1024, 4096])
    od_w_out = din("od_w_out", [1024, 1024])
    lbraw = din("lbraw", [128, 2, 2, 512])
    gnb_d = din("gnb", [128, 512])
    featsT = din("featsT", [33, 2048])
    hy_w1 = din("hy_w1", [33, 64])
    hy_w2 = din("hy_w2", [64, 64])
    hy_w3 = din("hy_w3", [64, 2048])
    hy_small = din("hy_small", [128, 8])
    hy_decay = din("hy_decay", [128, 16, 512])
    hy_sw = din("hy_sw", [128, 12, 3])
    hy_sb = din("hy_sb", [128, 12])
    hy_skip = din("hy_skip", [128, 2, 512])
    dftf = din("dftf", [17, 2, 128, 16, 128], BF16)
    dfti = din("dfti", [16, 128, 17, 2, 128], BF16)
    spec = B.dram("spec", [2, 17, 128, 2, 512], F32, "Internal")
    xsp = B.dram("xsp", [2, 128, 8, NT], F32, "Internal")
    ofs = B.dram("ofs", [2, 16, 64, 1024], F32, "Internal")
    out = B.dram("out", [2, 128, 8, 2048], F32, "ExternalOutput")
    tapd = {}
    for t in P.taps:
        if t in ("dlat", "clat"):
            tapd[t] = B.dram("tap_" + t, [2, 128, 4, 2048], F32, "ExternalOutput")
        else:
            tapd[t] = B.dram("tap_" + t, [2, 128, 8, NT], F32, "ExternalOutput")

    cst = B.sbuf([128, 10, 128], F32, name="cst")
    IDENT, ONES, BLK1, PSW, ESE, ESO, TRIF, TRIB, STRF, STRB = (cst[:, i, :] for i in range(10))
    B.dma("sp", cst[:], consts, w=["cst"])
    small = B.sbuf([128, 64], F32, name="small")
    B.memset(small[:, 0:1], EPS, w=["small"])
    epsc = small[:, 0:1]
    gv = B.sbuf([128, 5, 8], F32, name="gv")
    B.dma("sp", gv[:], gvec, w=["gv"])
    evs = B.sbuf([128, 16], F32, name="evs")
    B.dma("sp", evs[:], ev_small, w=["evs"])
    evcw = B.sbuf([128, 4, 31], F32, name="evcw")
    B.dma("sp", evcw[:], ev_cw, w=["evcw"])
    fcw = B.sbuf([128, 2, 44, 3], F32, name="fcw")
    B.dma("sp", fcw[:], ffn_cw, w=["fcw"])
    fcb = B.sbuf([128, 2, 44], F32, name="fcb")
    B.dma("sp", fcb[:], ffn_cb, w=["fcb"])
    identb = B.sbuf([128, 128], BF16, name="identb")
    B.cp(identb[:], IDENT, r=["cst"], w=["identb"])
    h2st = B.sbuf([128, 8], BF16, name="h2st")
    mod = B.sbuf([128, 2, 48, 3], F32, name="mod")
    mA = B.sbuf([128, 2, 2, 8, 3], F32, name="mA")

    def stage_mod():
        C = Carver(P, 0)
        cv = C.get([8, 3], F32)
        sc = C.get([8, 3], F32)
        ab = C.get([2, 48], F32)
        B.dma("sp", cv, cvec, w=["cv"])
        B.dma("sp", ab, abT, w=["ab"])
        B.act(sc, cv, AF.Silu, r=["cv"], w=["sc"])
        wsl = [C.get([8, 512], F32) for _ in range(2)]
        for l in range(2):
            awr = ada_w[l].rearrange("(kt p) f -> p kt f", p=128)
            for g in range(12):
                wb = wsl[g % 2]
                wk = f"adaw{g % 2}"
                B.dma("sp", wb, awr[:, :, g * 512:(g + 1) * 512], w=[wk])
                for jj in range(4):
                    j = g * 4 + jj
                    pb = j % 4
                    for kt in range(8):
                        B.mm(ps[pb][:, 0:3], wb[:, kt, jj * 128:(jj + 1) * 128], sc[:, kt, :],
                             kt == 0, kt == 7, r=[wk, "sc"], w=[PSK[pb]])
                    B.act(mod[:, l, j, :], ps[pb][:, 0:3], AF.Identity, r=[PSK[pb], "ab"], w=["mod"],
                          bias=ab[:, l, j:j + 1])
            for ni, (gi, mi) in enumerate(((l, 1), (2 + l, 4))):
                for col in range(3):
                    B.ts(mA[:, l, ni, :, col], mod[:, l, mi * 8:(mi + 1) * 8, col], 1.0, None, ALU.add, None,
                         r=["mod"], w=["mA"])
                    B.tt(mA[:, l, ni, :, col], mA[:, l, ni, :, col], gv[:, gi, :], ALU.mult,
                         r=["mA", "gv"], w=["mA"])
        B.fence()

    stage_mod()


    PI = float(np.pi)

    def stage_filter():
        B.fence()
        C = Carver(P, 0)
        HD = C.get([16, 4, 512], F32)
        AB = P.view(C.off, [16, 2, 512], BF16)
        fT = C.get([2048], F32)
        h1 = C.get([2048], F32)
        h2 = C.get([2048], F32)
        w3 = C.get([2048], F32)
        tmp = [C.get([512], F32) for _ in range(2)]
        rn = C.get([512], F32)
        slab = [C.get([16, 128], BF16) for _ in range(3)]
        spb = [C.get([2, 512], F32) for _ in range(2)]
        dct = [C.get([512], F32) for _ in range(2)]
        sm = B.sbuf([128, 8], F32, name="hysm")
        w1s = B.sbuf([128, 64], F32, name="hyw1")
        w2s = B.sbuf([128, 64], F32, name="hyw2")
        B.dma("sp", sm[:], hy_small, w=["hysm"])
        B.dma("sp", w1s[0:33, :], hy_w1, w=["hyw1"])
        B.dma("sp", w2s[0:64, :], hy_w2, w=["hyw2"])
        B.dma("sp", fT[0:33, :], featsT, w=["fT"])
        B.dma("sp", w3[0:64, :], hy_w3, w=["w3"])
        B.tt(sm[0:64, 3:4], sm[0:64, 0:1], sm[0:64, 1:2], ALU.mult, r=["hysm"], w=["hysm"])
        B.tt(sm[0:64, 4:5], sm[0:64, 2:3], sm[0:64, 1:2], ALU.mult, r=["hysm"], w=["hysm"])

        def sin_layer(wT, kin, src, dst, bcol, srck, dstk, wk):
            for c in range(4):
                cs = slice(c * 512, (c + 1) * 512)
                B.mm(ps[c % 2][0:64, :512], wT[0:kin, 0:64], src[0:kin, cs], True, True, r=[wk, srck], w=[PSK[c % 2]])
                y = dst[0:64, cs]
                B.act(y, ps[c % 2][0:64, :512], AF.Identity, r=[PSK[c % 2], "hysm"], w=[dstk],
                      bias=sm[0:64, bcol:bcol + 1], scale=sm[0:64, 1:2])
                t = tmp[c % 2][0:64, :]
                tk = f"ftmp{c % 2}"
                for _ in range(2):
                    B.ts(t, y, PI, -2 * PI, ALU.is_gt, ALU.mult, r=[dstk], w=[tk])
                    B.tt(y, y, t, ALU.add, r=[dstk, tk], w=[dstk])
                    B.ts(t, y, -PI, 2 * PI, ALU.is_lt, ALU.mult, r=[dstk], w=[tk])
                    B.tt(y, y, t, ALU.add, r=[dstk, tk], w=[dstk])
                B.ts(y, y, PI, -PI, ALU.min, ALU.max, r=[dstk], w=[dstk])
                B.act(y, y, AF.Sin, r=[dstk], w=[dstk])

        sin_layer(w1s, 33, fT, h1, 3, "fT", "h1", "hyw1")
        sin_layer(w2s, 64, h1, h2, 4, "h1", "h2", "hyw2")
        for tt_ in range(16):
            d_ = dct[tt_ % 2]
            dk_ = f"dct{tt_ % 2}"
            B.dma("sp", d_, hy_decay[:, tt_, :], w=[dk_])
            for g in range(4):
                pb = (tt_ * 4 + g) % 4
                B.mm(ps[pb][:, :512], h2[0:64, tt_ * 128:(tt_ + 1) * 128], w3[0:64, g * 512:(g + 1) * 512], True, True,
                     r=["h2", "w3"], w=[PSK[pb]])
                B.tt(HD[:, tt_, g, :], ps[pb][:, :512], d_, ALU.mult, r=[PSK[pb], dk_], w=["HD"])
        for g in (1, 3):
            B.memset(HD[0:1, 0, g, :], 0.0, w=["HD"])
        B.fence()
        for o in range(2):
            i = 0
            for tt_ in range(16):
                for dr in range(2):
                    t = tmp[i % 2]
                    tk = f"ftmp{i % 2}"
                    B.act(t, HD[:, tt_, o * 2 + dr, :], AF.Abs, r=["HD"], w=[tk])
                    B.mm(ps[4][:, :512], ONES, t, i == 0, i == 31, r=["cst", tk], w=[PSK[4]])
                    i += 1
            B.recip(rn, ps[4][:, :512], r=[PSK[4]], w=["rn"])
            for tt_ in range(16):
                t = tmp[tt_ % 2]
                tk = f"ftmp{tt_ % 2}"
                B.tt(t, HD[:, tt_, o * 2, :], HD[:, tt_, o * 2 + 1, :], ALU.add, r=["HD"], w=[tk])
                B.tt(AB[:, tt_, 0, :], t, rn, ALU.mult, r=[tk, "rn"], w=["AB"])
                B.tt(t, HD[:, tt_, o * 2, :], HD[:, tt_, o * 2 + 1, :], ALU.subtract, r=["HD"], w=[tk])
                B.tt(AB[:, tt_, 1, :], t, rn, ALU.mult, r=[tk, "rn"], w=["AB"])
            q = 0
            for ft in range(17):
                sb_ = spb[ft % 2]
                sk_ = f"spb{ft % 2}"
                for part in range(2):
                    sl, slk = slab[q % 3], f"fslab{q % 3}"
                    q += 1
                    B.dma("sp", sl, dftf[ft, part], w=[slk])
                    pb = 5 + (ft * 2 + part) % 3
                    for tt_ in range(16):
                        B.mm(ps[pb][:, :512], sl[:, tt_, :], AB[:, tt_, part, :], tt_ == 0, tt_ == 15,
                             r=[slk, "AB"], w=[PSK[pb]])
                    if part == 0:
                        B.act(sb_[:, part, :], ps[pb][:, :512], AF.Identity, r=[PSK[pb]], w=[sk_])
                    else:
                        B.cp(sb_[:, part, :], ps[pb][:, :512], r=[PSK[pb]], w=[sk_])
                B.dma("sp", spec[o, ft], sb_, r=[sk_], w=[f"spec{o}"])
        B.fence()

    if 1 in layers and DO_HY:
        stage_filter()

    def modv(l, m, k, col):
        return mod[:, l, m * 8 + k, col:col + 1]

    def norm_mod(src_fn, n, dst_fn, l, ni, col, C, srck, dstk):
        kk = f"nm@{C.off}:"
        sq = [C.get([512], F32) for _ in range(2)]
        rs = C.get([512], F32)
        tmp = [C.get([512], F32) for _ in range(2)]
        pb = 7
        for k in range(8):
            s = sq[k % 2]
            B.tt(s[:, :n], src_fn(k), src_fn(k), ALU.mult, r=srck, w=[kk + f"sq{k % 2}"])
            B.mm(ps[pb][:, :n], ONES, s[:, :n], k == 0, k == 7, r=[kk + f"sq{k % 2}", "cst"], w=[PSK[pb]])
        B.act(rs[:, :n], ps[pb][:, :n], AF.Sqrt, r=[PSK[pb], "small"], w=[kk + "rs"], bias=epsc, scale=1.0 / 1024)
        B.recip(rs[:, :n], rs[:, :n], r=[kk + "rs"], w=[kk + "rs"])
        mshift = 0 if ni == 0 else 3
        for k in range(8):
            t = tmp[k % 2]
            B.tt(t[:, :n], src_fn(k), rs[:, :n], ALU.mult, r=list(srck) + [kk + "rs"], w=[kk + f"t{k % 2}"])
            B.act(dst_fn(k), t[:, :n], AF.Identity, r=[kk + f"t{k % 2}", "mA", "mod"], w=dstk,
                  bias=modv(l, mshift, k, col), scale=mA[:, l, ni, k, col:col + 1])

    def ffn(l, b, X, lat_only):
        if lat_only:
            scs = [[(256, 1024, False, True)], [(1280, 1024, True, False)]]
        else:
            scs = [[(0, 256, False, False), (256, 896, False, True)], [(1152, 1152, True, False)]]
        wup = ffn_w_up[l].rearrange("(kt p) f -> p kt f", p=128)
        wdn = ffn_w_down[l].rearrange("(ft p) d -> p ft d", p=128)
        for si, segs in enumerate(scs):
            B.fence()
            C = Carver(P, 73728)
            t0 = segs[0][0] - (1 if segs[0][2] else 0)
            t1 = segs[-1][0] + segs[-1][1] + (1 if segs[-1][3] else 0)
            next_ = t1 - t0
            h2 = C.get([8, next_], BF16)
            ntok = sum(s[1] for s in segs)
            G = C.get([22, ntok], BF16)
            ubw = sum(s[1] + 2 for s in segs)
            ub = [C.get([2, ubw], BF16) for _ in range(2)]
            wsl = [C.get([8, 256], BF16) for _ in range(3)]
            dg = [C.get([6, 128], BF16) for _ in range(2)]
            wds = [C.get([22, 128], BF16) for _ in range(2)]
            sg = [C.get([512], F32) for _ in range(2)]
            h2k = P.k("h2")
            pos = t0
            if segs[0][2]:
                B.cp(h2[:, :, 0], h2st[:], r=["h2st"], w=[h2k])
                pos = t0 + 1
            while pos < t1:
                n = min(512, t1 - pos)
                if pos < 256 < pos + n:
                    n = 256 - pos
                col = 2 if pos < 256 else b
                Cn = Carver(P, C.off)
                norm_mod(lambda k, pos=pos, n=n: X[:, k, pos:pos + n], n,
                         lambda k, pos=pos, n=n: h2[:, k, pos - t0:pos - t0 + n], l, 1, col, Cn,
                         srck=["X"], dstk=[h2k])
                pos += n
            if si == 0:
                nxt = scs[1][0][0] - 1
                B.cp(h2st[:], h2[:, :, nxt - t0], r=[h2k], w=["h2st"])
            Gk = P.k("G")
            for j in range(22):
                wb = wsl[j % 3]
                wk = f"wup{j % 3}"
                B.dma("pool", wb[:, :, 0:128], wup[:, :, j * 128:(j + 1) * 128], w=[wk])
                B.dma("pool", wb[:, :, 128:256], wup[:, :, 2816 + j * 128:2816 + (j + 1) * 128], w=[wk])
                d = dg[j % 2]
                dk = f"dg{j % 2}"
                for vg in range(2):
                    for kk in range(3):
                        B.ts(d[:, vg * 3 + kk, :], IDENT, fcw[:, l, vg * 22 + j, kk:kk + 1], None, ALU.mult, None,
                             r=["cst", "fcw"], w=[dk])
                u = ub[j % 2]
                uk = f"ub{j % 2}"
                B.memset(u[:], 0.0, w=[uk], eng="pool")
                ucol = 0
                seg_ucol = []
                for (s0, sn, lh, rh) in segs:
                    seg_ucol.append(ucol)
                    a0 = s0 - (1 if lh else 0)
                    a1 = s0 + sn + (1 if rh else 0)
                    pos = a0
                    while pos < a1:
                        n = min(512, a1 - pos)
                        for vg in range(2):
                            pb = (2 * (pos // 512) + vg) % 4
                            for kt in range(8):
                                B.mm(ps[pb][:, :n], wb[:, kt, vg * 128:(vg + 1) * 128],
                                     h2[:, kt, pos - t0:pos - t0 + n], kt == 0, kt == 7,
                                     r=[wk, h2k], w=[PSK[pb]])
                            c0 = ucol + 1 + (pos - s0)
                            if vg == 0:
                                B.act(u[:, vg, c0:c0 + n], ps[pb][:, :n], AF.Identity, r=[PSK[pb]], w=[uk])
                            else:
                                B.cp(u[:, vg, c0:c0 + n], ps[pb][:, :n], r=[PSK[pb]], w=[uk])
                        pos += n
                    ucol += sn + 2
                gcol = 0
                for (s0, sn, lh, rh), uc in zip(segs, seg_ucol):
                    pos = 0
                    while pos < sn:
                        n = min(512, sn - pos)
                        pv, pg = 4 + (pos // 512) % 2 * 2, 5 + (pos // 512) % 2 * 2
                        for vg, pb in ((1, pg), (0, pv)):
                            for kk in range(3):
                                B.mm(ps[pb][:, :n], d[:, vg * 3 + kk, :], u[:, vg, uc + pos + kk:uc + pos + kk + n],
                                     kk == 0, kk == 2, r=[dk, uk], w=[PSK[pb]])
                        s = sg[(pos // 512) % 2]
                        sk = f"sg{(pos // 512) % 2}"
                        B.act(s[:, :n], ps[pg][:, :n], AF.Silu, r=[PSK[pg], "fcb"], w=[sk],
                              bias=fcb[:, l, 22 + j:23 + j], scale=1.0)
                        B.stt(G[:, j, gcol + pos:gcol + pos + n], ps[pv][:, :n], fcb[:, l, j:j + 1], s[:, :n],
                              ALU.add, ALU.mult, r=[PSK[pv], sk, "fcb"], w=[Gk])
                        pos += n
                    gcol += sn
            for dm in range(8):
                wd = wds[dm % 2]
                wdk = f"wdn{dm % 2}"
                B.dma("pool", wd, wdn[:, :, dm * 128:(dm + 1) * 128], w=[wdk])
                gcol = 0
                ci = 0
                for (s0, sn, lh, rh) in segs:
                    pos = 0
                    while pos < sn:
                        n = min(512, sn - pos)
                        pb = ci % 4
                        ci += 1
                        for ft in range(22):
                            B.mm(ps[pb][:, :n], wd[:, ft, :], G[:, ft, gcol + pos:gcol + pos + n],
                                 ft == 0, ft == 21, r=[wdk, Gk], w=[PSK[pb]])
                        col = 2 if s0 < 256 else b
                        xs = X[:, dm, s0 + pos:s0 + pos + n]
                        B.stt(xs, ps[pb][:, :n], modv(l, 5, dm, col), xs, ALU.mult, ALU.add,
                              r=[PSK[pb], "mod", "X"], w=["X"])
                        pos += n
                    gcol += sn
        B.fence()

    def u0col(tok):
        return tok + 15 if tok < 256 else tok + 45

    def layer0(b):
        l = 0
        B.fence()
        C = Carver(P, 0)
        QT = C.get([6, NT], BF16)
        KD = C.get([4, NT], BF16)
        VA = C.get([18, 4, 192], BF16)
        assert C.off <= 73728
        X = P.view(0, [8, NT], F32)
        C = Carver(P, 73728)
        H = C.get([8, NT], BF16)
        U0 = C.get([4, 2368], BF16)
        CO = C.get([4, NT], BF16)
        tbase = C.off
        xs = C.get([8, 512], F32)
        RC = C.get([2048], F32)
        RS = C.get([2048], F32)
        tmp_base = C.off
        B.dma("sp", RC, rope[:, 0, :], w=["RC"])
        B.dma("sp", RS, rope[:, 1, :], w=["RS"])
        B.memset(U0[:], 0.0, w=["U0"], eng="pool")
        B.memset(VA[:], 0.0, w=["VA"], eng="pool")
        B.memset(VA[:, :, :, 0:1], 1.0, w=["VA"], eng="pool")
        B.memset(VA[:, :, :, 128:129], 1.0, w=["VA"], eng="pool")
        for ci, (c0, n) in enumerate(CH):
            B.dma("sp", xs[:, :, :n], xin[b][:, :, c0:c0 + n], w=["xs"])
            Cn = Carver(P, tmp_base)
            norm_mod(lambda k, n=n: xs[:, k, :n], n, lambda k, c0=c0, n=n: H[:, k, c0:c0 + n], l, 0,
                     2 if ci == 0 else b, Cn, srck=["xs"], dstk=["H"])
        B.fence()
        Ct = Carver(P, tmp_base)
        wsl = [Ct.get([8, 128], BF16) for _ in range(3)]
        wv = P.view(tbase, [8, 256], BF16)
        qs_t = Ct.get([512], F32)
        q2_t = Ct.get([512], F32)
        rs_t = Ct.get([512], F32)
        t1_t = Ct.get([512], F32)
        t2_t = Ct.get([512], F32)
        win = ev_w_in.rearrange("(kt p) f -> p kt f", p=128)
        slab_i = [0]

        def load_slab(col_list):
            i = slab_i[0] % 3
            slab_i[0] += 1
            wb, wk = wsl[i], f"win{i}"
            wdt = 128 // len(col_list)
            for q, c in enumerate(col_list):
                B.dma("pool", wb[:, :, q * wdt:(q + 1) * wdt], win[:, :, c:c + wdt], w=[wk])
            return wb, wk

        def proj_chunk(wb, wk, c0, n, pb):
            for kt in range(8):
                B.mm(ps[pb][:, :n], wb[:, kt, :], H[:, kt, c0:c0 + n], kt == 0, kt == 7, r=[wk, "H"], w=[PSK[pb]])

        def qk_post(pb, ci, c0, n, gcol, dst, dstk):
            B.act(qs_t[:, :n], ps[pb][:, :n], AF.Identity, r=[PSK[pb], "evs"], w=["qs"], scale=evs[:, gcol:gcol + 1])
            B.act(q2_t[:, :n], ps[pb][:, :n], AF.Square, r=[PSK[pb]], w=["q2"])
            B.mm(ps[4][:, :n], BLK1, q2_t[:, :n], True, True, r=["cst", "q2"], w=[PSK[4]])
            if ci > 0:
                B.mm(ps[5][:, :n], PSW, qs_t[:, :n], True, True, r=["cst", "qs"], w=[PSK[5]])
            B.act(rs_t[:, :n], ps[4][:, :n], AF.Sqrt, r=[PSK[4], "small"], w=["rs"], bias=epsc, scale=1.0 / 64)
            B.recip(rs_t[:, :n], rs_t[:, :n], r=["rs"], w=["rs"])
            if ci > 0:
                r0 = c0 - 256
                B.tt(t1_t[:, :n], qs_t[:, :n], RC[:, r0:r0 + n], ALU.mult, r=["qs", "RC"], w=["t1"])
                B.tt(t2_t[:, :n], ps[5][:, :n], RS[:, r0:r0 + n], ALU.mult, r=[PSK[5], "RS"], w=["t2"])
                B.tt(t1_t[:, :n], t1_t[:, :n], t2_t[:, :n], ALU.add, r=["t1", "t2"], w=["t1"])
                B.tt(dst, t1_t[:, :n], rs_t[:, :n], ALU.mult, r=["t1", "rs"], w=dstk)
            else:
                B.tt(dst, qs_t[:, :n], rs_t[:, :n], ALU.mult, r=["qs", "rs"], w=dstk)

        for j in range(6):
            wb, wk = load_slab([j * 128])
            for ci, (c0, n) in enumerate(CH):
                pb = ci % 2
                proj_chunk(wb, wk, c0, n, pb)
                qk_post(pb, ci, c0, n, 0, QT[:, j, c0:c0 + n], ["QT"])
        for g in range(4):
            wb, wk = load_slab([768 + 64 * g, 768 + 64 * g])
            for ci, (c0, n) in enumerate(CH):
                pb = ci % 2
                proj_chunk(wb, wk, c0, n, pb)
                qk_post(pb, ci, c0, n, 1, KD[:, g, c0:c0 + n], ["KD"])
        B.dma("pool", wv, win[:, :, 1024:1280], w=["wv"])
        for tt_ in range(18):
            pb = tt_ % 2
            for kt in range(8):
                B.mm(ps[pb][:, 0:256], H[:, kt, tt_ * 128:(tt_ + 1) * 128], wv[:, kt, :], kt == 0, kt == 7,
                     r=["wv", "H"], w=[PSK[pb]])
            B.cp(VA[:, tt_, :, 64:128], ps[pb][:, 0:256].rearrange("p (g d) -> p g d", d=64), r=[PSK[pb]], w=["VA"])
        for j in range(4):
            wa, wak = load_slab([1280 + j * 128])
            wg, wgk = load_slab([1792 + j * 128])
            for ci, (c0, n) in enumerate(CH):
                pa, pg = 2 * (ci % 2), 2 * (ci % 2) + 1
                proj_chunk(wg, wgk, c0, n, pg)
                proj_chunk(wa, wak, c0, n, pa)
                B.act(qs_t[:, :n], ps[pg][:, :n], AF.Sigmoid, r=[PSK[pg]], w=["qs"])
                uc = u0col(c0)
                B.tt(U0[:, j, uc:uc + n], ps[pa][:, :n], qs_t[:, :n], ALU.mult, r=[PSK[pa], "qs"], w=["U0"])
        B.fence()
        Cd = Carver(P, tbase)
        DG = Cd.get([4, 31, 128], BF16)
        Ct = Carver(P, Cd.off)
        ub = Ct.get([4, 512], F32)
        u2 = [Ct.get([512], F32) for _ in range(2)]
        mean = Ct.get([512], F32)
        var = Ct.get([512], F32)
        dd = [Ct.get([512], F32) for _ in range(2)]
        for ct in range(4):
            for kk in range(31):
                B.ts(DG[:, ct, kk, :], IDENT, evcw[:, ct, kk:kk + 1], None, ALU.mult, None, r=["cst", "evcw"], w=["DG"])
        for ci, (c0, n) in enumerate(CH):
            uc = u0col(c0) - 15
            for ct in range(4):
                pb = ct % 2
                for kk in range(31):
                    B.mm(ps[pb][:, :n], DG[:, ct, kk, :], U0[:, ct, uc + kk:uc + kk + n], kk == 0, kk == 30,
                         r=["DG", "U0"], w=[PSK[pb]])
                B.act(ub[:, ct, :n], ps[pb][:, :n], AF.Identity, r=[PSK[pb], "evs"], w=["ub"],
                      bias=evs[:, 2 + ct:3 + ct], scale=1.0)
                B.tt(u2[ct % 2][:, :n], ub[:, ct, :n], ub[:, ct, :n], ALU.mult, r=["ub"], w=[f"u2{ct % 2}"])
                B.mm(ps[2][:, :n], ONES, ub[:, ct, :n], ct == 0, ct == 3, r=["cst", "ub"], w=[PSK[2]])
                B.mm(ps[3][:, :n], ONES, u2[ct % 2][:, :n], ct == 0, ct == 3, r=["cst", f"u2{ct % 2}"], w=[PSK[3]])
            B.act(mean[:, :n], ps[2][:, :n], AF.Identity, r=[PSK[2]], w=["mean"], scale=1.0 / 512)
            B.tt(var[:, :n], mean[:, :n], mean[:, :n], ALU.mult, r=["mean"], w=["var"])
            B.stt(var[:, :n], ps[3][:, :n], 1.0 / 512, var[:, :n], ALU.mult, ALU.subtract, r=[PSK[3], "var"], w=["var"])
            B.act(var[:, :n], var[:, :n], AF.Sqrt, r=["var", "small"], w=["var"], bias=epsc, scale=1.0)
            B.recip(var[:, :n], var[:, :n], r=["var"], w=["var"])
            for ct in range(4):
                d = dd[ct % 2]
                B.tt(d[:, :n], ub[:, ct, :n], mean[:, :n], ALU.subtract, r=["ub", "mean"], w=[f"dd{ct % 2}"])
                B.tt(d[:, :n], d[:, :n], var[:, :n], ALU.mult, r=[f"dd{ct % 2}", "var"], w=[f"dd{ct % 2}"])
                B.act(CO[:, ct, c0:c0 + n], d[:, :n], AF.Silu, r=[f"dd{ct % 2}", "evs"], w=["CO"],
                      bias=evs[:, 10 + ct:11 + ct], scale=evs[:, 6 + ct:7 + ct])
        B.fence()
        Ct = Carver(P, tbase)
        PT = [Ct.get([512], BF16) for _ in range(4)]
        OS = [Ct.get([512], F32) for _ in range(4)]
        ui = 0
        pending = []
        for j in range(6):
            hA, hB = 2 * j, 2 * j + 1
            gs = (hA // 3, hB // 3)
            for ci, (c0, n) in enumerate(CH):
                nkt = 2 if ci == 0 else 18
                ob = 0
                osi = 2 * (ui % 2)
                Os = (OS[osi], OS[osi + 1])
                oks = (f"OS{osi}", f"OS{osi + 1}")
                ui += 1

                def s_pair(kt, n=n, c0=c0, j=j, gs=gs):
                    bk = 4 + 2 * (kt % 2)
                    for _ in range(N_WARM):
                        B.mm(ps[2][:, :n], KD[:, gs[0], kt * 128:(kt + 1) * 128], QT[:, j, c0:c0 + n], True, True,
                             r=["KD", "QT"], w=[PSK[2], PSK[bk], PSK[bk + 1]])
                    for q in range(2):
                        off = 64 * q
                        wk = [PSK[bk], PSK[bk + 1]] if q == 0 else [PSK[bk + 1]]
                        B.mm(ps[bk + q][:, :n], KD[off:off + 64, gs[q], kt * 128:(kt + 1) * 128],
                             QT[off:off + 64, j, c0:c0 + n], True, True, r=["KD", "QT"], w=wk)

                def e_pair(kt, n=n):
                    bk = 4 + 2 * (kt % 2)
                    for q in range(2):
                        pi = 2 * (kt % 2) + q
                        B.act(PT[pi][:, :n], ps[bk + q][:, :n], AF.Exp, r=[PSK[bk + q]], w=[f"PT{pi}"], scale=0.125)

                def pv_pair(kt, n=n, gs=gs, ob=ob, nkt=nkt):
                    for q in range(2):
                        pi = 2 * (kt % 2) + q
                        rk = [f"PT{pi}", f"PT{pi + 1}"] if q == 0 else [f"PT{pi}"]
                        lhs = VA[:, kt, gs[q], 64:192] if q == 0 else VA[:, kt, gs[q], 0:128]
                        B.mm(ps[ob + q][:, :n], lhs, PT[pi][:, :n], kt == 0, kt == nkt - 1, r=["VA"] + rk,
                             w=[PSK[ob + q]])

                def evac(n=n, ob=ob, Os=Os, oks=oks):
                    for q in range(2):
                        O, ok, po = Os[q], oks[q], ob + q
                        if q == 0:
                            B.act(O[:, :n], ps[po][:, :n], AF.Identity, r=[PSK[po]], w=[ok])
                        else:
                            B.cp(O[:, :n], ps[po][:, :n], r=[PSK[po]], w=[ok])
                        dr = 64 if q == 0 else 0
                        B.recip(O[dr:dr + 1, :n], O[dr:dr + 1, :n], r=[ok], w=[ok])

                def epilogue(n=n, c0=c0, Os=Os, oks=oks, j=j):
                    for q in range(2):
                        O, ok, off = Os[q], oks[q], 64 * q
                        if q == 0:
                            B.mm(ps[3][0:64, :n], ESE[0:65, 0:64], O[0:65, :n], True, True, r=["cst", ok], w=[PSK[3]])
                        else:
                            B.mm(ps[3][:, :n], ESO, O[:, :n], True, True, r=["cst", ok], w=[PSK[3]])
                        B.tt(H[off:off + 64, j, c0:c0 + n], O[off:off + 64, :n], ps[3][off:off + 64, :n], ALU.mult,
                             r=[ok, PSK[3]], w=["H"])

                s_pair(0)
                e_pair(0)
                for kt in range(nkt):
                    if kt + 1 < nkt:
                        s_pair(kt + 1)
                        e_pair(kt + 1)
                    if kt == min(3, nkt - 1):
                        for ep in pending:
                            ep()
                        pending = []
                    pv_pair(kt)
                evac()
                pending.append(epilogue)
        for ep in pending:
            ep()
        B.fence()
        Ct = Carver(P, tbase)
        wos = [Ct.get([10, 128], BF16) for _ in range(2)]
        wo = ev_w_out.rearrange("(ft p) d -> p ft d", p=128)
        for dm in range(8):
            w_, wk = wos[dm % 2], f"wo{dm % 2}"
            B.dma("pool", w_, wo[:, :, dm * 128:(dm + 1) * 128], w=[wk])
            B.dma("sp", X[:, dm, :], xin[b][:, dm, :], w=["X"])
            for ci, (c0, n) in enumerate(CH):
                pb = ci % 4
                for ft in range(10):
                    rhs = H[:, ft, c0:c0 + n] if ft < 6 else CO[:, ft - 6, c0:c0 + n]
                    B.mm(ps[pb][:, :n], w_[:, ft, :], rhs, ft == 0, ft == 9, r=[wk, "H", "CO"], w=[PSK[pb]])
                col = 2 if ci == 0 else b
                xs_ = X[:, dm, c0:c0 + n]
                B.stt(xs_, ps[pb][:, :n], modv(l, 2, dm, col), xs_, ALU.mult, ALU.add,
                      r=[PSK[pb], "mod", "X"], w=["X"])
        B.fence()
        if "xmid0" in P.taps:
            B.dma("sp", tapd["xmid0"][b], X, r=["X"], w=[P.k("tap")])
        ffn(0, b, X, lat_only=False)
        if "x0" in P.taps:
            B.dma("sp", tapd["x0"][b], X, r=["X"], w=[P.k("tap")])
        return X


    AX = mybir.AxisListType

    def sub(pk, hs=range(4)):
        return [pk]

    def layer1(b, X, do_hyena=True):
        l = 1
        B.fence()
        B.dma("sp", xsp[b], X, r=["X"], w=["xsp"])
        H = P.view(73728, [8, NT], BF16)
        for ci, (c0, n) in enumerate(CH):
            Cn = Carver(P, 161792)
            norm_mod(lambda k, c0=c0, n=n: X[:, k, c0:c0 + n], n, lambda k, c0=c0, n=n: H[:, k, c0:c0 + n], l, 0,
                     2 if ci == 0 else b, Cn, srck=["X"], dstk=["H"])
        B.fence()
        PT = P.view(0, [12, 2050], BF16)
        QF = P.view(49200, [4, NT], BF16)
        MIX = P.view(110592, [8, 2048], BF16)
        VT = P.view(143360, [18, 512], BF16)
        win = od_w_in.rearrange("(kt p) f -> p kt f", p=128)
        Ct = Carver(P, 161792)
        wsl = [Ct.get([8, 128], BF16) for _ in range(3)]
        B.memset(PT[:, :, 0:1], 0.0, w=["PT"])
        B.memset(PT[:, :, 2049:2050], 0.0, w=["PT"])
        for j in range(16):
            wb, wk = wsl[j % 3], f"w1in{j % 3}"
            B.dma("pool", wb, win[:, :, j * 128:(j + 1) * 128], w=[wk])
            for ci, (c0, n) in enumerate(CH):
                if j < 12 and ci == 0:
                    continue
                pb = ci % 4
                for kt in range(8):
                    B.mm(ps[pb][:, :n], wb[:, kt, :], H[:, kt, c0:c0 + n], kt == 0, kt == 7, r=[wk, "H"], w=[PSK[pb]])
                if j < 12:
                    dst, dk = PT[:, j, 1 + c0 - 256:1 + c0 - 256 + n], "PT"
                else:
                    dst, dk = QF[:, j - 12, c0:c0 + n], "QF"
                if ci % 2 == 0:
                    B.act(dst, ps[pb][:, :n], AF.Identity, r=[PSK[pb]], w=[dk])
                else:
                    B.cp(dst, ps[pb][:, :n], r=[PSK[pb]], w=[dk])
        B.fence()
        Ct = Carver(P, 161792)
        Wf = Ct.get([8, 512], BF16)
        Wi = Ct.get([8, 512], BF16)
        fr = Ct.get([512], F32)
        lg = Ct.get([512], F32)
        kk_ = Ct.get([512], F32)
        e1 = Ct.get([512], F32)
        kd = Ct.get([512], F32)
        ebT = Ct.get([4, 128], F32)
        kdec = Ct.get([512], BF16)
        qdT = Ct.get([4, 128], BF16)
        kdT = Ct.get([512], BF16)
        AT = Ct.get([4, 128], BF16)
        S32 = Ct.get([4, 128], F32)
        S16 = Ct.get([4, 128], BF16)
        dec = Ct.get([4, 2], F32)
        Cx = Carver(P, 110592, 110592 + 16384)
        lbv = Cx.get([2, 512], F32)
        oml = Cx.get([2, 512], F32)
        osb = Cx.get([2, 512], F32)
        ofl = Cx.get([2, 512], F32)
        Cy = Carver(P, 49200 + 18432, 73728)
        ss = Cy.get([8], F32)
        MSK = Cy.get([2, 4, 128], F32)
        gnb = Ct.get([512], F32)
        B.dma("sp", lbv, lbraw[:, 0], w=["lbv"])
        B.dma("sp", oml, lbraw[:, 1], w=["oml"])
        B.dma("sp", gnb, gnb_d, w=["gnb"])
        B.act(lbv, lbv, AF.Exp, r=["lbv"], w=["lbv"])
        B.act(oml, oml, AF.Exp, r=["oml"], w=["oml"])
        B.tt(lbv, lbv, oml, ALU.add, r=["lbv", "oml"], w=["lbv"])
        B.recip(lbv, lbv, r=["lbv"], w=["lbv"])
        B.tt(lbv, lbv, oml, ALU.mult, r=["lbv", "oml"], w=["lbv"])
        B.ts(oml, lbv, -1.0, 1.0, ALU.mult, ALU.add, r=["lbv"], w=["oml"])
        for hd in range(4):
            B.cp(MSK[:, 0, hd, :], TRIF, r=["cst"], w=["MSK"])
            B.cp(MSK[:, 1, hd, :], TRIB, r=["cst"], w=["MSK"])

        def wload(dst, col0, key):
            B.dma("pool", dst, win[:, :, col0:col0 + 512], w=[key])

        def hg_tile(tt_, d):
            lat = tt_ >= 2
            tl = tt_ - 2
            tsl = slice(tt_ * 128, (tt_ + 1) * 128)
            for kt in range(8):
                B.mm(ps[0][:, :512], H[:, kt, tsl], Wf[:, kt, :], kt == 0, kt == 7, r=["H", "Wf"], w=[PSK[0]])
            B.act(fr, ps[0][:, :512], AF.Sigmoid, r=[PSK[0]], w=["fr"])
            B.tt(fr, fr, oml[:, d, :], ALU.mult, r=["fr", "oml"], w=["fr"])
            B.tt(fr, fr, lbv[:, d, :], ALU.add, r=["fr", "lbv"], w=["fr"])
            B.act(lg, fr, AF.Ln, r=["fr"], w=["lg"])
            B.ts(kk_, fr, -1.0, 1.0, ALU.mult, ALU.add, r=["fr"], w=["kk"])
            if d == 0:
                for kt in range(8):
                    B.mm(ps[1][:, :512], H[:, kt, tsl], Wi[:, kt, :], kt == 0, kt == 7, r=["H", "Wi"], w=sub(PSK[1]))
                B.act(VT[:, tt_, :], ps[1][:, :512], AF.Identity, r=sub(PSK[1]), w=["VT"])
            TRI, STR = (TRIF, STRF) if d == 0 else (TRIB, STRB)
            B.mm(ps[2][:, :512], TRI, lg, True, True, r=["cst", "lg"], w=sub(PSK[2]))
            B.mm(ps[3][:, :512], STR, lg, True, True, r=["cst", "lg"], w=sub(PSK[3]))
            B.act(e1, ps[2][:, :512], AF.Exp, r=sub(PSK[2]), w=["e1"], scale=-1.0)
            B.tt(kd, kk_, e1, ALU.mult, r=["kk", "e1"], w=["kd"])
            B.act(e1, ps[3][:, :512], AF.Exp, r=sub(PSK[3]), w=["e1"])
            B.tt(kdec, kk_, e1, ALU.mult, r=["kk", "e1"], w=["kdec"])
            for hd in range(4):
                hs = slice(hd * 128, (hd + 1) * 128)
                B.mm(ps[4][:, hs], lg[:, hs], TRI, True, True, r=["cst", "lg"], w=[PSK[4]])
            p4 = ps[4][:, :512].rearrange("p (h t) -> p h t", t=128)
            B.act(ebT, p4, AF.Exp, r=sub(PSK[4]), w=["ebT"])
            B.tt(qdT, QF[:, :, tsl], ebT, ALU.mult, r=["QF", "ebT"], w=["qdT"])
            cols = (63, 127) if d == 0 else (0, 64)
            for c in range(2):
                B.act(dec[:, :, c:c + 1], p4[:, :, cols[c]:cols[c] + 1], AF.Exp, r=sub(PSK[4]), w=["dec"])
            for hd in range(4):
                hs = slice(hd * 128, (hd + 1) * 128)
                B.tr(ps[5][:, hs], kd[:, hs], IDENT, r=["kd", "cst"], w=[PSK[5]])
            B.act(kdT, ps[5][:, :512], AF.Identity, r=sub(PSK[5]), w=["kdT"])
            SK = os.environ.get("HG_SKIP", "")
            if lat and "a" not in SK:
                for hd in range(4):
                    hs = slice(hd * 128, (hd + 1) * 128)
                    B.mm(ps[6][:, hs], kdT[:, hs], qdT[:, hd, :], True, True, r=["kdT", "qdT"], w=[PSK[6]])
                B.tt(AT, ps[6][:, :512].rearrange("p (h t) -> p h t", t=128), MSK[:, d], ALU.mult,
                     r=sub(PSK[6]) + ["MSK"], w=["AT"])
            corder = (0, 1) if d == 0 else (1, 0)
            for qi, c in enumerate(corder):
                cs = slice(c * 64, (c + 1) * 64)
                po = 7 if c == 0 else 1
                if lat and "o" not in SK:
                    for hd in range(4):
                        hs = slice(hd * 128, (hd + 1) * 128)
                        B.mm(ps[po][0:64, hs], AT[:, hd, cs], VT[:, tt_, hs], True, False, r=["AT", "VT"],
                             w=[PSK[po]])
                        B.mm(ps[po][0:64, hs], qdT[:, hd, cs], S16[:, hd, :], False, True, r=["qdT", f"S16:{hd}"],
                             w=[PSK[po]])
                pu = 2 + qi
                for hd in range(4):
                    hs = slice(hd * 128, (hd + 1) * 128)
                    B.mm(ps[pu][:, hs], kdec[cs, hs], VT[cs, tt_, hs], True, True, r=["kdec", "VT"],
                         w=[PSK[pu]])
                for hd in range(4):
                    hs = slice(hd * 128, (hd + 1) * 128)
                    B.stt(S32[:, hd, :], S32[:, hd, :], dec[:, hd, c:c + 1], ps[pu][:, hs], ALU.mult, ALU.add,
                          r=[f"S32:{hd}", "dec", PSK[pu]], w=[f"S32:{hd}"])
                    B.act(S16[:, hd, :], S32[:, hd, :], AF.Identity, r=[f"S32:{hd}"], w=[f"S16:{hd}"])
            if not lat or "r" in SK:
                return
            if d == 0:
                B.act(osb[0:64, 0, :], ps[7][0:64, :512], AF.Identity, r=sub(PSK[7]), w=["osb"])
                B.cp(osb[0:64, 1, :], ps[1][0:64, :512], r=sub(PSK[1]), w=["osb"])
                B.dma("sp", ofs[b, tl].rearrange("p (c f) -> p c f", c=2), osb[0:64], r=["osb"], w=[f"ofs{tl}"])
                return
            B.dma("sp", ofl[0:64], ofs[b, tl].rearrange("p (c f) -> p c f", c=2), r=[f"ofs{tl}"], w=["ofl"])
            B.tt(osb[0:64, 0, :], ofl[0:64, 0, :], ps[7][0:64, :512], ALU.add, r=["ofl"] + sub(PSK[7]), w=["osb"])
            B.tt(osb[0:64, 1, :], ofl[0:64, 1, :], ps[1][0:64, :512], ALU.add, r=["ofl"] + sub(PSK[1]), w=["osb"])
            B.tt(ofl[0:64], osb[0:64], osb[0:64], ALU.mult, r=["osb"], w=["ofl"])
            B.op("dve", lambda e: e.tensor_reduce(ss[0:64, :], ofl[0:64].rearrange("p c (h e) -> p (c h) e", e=128),
                                                  AX.X, ALU.add), r=["ofl"], w=["ss"])
            B.act(ss[0:64, :], ss[0:64, :], AF.Sqrt, r=["ss", "small"], w=["ss"], bias=epsc[0:64], scale=1.0 / 128)
            B.recip(ss[0:64, :], ss[0:64, :], r=["ss"], w=["ss"])
            for c in range(2):
                for hd in range(4):
                    hs = slice(hd * 128, (hd + 1) * 128)
                    B.stt(osb[0:64, c, hs], osb[0:64, c, hs], ss[0:64, c * 4 + hd:c * 4 + hd + 1], gnb[0:64, hs],
                          ALU.mult, ALU.mult, r=["osb", "ss", "gnb"], w=["osb"])
            for c in range(2):
                pg = 4 + c
                for kt in range(8):
                    B.mm(ps[pg][0:64, :512], H[:, kt, tt_ * 128 + c * 64:tt_ * 128 + (c + 1) * 64], Wi[:, kt, :],
                         kt == 0, kt == 7, r=["H", "Wi"], w=sub(PSK[pg]))
                B.act(ofl[0:64, c, :], ps[pg][0:64, :512], AF.Silu, r=sub(PSK[pg]), w=["ofl"])
            B.tt(osb[0:64], osb[0:64], ofl[0:64], ALU.mult, r=["osb", "ofl"], w=["osb"])
            for c in range(2):
                for hd in range(4):
                    B.tr(ps[6][:, hd * 128 + c * 64:hd * 128 + (c + 1) * 64], osb[0:64, c, hd * 128:(hd + 1) * 128],
                         cst[0:64, 0, 0:64], r=["osb", "cst"], w=[PSK[6]])
            B.act(MIX[:, 4:8, tl * 128:(tl + 1) * 128], ps[6][:, :512].rearrange("p (h t) -> p h t", t=128),
                  AF.Identity, r=sub(PSK[6]), w=["MIX"])

        for d in range(int(os.environ.get("HG_ND", "2"))):
            wload(Wf, 1536 + 512 + d * 512, "Wf")
            wload(Wi, 1536 + (1536 if d == 0 else 2048), "Wi")
            for hd in range(4):
                B.memset(S32[:, hd, :], 0.0, w=[f"S32:{hd}"])
                B.memset(S16[:, hd, :], 0.0, w=[f"S16:{hd}"])
            order = list(range(18)) if d == 0 else [1, 0] + list(range(17, 1, -1))
            order = order[:int(os.environ.get("HG_NT", "18"))]
            for tt_ in order:
                hg_tile(tt_, d)
        B.fence()
        if "dlat" in P.taps:
            B.dma("pool", tapd["dlat"][b], MIX[:, 4:8, :], r=["MIX"], w=[P.k("tap")])

        if DO_HY:
            VX = P.view(143360, [3, 16, 512], BF16)
            Z2 = P.view(73728, [16, 512], BF16)
            Ct = Carver(P, 73728 + 16384, 110592)
            dgs = [Ct.get([3, 128], BF16) for _ in range(2)]
            uT = [Ct.get([512], F32) for _ in range(2)]
            hsw = Ct.get([12, 3], F32)
            hsb = Ct.get([12], F32)
            skb = Ct.get([2, 512], F32)
            B.dma("sp", hsw, hy_sw, w=["hsw"])
            B.dma("sp", hsb, hy_sb, w=["hsb"])
            B.dma("sp", skb, hy_skip, w=["skb"])
            for ct in range(12):
                dg_, dgk = dgs[ct % 2], f"hdg{ct % 2}"
                for kk in range(3):
                    B.ts(dg_[:, kk, :], IDENT, hsw[:, ct, kk:kk + 1], None, ALU.mult, None, r=["cst", "hsw"], w=[dgk])
                for c in range(4):
                    pb = c % 2
                    u_, uk = uT[c % 2], f"uT{c % 2}"
                    for kk in range(3):
                        B.mm(ps[pb][:, :512], dg_[:, kk, :], PT[:, ct, c * 512 + kk:c * 512 + kk + 512], kk == 0, kk == 2,
                             r=[dgk, "PT"], w=[PSK[pb]])
                    B.act(u_, ps[pb][:, :512], AF.Identity, r=[PSK[pb], "hsb"], w=[uk], bias=hsb[:, ct:ct + 1], scale=1.0)
                    pt_ = 2 + c % 2
                    for q in range(4):
                        B.tr(ps[pt_][:, q * 128:(q + 1) * 128], u_[:, q * 128:(q + 1) * 128], IDENT, r=[uk, "cst"],
                             w=[PSK[pt_]])
                    dst = VX[:, ct // 4, c * 4:(c + 1) * 4, (ct % 4) * 128:(ct % 4 + 1) * 128]
                    B.cp(dst, ps[pt_][:, :512].rearrange("p (q c) -> p q c", c=128), r=[PSK[pt_]], w=["VX"])
            B.fence()
            Ct = Carver(P, 0, 73728)
            Y = Ct.get([17, 2, 512], BF16)
            fsl = [Ct.get([16, 128], BF16) for _ in range(3)]
            isl = [Ct.get([17, 2, 128], BF16) for _ in range(2)]
            Ct2 = Carver(P, 73728 + 16384, 110592)
            skb = Ct2.get([2, 512], F32)
            spt = [Ct2.get([2, 512], F32) for _ in range(2)]
            t1s = [Ct2.get([512], F32) for _ in range(2)]
            B.dma("sp", skb, hy_skip, w=["skb2"])
            for o in range(2):
                zin = VX[:, 0] if o == 0 else Z2
                zk = "VX" if o == 0 else "Z2"
                q = 0
                for ft in range(17):
                    sp_, spk = spt[ft % 2], f"spt{ft % 2}"
                    B.dma("sp", sp_, spec[o, ft], r=[f"spec{o}"], w=[spk])
                    pz = [4 + 2 * (ft % 2), 5 + 2 * (ft % 2)]
                    for part in range(2):
                        sl, slk = fsl[q % 3], f"hfsl{q % 3}"
                        q += 1
                        B.dma("sp", sl, dftf[ft, part], w=[slk])
                        for tt_ in range(16):
                            B.mm(ps[pz[part]][:, :512], sl[:, tt_, :], zin[:, tt_, :], tt_ == 0, tt_ == 15, r=[slk, zk],
                                 w=[PSK[pz[part]]])
                    a, bq = t1s
                    B.tt(a, ps[pz[0]][:, :512], sp_[:, 0, :], ALU.mult, r=[PSK[pz[0]], spk], w=["t1a"])
                    B.tt(bq, ps[pz[1]][:, :512], sp_[:, 1, :], ALU.mult, r=[PSK[pz[1]], spk], w=["t1b"])
                    B.tt(Y[:, ft, 0, :], a, bq, ALU.subtract, r=["t1a", "t1b"], w=["Y"])
                    B.tt(a, ps[pz[0]][:, :512], sp_[:, 1, :], ALU.mult, r=[PSK[pz[0]], spk], w=["t1a"])
                    B.tt(bq, ps[pz[1]][:, :512], sp_[:, 0, :], ALU.mult, r=[PSK[pz[1]], spk], w=["t1b"])
                    B.tt(Y[:, ft, 1, :], a, bq, ALU.add, r=["t1a", "t1b"], w=["Y"])
                for tt_ in range(16):
                    il, ilk = isl[tt_ % 2], f"hisl{tt_ % 2}"
                    B.dma("sp", il, dfti[tt_], w=[ilk])
                    pb = tt_ % 2
                    i = 0
                    for ft in range(17):
                        for part in range(2):
                            B.mm(ps[pb][:, :512], il[:, ft, part, :], Y[:, ft, part, :], i == 0, i == 33, r=[ilk, "Y"],
                                 w=[PSK[pb]])
                            i += 1
                    a = t1s[tt_ % 2]
                    ak = f"t1{'ab'[tt_ % 2]}"
                    B.tt(a, zin[:, tt_, :], skb[:, o, :], ALU.mult, r=[zk, "skb2"], w=[ak])
                    B.tt(a, a, ps[pb][:, :512], ALU.add, r=[ak, PSK[pb]], w=[ak])
                    if o == 0:
                        B.tt(Z2[:, tt_, :], a, VX[:, 1, tt_, :], ALU.mult, r=[ak, "VX"], w=["Z2"])
                    else:
                        B.tt(a, a, VX[:, 2, tt_, :], ALU.mult, r=[ak, "VX"], w=[ak])
                        pt_ = 2 + tt_ % 2
                        for q4 in range(4):
                            B.tr(ps[pt_][:, q4 * 128:(q4 + 1) * 128], a[:, q4 * 128:(q4 + 1) * 128], IDENT, r=[ak, "cst"],
                                 w=[PSK[pt_]])
                        B.act(MIX[:, 0:4, tt_ * 128:(tt_ + 1) * 128],
                              ps[pt_][:, :512].rearrange("p (h t) -> p h t", t=128), AF.Identity, r=[PSK[pt_]], w=["MIX"])
            B.fence()
            if "clat" in P.taps:
                B.dma("pool", tapd["clat"][b], MIX[:, 0:4, :], r=["MIX"], w=[P.k("tap")])
        else:
            B.memset(MIX[:, 0:4, :], 0.0, w=["MIX"])
            B.fence()
        wos = [P.view(143360 + i * 2048, [8, 128], BF16) for i in range(2)]
        wo = od_w_out.rearrange("(ft p) d -> p ft d", p=128)
        for dm in range(8):
            w_, wk = wos[dm % 2], f"wo1{dm % 2}"
            B.dma("pool", w_, wo[:, :, dm * 128:(dm + 1) * 128], w=[wk])
            B.dma("sp", X[:, dm, 256:NT], xsp[b][:, dm, 256:NT], r=["xsp"], w=["X"])
            for c in range(4):
                pb = c % 4
                for ft in range(8):
                    B.mm(ps[pb][:, :512], w_[:, ft, :], MIX[:, ft, c * 512:(c + 1) * 512], ft == 0, ft == 7,
                         r=[wk, "MIX"], w=[PSK[pb]])
                xs_ = X[:, dm, 256 + c * 512:256 + (c + 1) * 512]
                B.stt(xs_, ps[pb][:, :512], modv(l, 2, dm, b), xs_, ALU.mult, ALU.add, r=[PSK[pb], "mod", "X"], w=["X"])
        B.fence()
        if "xmid1" in P.taps:
            B.dma("sp", tapd["xmid1"][b], X, r=["X"], w=[P.k("tap")])
        ffn(1, b, X, lat_only=True)
        return X

    def final(b, X):
        B.fence()
        C = Carver(P, 73728)
        ob = [C.get([8, 512], F32) for _ in range(2)]
        sq = [C.get([512], F32) for _ in range(2)]
        rs = C.get([512], F32)
        for ci in range(4):
            c0 = 256 + ci * 512
            n = 512
            o_, okk = ob[ci % 2], f"ob{ci % 2}"
            for k in range(8):
                s = sq[k % 2]
                B.tt(s[:, :n], X[:, k, c0:c0 + n], X[:, k, c0:c0 + n], ALU.mult, r=["X"], w=[f"fsq{k % 2}"])
                B.mm(ps[7][:, :n], ONES, s[:, :n], k == 0, k == 7, r=[f"fsq{k % 2}", "cst"], w=[PSK[7]])
            B.act(rs[:, :n], ps[7][:, :n], AF.Sqrt, r=[PSK[7], "small"], w=["frs"], bias=epsc, scale=1.0 / 1024)
            B.recip(rs[:, :n], rs[:, :n], r=["frs"], w=["frs"])
            for k in range(8):
                B.stt(o_[:, k, :], X[:, k, c0:c0 + n], gv[:, 4, k:k + 1], rs[:, :n], ALU.mult, ALU.mult,
                      r=["X", "gv", "frs"], w=[okk])
            B.dma("sp", out[b][:, :, ci * 512:(ci + 1) * 512], o_, r=[okk], w=[P.k("out")])

    import os
    for b in range(2):
        if 0 in layers:
            X = layer0(b)
        else:
            X = P.view(0, [8, NT], F32)
            B.dma("sp", X, xin[b], w=["X"])
        if 1 in layers:
            layer1(b, X)
        final(b, X)
    nc = B.build()
    return nc, P


def host_consts():
    c = np.zeros((128, 10, 128), np.float32)
    ii = np.arange(128)
    same = (ii[:, None] // 64) == (ii[None, :] // 64)
    c[:, 6, :] = same & (ii[:, None] <= ii[None, :])
    c[:, 7, :] = same & (ii[:, None] >= ii[None, :])
    c[:, 8, :] = same & (ii[:, None] > ii[None, :])
    c[:, 9, :] = same & (ii[:, None] < ii[None, :])
    c[:, 0, :] = np.eye(128)
    c[:, 1, :] = 1.0
    c[0:64, 2, 0:64] = 1.0
    c[64:128, 2, 64:128] = 1.0
    for i in range(64):
        c[2 * i + 1, 3, 2 * i] = -1.0
        c[2 * i, 3, 2 * i + 1] = 1.0
    c[64, 4, 0:64] = 1.0
    c[0, 5, 64:128] = 1.0
    return c


def host_rope():
    L, GW, dh = 2048, 64, 64
    row = np.repeat(np.arange(L // GW, dtype=np.float32), GW)
    col = np.tile(np.arange(GW, dtype=np.float32), L // GW)
    axis_dim = dh // 2
    inv = (10000.0 ** (-np.arange(0, axis_dim, 2, dtype=np.float32) / axis_dim)).astype(np.float32)
    ang = np.concatenate([row[:, None] * inv, col[:, None] * inv], axis=-1)
    cos, sin = np.cos(ang).astype(np.float32), np.sin(ang).astype(np.float32)
    r = np.zeros((128, 2, L), np.float32)
    for p in range(128):
        i = (p % 64) // 2
        r[p, 0] = cos[:, i]
        r[p, 1] = sin[:, i]
    return r

_HC = {}


def host_hyena_consts():
    if _HC:
        return _HC
    L, N = 2048, 4096
    f32 = np.float32
    t = np.linspace(0.0, 1.0, L, dtype=f32)[:, None]
    bands = 16
    fb = np.linspace(1e-4, bands - 1, bands, dtype=f32)
    wpos = (f32(2.0 * math.pi / L) * np.arange(L, dtype=f32))[:, None]
    feats = np.concatenate([t, np.cos(fb * wpos), -np.sin(fb * wpos)], axis=-1).astype(f32)
    _HC["featsT"] = np.ascontiguousarray(feats.T)
    deltas = np.abs(np.linspace(math.log(1e-2) / 1.5, math.log(1e-2) / 0.3, 512, dtype=f32))
    dec = np.exp(-t * deltas).astype(f32)
    _HC["hy_decay"] = np.ascontiguousarray(np.transpose(dec.reshape(16, 128, 512), (1, 0, 2)))
    tt = np.arange(L, dtype=np.int64)[:, None]
    ff = np.arange(17 * 128, dtype=np.int64)[None, :]
    ang = 2.0 * np.pi * ((tt * ff) % N).astype(np.float64) / N
    valid = (ff <= L).astype(np.float64)
    Mc = np.cos(ang) * valid
    Ms = -np.sin(ang) * valid
    fw = np.stack([Mc, Ms], 0).reshape(2, 16, 128, 17, 128)
    _HC["dftf"] = np.ascontiguousarray(np.transpose(fw, (3, 0, 2, 1, 4))).astype(ml_dtypes.bfloat16)
    wf = np.where((ff == 0) | (ff == L), 1.0, 2.0) * valid / N
    Ic = (np.cos(ang) * wf).T
    Is = (-np.sin(ang) * wf).T
    iw = np.stack([Ic, Is], 0).reshape(2, 17, 128, 16, 128)
    _HC["dfti"] = np.ascontiguousarray(np.transpose(iw, (3, 2, 1, 0, 4))).astype(ml_dtypes.bfloat16)
    return _HC


def fm(v):
    return np.ascontiguousarray(v.reshape(-1, 128).T)


def prep_inputs(inp):
    f = lambda a: np.ascontiguousarray(np.asarray(a, dtype=np.float32))
    x, ctx = f(inp["x"]), f(inp["ctx"])
    shared = {
        "ada_w": f(inp["ada_w"]),
        "abT": np.ascontiguousarray(np.stack([fm(inp["ada_b"][l]) for l in range(2)], axis=1)),
        "gvec": np.ascontiguousarray(np.stack([fm(inp["norm1_g"][0]), fm(inp["norm1_g"][1]), fm(inp["norm2_g"][0]),
                                               fm(inp["norm2_g"][1]), fm(inp["final_g"])], axis=1)),
        "consts": host_consts(),
        "rope": host_rope(),
        "ev_w_in": f(inp["ev_w_in"][0]),
        "ev_cw": np.ascontiguousarray(np.transpose(f(inp["ev_conv_w"][0]).reshape(31, 4, 128), (2, 1, 0))),
        "ev_w_out": f(inp["ev_w_out"][0]),
        "ffn_w_up": f(inp["ffn_w_up"]),
        "ffn_cw": np.ascontiguousarray(np.transpose(f(inp["ffn_conv_w"]).reshape(2, 3, 44, 128), (3, 0, 2, 1))),
        "ffn_cb": np.ascontiguousarray(np.transpose(f(inp["ffn_conv_b"]).reshape(2, 44, 128), (2, 0, 1))),
        "ffn_w_down": f(inp["ffn_w_down"]),
        "od_w_in": f(inp["od_w_in"][0]),
        "od_w_out": f(inp["od_w_out"][0]),
        "lbraw": np.ascontiguousarray(np.broadcast_to(f(inp["od_lower_bound"])[None], (128, 2, 2, 512))),
        "gnb": np.ascontiguousarray(np.broadcast_to(np.tile(f(inp["od_gnorm_g"][0]), 4)[None], (128, 512))),
        "hy_w1": f(inp["od_filt_w1"][0]),
        "hy_w2": f(inp["od_filt_w2"][0]),
        "hy_w3": f(inp["od_filt_w3"][0]),
        "hy_sw": np.ascontiguousarray(np.transpose(f(inp["od_short_w"][0]).reshape(3, 12, 128), (2, 1, 0))),
        "hy_sb": fm(f(inp["od_short_b"][0])),
        "hy_skip": np.ascontiguousarray(np.broadcast_to(f(inp["od_hyena_skip"][0])[None], (128, 2, 512))),
    }
    hs = np.zeros((128, 8), np.float32)
    hs[0:64, 0] = f(inp["od_filt_b1"][0])
    hs[0:64, 1] = f(inp["od_filt_freq"][0])
    hs[0:64, 2] = f(inp["od_filt_b2"][0])
    shared["hy_small"] = hs
    shared.update(host_hyena_consts())
    evs = np.zeros((128, 16), np.float32)
    evs[:, 0] = np.tile(f(inp["ev_q_gain"][0]), 2)
    evs[:, 1] = np.tile(f(inp["ev_k_gain"][0]), 2)
    evs[:, 2:6] = fm(f(inp["ev_conv_b"][0]))
    evs[:, 6:10] = fm(f(inp["ev_ln_g"][0]))
    evs[:, 10:14] = fm(f(inp["ev_ln_b"][0]))
    shared["ev_small"] = evs
    maps = []
    for core in range(8):
        m = dict(shared)
        xi = np.zeros((2, 128, 8, NT), np.float32)
        cv = np.zeros((128, 8, 3), np.float32)
        for q in range(2):
            bidx = 2 * core + q
            full = np.concatenate([ctx[bidx], x[bidx]], axis=0)
            xi[q] = np.transpose(full.reshape(NT, 8, 128), (2, 1, 0))
            cv[:, :, q] = fm(f(inp["c"][bidx]))
        cv[:, :, 2] = fm(f(inp["c_ctx"]))
        m["xin"] = xi
        m["cvec"] = cv
        maps.append(m)
    return maps


_CACHE = {}


def kernel(**inputs):
    maps = prep_inputs(inputs)
    if "nc" not in _CACHE:
        _CACHE["nc"] = build_program()[0]
    nc = _CACHE["nc"]
    res = run_bass_kernel_spmd(nc, maps, core_ids=list(range(8)))
    outp = np.zeros((16, 2048, 1024), np.float32)
    for core in range(8):
        o = res.results[core]["out"]
        for q in range(2):
            outp[2 * core + q] = np.transpose(o[q], (2, 1, 0)).reshape(2048, 1024)
    return outp
```
